# Optimizing a Trainium2 kernel written in Bass

```python
import math
import jax, jax.numpy as jnp
from jax import lax
import numpy as np


D_MODEL = 1024
BATCH = 8
SEQ = 8192
DEPTH = 4
DEC_BATCH = 8
DEC_SEQ = 4096
PAST_LEN = 128

ROPE_THETA = 10000.0
NORM_EPS = 1e-5
Q_BLOCK = 128
D_FF = 2816
A_HEADS = 8
A_NOPE = 64
A_ROPE = 32
A_V = 64
A_Q_LORA = 384
A_KV_LORA = 256
B_HEADS = 4
B_D = 32
B_V = 2 * B_D
C_HEADS = 4
C_DIM = 64
C_BRANCHES = ((128, 1), (512, 4), (2048, 16))
IN_SPLITS = (A_Q_LORA, A_KV_LORA, A_ROPE,
             B_HEADS * 2 * B_D, B_HEADS * 2 * B_D, B_HEADS * B_V,
             C_HEADS * C_DIM, C_HEADS * C_DIM, C_HEADS * C_DIM)
IN_COLS = sum(IN_SPLITS)
MIX_WIDTH = A_HEADS * A_V + B_HEADS * B_V + C_HEADS * C_DIM
ALPHA = (2 * DEPTH) ** 0.25
BETA = (8 * DEPTH) ** -0.25

kernel_name = 'hybrid_mla_diff_dilated_encoder'


def layer_norm(x, g, b):
    xf = x.astype(jnp.float32)
    mu = jnp.mean(xf, axis=-1, keepdims=True)
    var = jnp.mean(jnp.square(xf - mu), axis=-1, keepdims=True)
    y = (xf - mu) * lax.rsqrt(var + NORM_EPS) * g.astype(jnp.float32) + b.astype(jnp.float32)
    return y.astype(x.dtype)


def rms_norm(x, g):
    xf = x.astype(jnp.float32)
    y = xf * lax.rsqrt(jnp.mean(jnp.square(xf), axis=-1, keepdims=True) + NORM_EPS) * g.astype(jnp.float32)
    return y.astype(x.dtype)


def apply_rope(x):
    seq, dim = x.shape[1], x.shape[-1]
    inv = 1.0 / (ROPE_THETA ** (jnp.arange(0, dim, 2, dtype=jnp.float32) / dim))
    ang = jnp.arange(seq, dtype=jnp.float32)[:, None] * inv[None, :]
    shp = (seq,) + (1,) * (x.ndim - 3) + (dim // 2,)
    c, s = jnp.cos(ang).reshape(shp), jnp.sin(ang).reshape(shp)
    x1, x2 = jnp.split(x.astype(jnp.float32), 2, axis=-1)
    return jnp.concatenate([x1 * c - x2 * s, x2 * c + x1 * s], axis=-1).astype(x.dtype)


def swiglu(x, wg, wu, wd):
    return (jax.nn.silu(x @ wg) * (x @ wu)) @ wd


def to_query_blocks(q):
    b, s = q.shape[0], q.shape[1]
    nb = s // Q_BLOCK
    qb = q.reshape((b, nb, Q_BLOCK) + q.shape[2:])
    return jnp.moveaxis(qb, 1, 0), nb


def from_query_blocks(o):
    o = jnp.moveaxis(o, 0, 1)
    return o.reshape((o.shape[0], o.shape[1] * o.shape[2]) + o.shape[3:])


def dense_attention(q, k, v, scale):
    qb, _ = to_query_blocks(q)

    def one(qblk):
        s = jnp.einsum('bqhd,bkhd->bhqk', qblk, k).astype(jnp.float32) * scale
        p = jax.nn.softmax(s, axis=-1).astype(v.dtype)
        return jnp.einsum('bhqk,bkhd->bqhd', p, v)

    return from_query_blocks(lax.map(one, qb))


def differential_attention(q, k, v, lam, scale):
    qb, _ = to_query_blocks(q)

    def one(qblk):
        s = jnp.einsum('bqhcd,bkhcd->bchqk', qblk, k).astype(jnp.float32) * scale
        p = jax.nn.softmax(s, axis=-1)
        a = (p[:, 0] - lam * p[:, 1]).astype(v.dtype)
        return jnp.einsum('bhqk,bkhd->bqhd', a, v)

    return from_query_blocks(lax.map(one, qb))


def dilated_branch(q, k, v, window, dilation, scale):
    b, s, h, dh = q.shape
    half = window // (2 * dilation)
    L = s // dilation
    n = b * dilation

    def to_sub(t):
        return t.reshape(b, L, dilation, h, dh).transpose(0, 2, 1, 3, 4).reshape(n, L, h, dh)

    qs, ks, vs = to_sub(q), to_sub(k), to_sub(v)
    nb = -(-L // half)
    lp = nb * half
    qs = jnp.pad(qs, ((0, 0), (0, lp - L), (0, 0), (0, 0))).reshape(n, nb, half, h, dh)

    def neighbours(t):
        tb = jnp.pad(t, ((0, 0), (half, lp - L + half), (0, 0), (0, 0))).reshape(n, nb + 2, half, h, dh)
        return jnp.concatenate([tb[:, :-2], tb[:, 1:-1], tb[:, 2:]], axis=2)

    kb, vb = neighbours(ks), neighbours(vs)
    sc = jnp.einsum('njqhd,njkhd->njhqk', qs, kb).astype(jnp.float32) * scale
    qa = jnp.arange(half)
    kc = jnp.arange(3 * half)
    band = jnp.abs(qa[:, None] + half - kc[None, :]) <= half
    kpos = jnp.arange(nb)[:, None] * half - half + kc[None, :]
    kvalid = (kpos >= 0) & (kpos < L)
    mask = band[None, :, :] & kvalid[:, None, :]
    sc = jnp.where(mask[None, :, None, :, :], sc, -1e30)
    m = jnp.max(sc, axis=-1, keepdims=True)
    p = jnp.exp(sc - m)
    den = jnp.sum(p, axis=-1, keepdims=True)
    o = jnp.einsum('njhqk,njkhd->njqhd', (p / den).astype(v.dtype), vb)
    lse = jnp.swapaxes((m + jnp.log(den))[..., 0], 2, 3)

    def from_sub(t):
        t = t.reshape((n, lp) + t.shape[3:])[:, :L]
        t = t.reshape((b, dilation, L) + t.shape[2:])
        t = jnp.swapaxes(t, 1, 2)
        return t.reshape((b, s) + t.shape[3:])

    return from_sub(o), from_sub(lse)


def token_mixing(h, w_in, q_norm, kv_norm, w_uq, w_ukv, diff_lam, diff_g, w_o, lambda_init):
    b, s, _ = h.shape
    proj = h @ w_in
    offs = list(np.cumsum(IN_SPLITS)[:-1])
    cq, ckv, kpe, qb_, kb_, vb_, qc, kc, vc = jnp.split(proj, offs, axis=-1)

    qa = (rms_norm(cq, q_norm) @ w_uq).reshape(b, s, A_HEADS, A_NOPE + A_ROPE)
    q_nope, q_pe = qa[..., :A_NOPE], apply_rope(qa[..., A_NOPE:])
    kva = (rms_norm(ckv, kv_norm) @ w_ukv).reshape(b, s, A_HEADS, A_NOPE + A_V)
    k_nope, v_a = kva[..., :A_NOPE], kva[..., A_NOPE:]
    k_pe = jnp.broadcast_to(apply_rope(kpe.reshape(b, s, 1, A_ROPE)), (b, s, A_HEADS, A_ROPE))
    o_a = dense_attention(jnp.concatenate([q_nope, q_pe], axis=-1),
                          jnp.concatenate([k_nope, k_pe], axis=-1),
                          v_a, (A_NOPE + A_ROPE) ** -0.5)

    lf = diff_lam.astype(jnp.float32)
    lam = jnp.exp(jnp.sum(lf[0] * lf[1])) - jnp.exp(jnp.sum(lf[2] * lf[3])) + lambda_init
    q2 = apply_rope(qb_.reshape(b, s, B_HEADS, 2, B_D))
    k2 = apply_rope(kb_.reshape(b, s, B_HEADS, 2, B_D))
    o_b = differential_attention(q2, k2, vb_.reshape(b, s, B_HEADS, B_V), lam, B_D ** -0.5)
    o_b = rms_norm(o_b, diff_g) * (1.0 - lambda_init)

    qc = apply_rope(qc.reshape(b, s, C_HEADS, C_DIM))
    kc = apply_rope(kc.reshape(b, s, C_HEADS, C_DIM))
    vc = vc.reshape(b, s, C_HEADS, C_DIM)
    outs, lses = [], []
    for window, dilation in C_BRANCHES:
        o_i, l_i = dilated_branch(qc, kc, vc, window, dilation, C_DIM ** -0.5)
        outs.append(o_i)
        lses.append(l_i)
    wts = jax.nn.softmax(jnp.stack(lses, axis=0), axis=0)
    o_c = jnp.einsum('nbsh,nbshd->bshd', wts.astype(vc.dtype), jnp.stack(outs, axis=0))

    o = jnp.concatenate([o_a.reshape(b, s, -1), o_b.reshape(b, s, -1), o_c.reshape(b, s, -1)], axis=-1)
    return o @ w_o


def setup_inputs(seed: int = 0) -> dict:
    key = jax.random.key(seed)
    ks = jax.random.split(key, 16)
    f = jnp.float32
    nrm = lambda k, shp: jax.random.normal(k, shp, f)
    return {
        'x_prompt': nrm(ks[0], (BATCH, SEQ, D_MODEL)),
        'x_sample': nrm(ks[1], (DEC_BATCH, DEC_SEQ, D_MODEL)),
        'ln_g': 1.0 + 0.01 * nrm(ks[2], (DEPTH, 3, D_MODEL)),
        'ln_b': 0.01 * nrm(ks[3], (DEPTH, 3, D_MODEL)),
        'ffn_w_gate': nrm(ks[4], (DEPTH, 2, D_MODEL, D_FF)) * D_MODEL ** -0.5,
        'ffn_w_up': nrm(ks[5], (DEPTH, 2, D_MODEL, D_FF)) * D_MODEL ** -0.5,
        'ffn_w_down': nrm(ks[6], (DEPTH, 2, D_FF, D_MODEL)) * (D_FF ** -0.5 * BETA),
        'w_in': nrm(ks[7], (DEPTH, D_MODEL, IN_COLS)) * D_MODEL ** -0.5,
        'mla_q_norm': 1.0 + 0.01 * nrm(ks[8], (DEPTH, A_Q_LORA)),
        'mla_kv_norm': 1.0 + 0.01 * nrm(ks[9], (DEPTH, A_KV_LORA)),
        'mla_w_uq': nrm(ks[10], (DEPTH, A_Q_LORA, A_HEADS * (A_NOPE + A_ROPE))) * A_Q_LORA ** -0.5,
        'mla_w_ukv': nrm(ks[11], (DEPTH, A_KV_LORA, A_HEADS * (A_NOPE + A_V))) * A_KV_LORA ** -0.5,
        'diff_lambda': 0.1 * nrm(ks[12], (DEPTH, 4, B_D)),
        'diff_subln': 1.0 + 0.01 * nrm(ks[13], (DEPTH, B_V)),
        'w_out': nrm(ks[14], (DEPTH, MIX_WIDTH, D_MODEL)) * (MIX_WIDTH ** -0.5 * BETA),
    }


def reference(x_prompt, x_sample, ln_g, ln_b, ffn_w_gate, ffn_w_up, ffn_w_down, w_in,
              mla_q_norm, mla_kv_norm, mla_w_uq, mla_w_ukv, diff_lambda, diff_subln, w_out):
    def run(x):
        for i in range(DEPTH):
            lambda_init = 0.8 - 0.6 * math.exp(-0.3 * i)
            x = layer_norm(ALPHA * x + 0.5 * swiglu(x, ffn_w_gate[i, 0], ffn_w_up[i, 0], ffn_w_down[i, 0]),
                           ln_g[i, 0], ln_b[i, 0])
            x = layer_norm(ALPHA * x + token_mixing(x, w_in[i], mla_q_norm[i], mla_kv_norm[i], mla_w_uq[i],
                                                    mla_w_ukv[i], diff_lambda[i], diff_subln[i], w_out[i],
                                                    lambda_init),
                           ln_g[i, 1], ln_b[i, 1])
            x = layer_norm(ALPHA * x + 0.5 * swiglu(x, ffn_w_gate[i, 1], ffn_w_up[i, 1], ffn_w_down[i, 1]),
                           ln_g[i, 2], ln_b[i, 2])
        return x

    y_prompt = run(x_prompt)
    y_sample = run(x_sample)
    return (y_prompt, y_sample)
```

```python
import math
import numpy as np
import ml_dtypes
import concourse.bass as bass
import concourse.mybir as mybir
from concourse.bass_utils import run_bass_kernel_spmd

F32 = mybir.dt.float32
BF16 = mybir.dt.bfloat16
AF = mybir.ActivationFunctionType
ALU = mybir.AluOpType
AX = mybir.AxisListType

D = 1024
DFF = 2816
NFF = 22
DEPTH = 4
ALPHA = (2 * DEPTH) ** 0.25
EPS = 1e-5
INC = 2208
SWC = 1056
O_CQ, O_CKV, O_KPE, O_QB, O_KB, O_VB, O_QC, O_KC, O_VC = 0, 384, 640, 672, 928, 1184, 1440, 1696, 1952
S_KPE, S_QB, S_KB, S_QC, S_KC = 0, 32, 288, 544, 800
MAXPOS = 8192


class Buf:
    __slots__ = ("w", "r")

    def __init__(self):
        self.w = None
        self.r = {}


def bufs(n):
    return [Buf() for _ in range(n)]


class Ctx:
    SB_LIMIT = 212800

    def __init__(self, nc):
        self.nc = nc
        self.eng = {"pe": nc.tensor, "act": nc.scalar, "dve": nc.vector, "pool": nc.gpsimd, "sp": nc.sync}
        self.sems = {}
        self.nsig = {}
        for k in ("pe", "act", "dve", "pool"):
            self.sems[k] = nc.alloc_semaphore("sem_" + k)
            self.nsig[k] = 0
        self.waited = {k: {} for k in self.eng}
        self.dcount = {}
        self.dma_free = []
        self.slot2sem = {}
        self.sb_base = 16640
        self.off = self.sb_base
        self.uid = 0
        self.keep = []

    def sb(self, name, shape, dtype):
        isz = 2 if dtype == BF16 else 4
        n = 1
        for s in shape[1:]:
            n *= s
        size = (n * isz + 63) // 64 * 64
        off = self.off
        self.off += size
        assert self.off <= self.SB_LIMIT, ("SBUF overflow", name, self.off)
        self.uid += 1
        return self.nc.alloc_sbuf_tensor_at("%s_%d" % (name, self.uid), list(shape), dtype, offset=off)

    def persist(self):
        self.sb_base = self.off

    def _deps(self, reads, writes):
        toks = []
        for b in reads:
            if b.w is not None:
                toks.append(b.w)
        for b in writes:
            if b.w is not None:
                toks.append(b.w)
            toks.extend(b.r.values())
        return toks

    def _wait(self, e, toks):
        need = {}
        for (sn, val, src) in toks:
            if src == "pe" and e == "pe":
                continue
            if val > need.get(sn, 0):
                need[sn] = val
        w = self.waited[e]
        for sn, val in need.items():
            if w.get(sn, 0) >= val:
                continue
            self.eng[e].wait_ge(self.sems[sn], val)
            w[sn] = val

    def _commit(self, tok, reads, writes):
        for b in reads:
            o = b.r.get(tok[0])
            if o is None or o[1] < tok[1]:
                b.r[tok[0]] = tok
        for b in writes:
            b.w = tok
            b.r = {}

    def op(self, e, fn, reads=(), writes=(), signal=True):
        self._wait(e, self._deps(reads, writes))
        ins = fn(self.eng[e])
        if signal:
            self.nsig[e] += 1
            ins.then_inc(self.sems[e], 1)
            tok = (e, self.nsig[e], e)
        else:
            tok = (e, self.nsig[e] + 1, e)
        self._commit(tok, reads, writes)

    def _slot_sem(self, slot):
        if slot not in self.slot2sem:
            if self.dma_free:
                name = self.dma_free.pop()
            else:
                name = "dsem%d" % len(self.dcount)
                self.sems[name] = self.nc.alloc_semaphore(name)
                self.dcount[name] = 0
            self.slot2sem[slot] = name
        return self.slot2sem[slot]

    def dma(self, q, out, in_, slot, reads=(), writes=(), **kw):
        self._wait(q, self._deps(reads, writes))
        sn = self._slot_sem(slot)
        ins = self.eng[q].dma_start(out=out, in_=in_, **kw)
        self.dcount[sn] += 16
        ins.then_inc(self.sems[sn], 16)
        tok = (sn, self.dcount[sn], "dma")
        self._commit(tok, reads, writes)

    def barrier(self):
        for e in self.eng:
            w = self.waited[e]
            for k in ("pe", "act", "dve", "pool"):
                if w.get(k, 0) < self.nsig[k]:
                    self.eng[e].wait_ge(self.sems[k], self.nsig[k])
                    w[k] = self.nsig[k]
            for sn, c in self.dcount.items():
                if w.get(sn, 0) < c:
                    self.eng[e].wait_ge(self.sems[sn], c)
                    w[sn] = c
        self.slot2sem = {}
        self.dma_free = list(self.dcount.keys())
        self.off = self.sb_base


def build(SP, SS, depth, dbg=False):
    NT = SP + SS
    seqs = [(0, SP), (SP, SS)]
    nc = bass.Bass("TRN2", target_bir_lowering=False)

    def din(name, shape, dtype=F32):
        return nc.dram_tensor(name, list(shape), dtype, kind="ExternalInput").ap()

    def dscr(name, shape, dtype):
        return nc.dram_tensor(name, list(shape), dtype, kind=("ExternalOutput" if dbg else "Internal")).ap()

    xin = din("xin", [NT, D])
    ln_g = din("ln_g", [DEPTH, 3, D])
    ln_b = din("ln_b", [DEPTH, 3, D])
    wg = din("wg", [DEPTH, 2, D, DFF])
    wu = din("wu", [DEPTH, 2, D, DFF])
    wd = din("wd", [DEPTH, 2, DFF, D])
    w_in = din("w_in", [DEPTH, D, INC])
    w_sw = din("w_sw", [DEPTH, D, SWC])
    q_norm = din("q_norm", [DEPTH, 384])
    kv_norm = din("kv_norm", [DEPTH, 256])
    w_uq = din("w_uq", [DEPTH, 384, 768])
    w_uqs = din("w_uqs", [DEPTH, 384, 256])
    w_ukv = din("w_ukv", [DEPTH, 256, 1024])
    dlam = din("dlam", [DEPTH, 128])
    dsub = din("dsub", [DEPTH, 64])
    w_out = din("w_out", [DEPTH, D, D])
    cos64 = din("cos64", [128, MAXPOS])
    sin64 = din("sin64", [128, MAXPOS])
    cos32 = din("cos32", [128, MAXPOS])
    sin32 = din("sin32", [128, MAXPOS])
    ident_d = din("ident", [128, 128], BF16)
    mask_d = din("bandmask", [128, 256], BF16)

    y = nc.dram_tensor("y", [NT, D], F32, kind="ExternalOutput").ap()
    xa = dscr("xa", [NT, D], F32)
    xb_ = dscr("xb", [NT, D], F32)
    xTa = dscr("xTa", [8, 128, NT], BF16)
    xTb = dscr("xTb", [8, 128, NT], BF16)
    QA = dscr("QA", [8, 96, NT], BF16)
    KA = dscr("KA", [8, 96, NT], BF16)
    VA = dscr("VA", [NT, 8 * 65], BF16)
    QB = dscr("QB", [4, 64, NT], BF16)
    KB = dscr("KB", [4, 64, NT], BF16)
    VB = dscr("VB", [NT, 4 * 65], BF16)
    QC = dscr("QC", [4, 64, NT], BF16)
    KC = dscr("KC", [4, 64, NT], BF16)
    VC = dscr("VC", [NT, 4 * 65], BF16)
    OT = dscr("OT", [8, 128, NT], BF16)

    C = Ctx(nc)
    PS = nc.alloc_psum_tensor("ps", [128, 8, 512], F32)
    psb = bufs(8)

    def bank(b):
        return PS[:, b, :]

    def bank16(b):
        return PS[:, b, :].bitcast(BF16)

    ident = C.sb("ident", [128, 128], BF16)
    mask = C.sb("mask", [128, 256], BF16)
    ones_f = C.sb("ones_f", [128, 128], F32)
    sel_f = C.sb("sel_f", [128, 64], F32)
    mhalf = C.sb("mhalf", [128, 1], F32)
    C.persist()
    cb = Buf()
    C.dma("sp", ident[:], ident_d[:, :], cb, writes=[cb])
    C.dma("sp", mask[:], mask_d[:, :], cb, writes=[cb])
    C.op("dve", lambda e: e.memset(ones_f[:], 1.0), writes=[cb])
    C.op("dve", lambda e: e.memset(sel_f[:], 0.0), writes=[cb])
    C.op("dve", lambda e: e.memset(sel_f[64:65, :], 1.0), writes=[cb])
    C.op("dve", lambda e: e.memset(mhalf[:], -0.5), writes=[cb])
    C.barrier()

    def ln_setup(l, j):
        g_t = C.sb("g_t", [128, D], F32)
        b_t = C.sb("b_t", [128, D], F32)
        gb = Buf()
        C.dma("sp", g_t[:], ln_g[l, j:j + 1, :].partition_broadcast(128), gb, writes=[gb])
        C.dma("sp", b_t[:], ln_b[l, j:j + 1, :].partition_broadcast(128), gb, writes=[gb])
        xbuf = [C.sb("xbuf", [128, D], F32) for _ in range(2)]
        ob = C.sb("ob", [128, D], BF16)
        xts = [C.sb("xts", [128, 8, 128], BF16) for _ in range(2)]
        st = C.sb("stats", [128, 2, 6], F32)
        mv = C.sb("mv", [128, 2], F32)
        sm = C.sb("sm", [128, 4], F32)
        return dict(g=g_t, b=b_t, gb=gb, xbuf=xbuf, xbb=bufs(2), ob=ob, obb=Buf(), xts=xts, xtsb=bufs(2),
                    st=st, stb=Buf(), mv=mv, sm=sm, cnt=0, lcnt=0)

    def ln_load_x(L, xsrc, t0):
        i = L["lcnt"] % 2
        L["lcnt"] += 1
        C.dma("sp", L["xbuf"][i][:], xsrc[t0:t0 + 128, :], L["xbb"][i], writes=[L["xbb"][i]])

    def ln_epilogue(L, ybanks, res_scale, k, xdst, xTdst, t0, trbank):
        i = L["cnt"] % 2
        L["cnt"] += 1
        xbuf = L["xbuf"][i]
        xbb = L["xbb"][i]
        for hf in range(2):
            sl = slice(hf * 512, (hf + 1) * 512)
            C.op("dve", lambda e: e.scalar_tensor_tensor(out=xbuf[:, sl], in0=xbuf[:, sl], scalar=float(res_scale),
                                                         in1=bank(ybanks[hf]), op0=ALU.mult, op1=ALU.add),
                 reads=[xbb, psb[ybanks[hf]]], writes=[xbb])
        stb = L["stb"]
        for hf in range(2):
            sl = slice(hf * 512, (hf + 1) * 512)
            C.op("dve", lambda e: e.bn_stats(out=L["st"][:, hf, :], in_=xbuf[:, sl]), reads=[xbb], writes=[stb])
        C.op("dve", lambda e: e.bn_aggr(out=L["mv"][:], in_=L["st"][:]), reads=[stb], writes=[stb])
        sm = L["sm"]
        C.op("dve", lambda e: e.tensor_scalar(out=sm[:, 0:1], in0=L["mv"][:, 1:2], scalar1=float(1.0 / (k * k)),
                                              scalar2=float(EPS), op0=ALU.mult, op1=ALU.add), reads=[stb], writes=[stb])
        C.op("pool", lambda e: e.tensor_tensor(out=sm[:, 1:2], in0=sm[:, 0:1], in1=mhalf[:], op=ALU.pow),
             reads=[stb], writes=[stb])
        C.op("dve", lambda e: e.tensor_scalar(out=sm[:, 2:3], in0=sm[:, 1:2], scalar1=float(1.0 / k), scalar2=None,
                                              op0=ALU.mult), reads=[stb], writes=[stb])
        C.op("dve", lambda e: e.tensor_scalar(out=sm[:, 3:4], in0=L["mv"][:, 0:1], scalar1=sm[:, 2:3], scalar2=-1.0,
                                              op0=ALU.mult, op1=ALU.mult), reads=[stb], writes=[stb])
        C.op("act", lambda e: e.activation(out=xbuf[:], in_=xbuf[:], func=AF.Identity, bias=sm[:, 3:4], scale=sm[:, 2:3]),
             reads=[xbb, stb], writes=[xbb])
        C.op("pool", lambda e: e.tensor_tensor(out=xbuf[:], in0=xbuf[:], in1=L["g"][:], op=ALU.mult),
             reads=[xbb, L["gb"]], writes=[xbb])
        C.op("pool", lambda e: e.tensor_tensor(out=xbuf[:], in0=xbuf[:], in1=L["b"][:], op=ALU.add),
             reads=[xbb, L["gb"]], writes=[xbb])
        C.dma("sp", xdst[t0:t0 + 128, :], xbuf[:], xbb, reads=[xbb])
        if xTdst is not None:
            obb = L["obb"]
            C.op("act", lambda e: e.copy(out=L["ob"][:], in_=xbuf[:]), reads=[xbb], writes=[obb])
            transpose_store(L["ob"], obb, L["xts"][i], L["xtsb"][i], xTdst, t0, trbank)

    def transpose_store(ob, obb, xts, xtsb, xTdst, t0, trbank):
        tb = bank16(trbank)
        for c in range(8):
            C.op("pe", lambda e: e.transpose(out=tb[:, c * 128:(c + 1) * 128], in_=ob[:, c * 128:(c + 1) * 128],
                                             identity=ident[:]),
                 reads=[obb], writes=[psb[trbank]], signal=(c == 7))
        C.op("dve", lambda e: e.tensor_copy(out=xts[:].rearrange("p c t -> p (c t)"), in_=tb[:, :]),
             reads=[psb[trbank]], writes=[xtsb])
        C.dma("sp", xTdst[:, :, t0:t0 + 128].rearrange("c p t -> p c t"), xts[:], xtsb, reads=[xtsb])

    def phase0(xsrc, xTdst):
        xbuf = [C.sb("p0x", [128, D], F32) for _ in range(2)]
        xbb = bufs(2)
        ob = [C.sb("p0o", [128, D], BF16) for _ in range(2)]
        obb = bufs(2)
        xts = [C.sb("p0t", [128, 8, 128], BF16) for _ in range(2)]
        xtsb = bufs(2)
        n = NT // 128
        C.dma("sp", xbuf[0][:], xsrc[0:128, :], xbb[0], writes=[xbb[0]])
        for s in range(n):
            i = s % 2
            if s + 1 < n:
                C.dma("sp", xbuf[1 - i][:], xsrc[(s + 1) * 128:(s + 2) * 128, :], xbb[1 - i], writes=[xbb[1 - i]])
            C.op("act", lambda e: e.copy(out=ob[i][:], in_=xbuf[i][:]), reads=[xbb[i]], writes=[obb[i]])
            transpose_store(ob[i], obb[i], xts[i], xtsb[i], xTdst, s * 128, 6 + i)
        C.barrier()

    def ffn_phase(l, j, xsrc, xTsrc, xdst, xTdst):
        Wg = C.sb("Wg", [128, 8, DFF], BF16)
        Wu = C.sb("Wu", [128, 8, DFF], BF16)
        Wd = C.sb("Wd", [128, NFF, D], BF16)
        wb = Buf()
        for (W, src) in ((Wg, wg), (Wu, wu)):
            for hf in range(2):
                C.dma("pool", W[:, :, hf * 1408:(hf + 1) * 1408],
                      src[l, j, :, hf * 1408:(hf + 1) * 1408].rearrange("(k p) f -> p k f", p=128), wb, writes=[wb])
        for hf in range(2):
            C.dma("pool", Wd[:, hf * 11:(hf + 1) * 11, :],
                  wd[l, j, hf * 1408:(hf + 1) * 1408, :].rearrange("(c p) d -> p c d", p=128), wb, writes=[wb])
        xT = C.sb("xTin", [128, 8, 512], BF16)
        xTb_ = Buf()
        hT = C.sb("hT", [128, NFF, 512], BF16)
        hTb = bufs(NFF)
        sil = [C.sb("sil", [128, 512], F32) for _ in range(2)]
        silb = bufs(2)
        L = ln_setup(l, 0 if j == 0 else 2)
        ntile = NT // 512
        C.dma("sp", xT[:], xTsrc[:, :, 0:512].rearrange("c p t -> p c t"), xTb_, writes=[xTb_])
        for t in range(ntile):
            t0 = t * 512
            ln_load_x(L, xsrc, t0)
            for c in range(NFF):
                gb_, ub_ = c % 2, 2 + c % 2
                for k in range(8):
                    C.op("pe", lambda e: e.matmul(bank(gb_), Wg[:, k, c * 128:(c + 1) * 128], xT[:, k, :],
                                                  start=(k == 0), stop=(k == 7)),
                         reads=[wb, xTb_], writes=[psb[gb_]], signal=(k == 7))
                for k in range(8):
                    C.op("pe", lambda e: e.matmul(bank(ub_), Wu[:, k, c * 128:(c + 1) * 128], xT[:, k, :],
                                                  start=(k == 0), stop=(k == 7)),
                         reads=[wb, xTb_], writes=[psb[ub_]], signal=(k == 7))
                C.op("act", lambda e: e.activation(out=sil[c % 2][:], in_=bank(gb_), func=AF.Silu),
                     reads=[psb[gb_]], writes=[silb[c % 2]])
                C.op("dve", lambda e: e.tensor_tensor(out=hT[:, c, :], in0=sil[c % 2][:], in1=bank(ub_), op=ALU.mult),
                     reads=[silb[c % 2], psb[ub_]], writes=[hTb[c]])
            if t + 1 < ntile:
                C.dma("sp", xT[:], xTsrc[:, :, t0 + 512:t0 + 1024].rearrange("c p t -> p c t"), xTb_, writes=[xTb_])
            for s in range(4):
                for hf in range(2):
                    yb = 4 + hf
                    for c in range(NFF):
                        C.op("pe", lambda e: e.matmul(bank(yb), hT[:, c, s * 128:(s + 1) * 128],
                                                      Wd[:, c, hf * 512:(hf + 1) * 512], start=(c == 0), stop=(c == NFF - 1)),
                             reads=[hTb[c], wb], writes=[psb[yb]], signal=(c == NFF - 1))
                if s < 3:
                    ln_load_x(L, xsrc, t0 + (s + 1) * 128)
                ln_epilogue(L, (4, 5), 2.0 * ALPHA, 2.0, xdst, xTdst, t0 + s * 128, 6)
        C.barrier()

    def proj_phase(l, xTsrc):
        Win = C.sb("Win", [128, 8, INC], BF16)
        Wsw = C.sb("Wsw", [128, 8, SWC], BF16)
        Wuq = C.sb("Wuq", [128, 3, 768], BF16)
        Wuqs = C.sb("Wuqs", [128, 3, 256], BF16)
        Wk = C.sb("Wk", [128, 2, 512], BF16)
        Wv = C.sb("Wv", [128, 2, 512], BF16)
        wb = Buf()
        for hf in range(2):
            C.dma("pool", Win[:, :, hf * 1104:(hf + 1) * 1104],
                  w_in[l, :, hf * 1104:(hf + 1) * 1104].rearrange("(k p) f -> p k f", p=128), wb, writes=[wb])
        C.dma("pool", Wsw[:], w_sw[l].rearrange("(k p) f -> p k f", p=128), wb, writes=[wb])
        stg = C.sb("stg", [128, 3, 1024], F32)
        stg2 = C.sb("stg2", [128, 3, 256], F32)
        stg3 = C.sb("stg3", [128, 2, 1024], F32)
        gq = C.sb("gq", [128, 3], F32)
        gkv = C.sb("gkv", [128, 2], F32)
        sb_ = Buf()
        C.dma("sp", stg[:, :, 0:768], w_uq[l].rearrange("(c p) f -> p c f", p=128), sb_, writes=[sb_])
        C.dma("sp", stg2[:], w_uqs[l].rearrange("(c p) f -> p c f", p=128), sb_, writes=[sb_])
        C.dma("sp", stg3[:], w_ukv[l].rearrange("(c p) f -> p c f", p=128), sb_, writes=[sb_])
        C.dma("sp", gq[:], q_norm[l].rearrange("(c p) -> p c", p=128), sb_, writes=[sb_], allow_slow_non_contiguous=True)
        C.dma("sp", gkv[:], kv_norm[l].rearrange("(c p) -> p c", p=128), sb_, writes=[sb_], allow_slow_non_contiguous=True)
        for c in range(3):
            C.op("dve", lambda e: e.tensor_scalar(out=Wuq[:, c, :], in0=stg[:, c, 0:768], scalar1=gq[:, c:c + 1],
                                                  scalar2=None, op0=ALU.mult), reads=[sb_], writes=[wb])
            C.op("dve", lambda e: e.tensor_scalar(out=Wuqs[:, c, :], in0=stg2[:, c, :], scalar1=gq[:, c:c + 1],
                                                  scalar2=None, op0=ALU.mult), reads=[sb_], writes=[wb])
        for c in range(2):
            src = stg3[:, c, :].rearrange("p (h t j) -> p h t j", h=8, t=2)
            C.op("dve", lambda e: e.tensor_scalar(out=Wk[:, c, :].rearrange("p (h j) -> p h j", h=8), in0=src[:, :, 0, :],
                                                  scalar1=gkv[:, c:c + 1], scalar2=None, op0=ALU.mult),
                 reads=[sb_], writes=[wb])
            C.op("dve", lambda e: e.tensor_scalar(out=Wv[:, c, :].rearrange("p (h j) -> p h j", h=8), in0=src[:, :, 1, :],
                                                  scalar1=gkv[:, c:c + 1], scalar2=None, op0=ALU.mult),
                 reads=[sb_], writes=[wb])

        xT = [C.sb("xTin", [128, 8, 512], BF16) for _ in range(2)]
        xTb_ = bufs(2)
        tabs = [C.sb("tabs", [128, 4, 512], F32) for _ in range(2)]
        tabb = bufs(2)
        cqn = C.sb("cqn", [128, 3, 512], BF16)
        cqnb = Buf()
        ckvn = C.sb("ckvn", [128, 2, 512], BF16)
        ckvnb = Buf()
        sq = [C.sb("sq", [128, 512], F32) for _ in range(2)]
        sqb = bufs(2)
        rr = C.sb("rr", [128, 512], F32)
        rrb = Buf()
        qas = C.sb("qas", [128, 8, 512], BF16)
        qasb = bufs(8)
        kas = C.sb("kas", [128, 8, 512], BF16)
        kasb = bufs(8)
        vas = C.sb("vas", [128, 4, 8, 65], BF16)
        vasb = Buf()
        rs = C.sb("rs", [128, 4, 2, 512], BF16)
        rsb = [bufs(2) for _ in range(4)]
        vbs = C.sb("vbs", [128, 4, 4, 65], BF16)
        vbsb = Buf()
        vcs = C.sb("vcs", [128, 4, 4, 65], BF16)
        vcsb = Buf()
        tmp = [C.sb("tmp", [128, 512], F32) for _ in range(2)]
        tmpb = bufs(2)
        C.op("pool", lambda e: e.memset(vas[:], 1.0), writes=[vasb])
        C.op("pool", lambda e: e.memset(vbs[:], 1.0), writes=[vbsb])
        C.op("pool", lambda e: e.memset(vcs[:], 1.0), writes=[vcsb])

        tiles = []
        for (s0, S) in seqs:
            for q in range(S // 512):
                tiles.append((s0 + q * 512, q * 512))

        def load_tile(ti):
            t0, p0 = tiles[ti]
            i = ti % 2
            C.dma("sp", xT[i][:], xTsrc[:, :, t0:t0 + 512].rearrange("c p t -> p c t"), xTb_[i], writes=[xTb_[i]])
            for n_, tab in enumerate((cos64, sin64, cos32, sin32)):
                C.dma("sp", tabs[i][:, n_, :], tab[:, p0:p0 + 512], tabb[i], writes=[tabb[i]])

        def proj_fm(bk, W, col0, ncol, xTi, xb, prow=0):
            for k in range(8):
                C.op("pe", lambda e: e.matmul(PS[prow:prow + ncol, bk, :], W[:, k, col0:col0 + ncol], xTi[:, k, :],
                                              start=(k == 0), stop=(k == 7)),
                     reads=[wb, xb], writes=[psb[bk]], signal=(k == 7))

        tcnt = [0]

        def rope_out(bk_a, bk_s, p0, p1, ctab, stab, out_ap, tb, out_buf, extra_reads=()):
            i0 = tcnt[0] % 2
            tcnt[0] += 1
            t_ = tmp[i0]
            C.op("dve", lambda e: e.tensor_tensor(out=t_[p0:p1, :], in0=ctab[p0:p1, :], in1=PS[p0:p1, bk_a, :], op=ALU.mult),
                 reads=[tb, psb[bk_a]], writes=[tmpb[i0]])
            i1 = tcnt[0] % 2
            tcnt[0] += 1
            t2 = tmp[i1]
            C.op("dve", lambda e: e.tensor_tensor(out=t2[p0:p1, :], in0=stab[p0:p1, :], in1=PS[p0:p1, bk_s, :], op=ALU.mult),
                 reads=[tb, psb[bk_s]], writes=[tmpb[i1]])
            C.op("pool", lambda e: e.tensor_tensor(out=out_ap, in0=t_[p0:p1, :], in1=t2[p0:p1, :], op=ALU.add),
                 reads=[tmpb[i0], tmpb[i1]] + list(extra_reads), writes=[out_buf])

        def rms_norm_fm(nch, col0, nfeat, xTi, xb, dst, dstb):
            for c in range(nch):
                proj_fm(c, Win, col0 + c * 128, 128, xTi, xb)
                C.op("act", lambda e: e.activation(out=sq[c % 2][:], in_=bank(c), func=AF.Square),
                     reads=[psb[c]], writes=[sqb[c % 2]])
                C.op("pe", lambda e: e.matmul(bank(3), ones_f[:, :], sq[c % 2][:], start=(c == 0), stop=(c == nch - 1)),
                     reads=[sqb[c % 2]], writes=[psb[3]], signal=True)
            C.op("act", lambda e: e.activation(out=rr[:], in_=bank(3), func=AF.Ln, bias=float(EPS), scale=float(1.0 / nfeat)),
                 reads=[psb[3]], writes=[rrb])
            C.op("act", lambda e: e.activation(out=rr[:], in_=rr[:], func=AF.Exp, scale=-0.5), reads=[rrb], writes=[rrb])
            for c in range(nch):
                C.op("dve", lambda e: e.tensor_tensor(out=dst[:, c, :], in0=rr[:], in1=bank(c), op=ALU.mult),
                     reads=[rrb, psb[c]], writes=[dstb])

        load_tile(0)
        for ti, (t0, p0) in enumerate(tiles):
            i = ti % 2
            if ti + 1 < len(tiles):
                load_tile(ti + 1)
            xTi, xb, tb = xT[i], xTb_[i], tabb[i]
            c64, s64, c32, s32 = tabs[i][:, 0, :], tabs[i][:, 1, :], tabs[i][:, 2, :], tabs[i][:, 3, :]
            rms_norm_fm(3, O_CQ, 384, xTi, xb, cqn, cqnb)
            for h in range(8):
                bk = 4 + (h % 2) * 2
                for c in range(3):
                    C.op("pe", lambda e: e.matmul(PS[0:96, bk, :], Wuq[:, c, h * 96:(h + 1) * 96], cqn[:, c, :],
                                                  start=(c == 0), stop=(c == 2)),
                         reads=[wb, cqnb], writes=[psb[bk]], signal=(c == 2))
                for c in range(3):
                    C.op("pe", lambda e: e.matmul(PS[64:96, bk + 1, :], Wuqs[:, c, h * 32:(h + 1) * 32], cqn[:, c, :],
                                                  start=(c == 0), stop=(c == 2)),
                         reads=[wb, cqnb], writes=[psb[bk + 1]], signal=(c == 2))
                C.op("act", lambda e: e.copy(out=qas[0:64, h, :], in_=PS[0:64, bk, :]), reads=[psb[bk]], writes=[qasb[h]])
                rope_out(bk, bk + 1, 64, 96, c32, s32, qas[64:96, h, :], tb, qasb[h])
                C.dma("sp", QA[h, :, t0:t0 + 512], qas[0:96, h, :], qasb[h], reads=[qasb[h]])
            rms_norm_fm(2, O_CKV, 256, xTi, xb, ckvn, ckvnb)
            proj_fm(4, Win, O_KPE, 32, xTi, xb, prow=64)
            proj_fm(5, Wsw, S_KPE, 32, xTi, xb, prow=64)
            rope_out(4, 5, 64, 96, c32, s32, kas[64:96, 0, :], tb, kasb[0])
            for h in range(8):
                bk = 6 + (h % 2)
                for c in range(2):
                    C.op("pe", lambda e: e.matmul(PS[0:64, bk, :], Wk[:, c, h * 64:(h + 1) * 64], ckvn[:, c, :],
                                                  start=(c == 0), stop=(c == 1)),
                         reads=[wb, ckvnb], writes=[psb[bk]], signal=(c == 1))
                C.op("act", lambda e: e.copy(out=kas[0:64, h, :], in_=PS[0:64, bk, :]), reads=[psb[bk]], writes=[kasb[h]])
                if h > 0:
                    C.op("pool", lambda e: e.tensor_copy(out=kas[64:96, h, :], in_=kas[64:96, 0, :]),
                         reads=[kasb[0]], writes=[kasb[h]])
                C.dma("sp", KA[h, :, t0:t0 + 512], kas[0:96, h, :], kasb[h], reads=[kasb[h]])
            for s in range(4):
                bk = 4 + (s % 2)
                for c in range(2):
                    C.op("pe", lambda e: e.matmul(bank(bk), ckvn[:, c, s * 128:(s + 1) * 128], Wv[:, c, :],
                                                  start=(c == 0), stop=(c == 1)),
                         reads=[wb, ckvnb], writes=[psb[bk]], signal=(c == 1))
                C.op("act", lambda e: e.copy(out=vas[:, s, :, 0:64], in_=bank(bk).rearrange("p (h j) -> p h j", h=8)),
                     reads=[psb[bk]], writes=[vasb])
            C.dma("sp", VA[t0:t0 + 512, :].rearrange("(s p) f -> p s f", p=128), vas[:].rearrange("p s h j -> p s (h j)"),
                  vasb, reads=[vasb])
            for gi, (oc, osw, ct, st_, dst) in enumerate(((O_QB, S_QB, c32, s32, QB), (O_KB, S_KB, c32, s32, KB),
                                                           (O_QC, S_QC, c64, s64, QC), (O_KC, S_KC, c64, s64, KC))):
                for ch in range(2):
                    ba = (gi * 2 + ch) % 2 * 2
                    proj_fm(ba, Win, oc + ch * 128, 128, xTi, xb)
                    proj_fm(ba + 1, Wsw, osw + ch * 128, 128, xTi, xb)
                    rope_out(ba, ba + 1, 0, 128, ct, st_, rs[:, gi, ch, :], tb, rsb[gi][ch])
                    C.dma("sp", dst[2 * ch:2 * ch + 2, :, t0:t0 + 512].rearrange("h p t -> (h p) t"), rs[:, gi, ch, :],
                          rsb[gi][ch], reads=[rsb[gi][ch]])
            for s in range(4):
                bk = 6 + (s % 2)
                for (col, off_) in ((O_VB, 0), (O_VC, 256)):
                    for k in range(8):
                        C.op("pe", lambda e: e.matmul(PS[:, bk, off_:off_ + 256], xTi[:, k, s * 128:(s + 1) * 128],
                                                      Win[:, k, col:col + 256], start=(k == 0), stop=(k == 7)),
                             reads=[wb, xb], writes=[psb[bk]], signal=(k == 7))
                C.op("act", lambda e: e.copy(out=vbs[:, s, :, 0:64], in_=PS[:, bk, 0:256].rearrange("p (h j) -> p h j", h=4)),
                     reads=[psb[bk]], writes=[vbsb])
                C.op("act", lambda e: e.copy(out=vcs[:, s, :, 0:64], in_=PS[:, bk, 256:512].rearrange("p (h j) -> p h j", h=4)),
                     reads=[psb[bk]], writes=[vcsb])
            C.dma("sp", VB[t0:t0 + 512, :].rearrange("(s p) f -> p s f", p=128), vbs[:].rearrange("p s h j -> p s (h j)"),
                  vbsb, reads=[vbsb])
            C.dma("sp", VC[t0:t0 + 512, :].rearrange("(s p) f -> p s f", p=128), vcs[:].rearrange("p s h j -> p s (h j)"),
                  vcsb, reads=[vcsb])
        C.barrier()

    def finalize_norm(osb, osbb, nbank):
        C.op("dve", lambda e: e.reciprocal(out=osb[64:65, :], in_=osb[64:65, :]), reads=[osbb], writes=[osbb])
        C.op("pe", lambda e: e.matmul(PS[0:64, nbank, :], sel_f[0:65, :], osb[0:65, :], start=True, stop=True),
             reads=[osbb], writes=[psb[nbank]], signal=True)

    def mla_phase():
        for (s0, S) in seqs:
            nkt = S // 128
            vall = C.sb("vall", [128, nkt, 520], BF16)
            vb_ = Buf()
            C.dma("sp", vall[:], VA[s0:s0 + S, :].rearrange("(k p) f -> p k f", p=128), vb_, writes=[vb_])
            kT = [C.sb("kT", [96, S], BF16) for _ in range(2)]
            kTb = bufs(2)
            qT = [C.sb("qT", [96, 512], BF16) for _ in range(2)]
            qTb = bufs(2)
            pT = [C.sb("pT", [128, 1024], BF16) for _ in range(2)]
            pTb = bufs(2)
            osb = [C.sb("osb", [65, 512], F32) for _ in range(2)]
            osbb = bufs(2)
            ost = [C.sb("ost", [64, 512], BF16) for _ in range(2)]
            ostb = bufs(2)
            nq = S // 512
            work = [(h, q) for h in range(8) for q in range(nq)]
            C.dma("sp", kT[0][:], KA[0, :, s0:s0 + S], kTb[0], writes=[kTb[0]])
            C.dma("sp", qT[0][:], QA[0, :, s0:s0 + 512], qTb[0], writes=[qTb[0]])
            scale = 96.0 ** -0.5
            for wi, (h, q) in enumerate(work):
                ki = h % 2
                qi = wi % 2
                if wi + 1 < len(work):
                    h2, q2 = work[wi + 1]
                    if h2 != h:
                        C.dma("sp", kT[h2 % 2][:], KA[h2, :, s0:s0 + S], kTb[h2 % 2], writes=[kTb[h2 % 2]])
                    C.dma("sp", qT[1 - qi][:], QA[h2, :, s0 + q2 * 512:s0 + (q2 + 1) * 512], qTb[1 - qi], writes=[qTb[1 - qi]])
                ob_ = 4 + qi
                npair = nkt // 2

                def scores(kp):
                    b0 = (kp % 2) * 2
                    for j in range(2):
                        kt = kp * 2 + j
                        C.op("pe", lambda e: e.matmul(bank(b0 + j), kT[ki][:, kt * 128:(kt + 1) * 128], qT[qi][:, :],
                                                      start=True, stop=True),
                             reads=[kTb[ki], qTb[qi]], writes=[psb[b0 + j]], signal=(j == 1))

                scores(0)
                for kp in range(npair):
                    if kp + 1 < npair:
                        scores(kp + 1)
                    b0 = (kp % 2) * 2
                    pi = kp % 2
                    C.op("act", lambda e: e.activation(out=pT[pi][:], in_=PS[:, b0:b0 + 2, :].rearrange("p b n -> p (b n)"),
                                                       func=AF.Exp, scale=float(scale)),
                         reads=[psb[b0], psb[b0 + 1]], writes=[pTb[pi]])
                    for j in range(2):
                        kt = kp * 2 + j
                        last = (kp == npair - 1 and j == 1)
                        C.op("pe", lambda e: e.matmul(PS[0:65, ob_, :], vall[:, kt, h * 65:(h + 1) * 65],
                                                      pT[pi][:, j * 512:(j + 1) * 512], start=(kt == 0), stop=last),
                             reads=[vb_, pTb[pi]], writes=[psb[ob_]], signal=(j == 1))
                C.op("dve", lambda e: e.tensor_copy(out=osb[qi][:], in_=PS[0:65, ob_, :]), reads=[psb[ob_]], writes=[osbb[qi]])
                finalize_norm(osb[qi], osbb[qi], 6 + qi)
                C.op("dve", lambda e: e.tensor_tensor(out=ost[qi][:], in0=osb[qi][0:64, :], in1=PS[0:64, 6 + qi, :], op=ALU.mult),
                     reads=[osbb[qi], psb[6 + qi]], writes=[ostb[qi]])
                C.dma("sp", OT[h // 2, (h % 2) * 64:(h % 2) * 64 + 64, s0 + q * 512:s0 + (q + 1) * 512], ost[qi][:],
                      ostb[qi], reads=[ostb[qi]])
            C.barrier()

    def diff_phase(l):
        lambda_init = 0.8 - 0.6 * math.exp(-0.3 * l)
        base0 = C.sb_base
        lt = C.sb("lt", [128, 128], F32)
        lsm = C.sb("lsm", [128, 8], F32)
        gs = C.sb("gs", [64, 1], F32)
        C.sb_base = C.off
        lb = Buf()
        C.dma("sp", lt[:], dlam[l:l + 1, :].partition_broadcast(128), lb, writes=[lb])
        C.dma("sp", gs[:], dsub[l].rearrange("(p o) -> p o", o=1), lb, writes=[lb])
        for i in range(2):
            C.op("dve", lambda e: e.tensor_tensor(out=lt[:, i * 64:i * 64 + 32], in0=lt[:, i * 64:i * 64 + 32],
                                                  in1=lt[:, i * 64 + 32:i * 64 + 64], op=ALU.mult), reads=[lb], writes=[lb])
            C.op("dve", lambda e: e.reduce_sum(out=lsm[:, i:i + 1], in_=lt[:, i * 64:i * 64 + 32], axis=AX.X),
                 reads=[lb], writes=[lb])
        C.op("act", lambda e: e.activation(out=lsm[:, 2:4], in_=lsm[:, 0:2], func=AF.Exp), reads=[lb], writes=[lb])
        C.op("dve", lambda e: e.tensor_tensor(out=lsm[:, 4:5], in0=lsm[:, 3:4], in1=lsm[:, 2:3], op=ALU.subtract),
             reads=[lb], writes=[lb])
        C.op("dve", lambda e: e.tensor_scalar(out=lsm[:, 5:6], in0=lsm[:, 4:5], scalar1=float(-lambda_init), scalar2=None,
                                              op0=ALU.add), reads=[lb], writes=[lb])
        C.op("dve", lambda e: e.tensor_scalar(out=gs[:], in0=gs[:], scalar1=float(1.0 - lambda_init), scalar2=None,
                                              op0=ALU.mult), reads=[lb], writes=[lb])
        neglam = lsm[0:64, 5:6]
        scale = 32.0 ** -0.5
        for (s0, S) in seqs:
            nkt = S // 128
            vall = C.sb("vall", [128, nkt, 260], BF16)
            vb_ = Buf()
            C.dma("sp", vall[:], VB[s0:s0 + S, :].rearrange("(k p) f -> p k f", p=128), vb_, writes=[vb_])
            kT = [C.sb("kT", [64, S], BF16) for _ in range(2)]
            kTb = bufs(2)
            qT = [C.sb("qT", [64, 512], BF16) for _ in range(2)]
            qTb = bufs(2)
            pT = [C.sb("pT", [128, 1024], BF16) for _ in range(2)]
            pTb = bufs(2)
            osb = [C.sb("osb", [65, 2, 512], F32) for _ in range(2)]
            osbb = bufs(2)
            t1 = C.sb("t1", [64, 512], F32)
            t2 = C.sb("t2", [64, 512], F32)
            tb_ = Buf()
            ost = [C.sb("ost", [64, 512], BF16) for _ in range(2)]
            ostb = bufs(2)
            nq = S // 512
            work = [(h, q) for h in range(4) for q in range(nq)]
            C.dma("sp", kT[0][:], KB[0, :, s0:s0 + S], kTb[0], writes=[kTb[0]])
            C.dma("sp", qT[0][:], QB[0, :, s0:s0 + 512], qTb[0], writes=[qTb[0]])
            for wi, (h, q) in enumerate(work):
                ki = h % 2
                qi = wi % 2
                if wi + 1 < len(work):
                    h2, q2 = work[wi + 1]
                    if h2 != h:
                        C.dma("sp", kT[h2 % 2][:], KB[h2, :, s0:s0 + S], kTb[h2 % 2], writes=[kTb[h2 % 2]])
                    C.dma("sp", qT[1 - qi][:], QB[h2, :, s0 + q2 * 512:s0 + (q2 + 1) * 512], qTb[1 - qi], writes=[qTb[1 - qi]])
                o1, o2 = 4, 5

                def scores(kt):
                    b0 = (kt % 2) * 2
                    for c in range(2):
                        C.op("pe", lambda e: e.matmul(bank(b0 + c), kT[ki][c * 32:(c + 1) * 32, kt * 128:(kt + 1) * 128],
                                                      qT[qi][c * 32:(c + 1) * 32, :], start=True, stop=True),
                             reads=[kTb[ki], qTb[qi]], writes=[psb[b0 + c]], signal=(c == 1))

                scores(0)
                for kt in range(nkt):
                    if kt + 1 < nkt:
                        scores(kt + 1)
                    b0 = (kt % 2) * 2
                    pi = kt % 2
                    C.op("act", lambda e: e.activation(out=pT[pi][:], in_=PS[:, b0:b0 + 2, :].rearrange("p b n -> p (b n)"),
                                                       func=AF.Exp, scale=float(scale)),
                         reads=[psb[b0], psb[b0 + 1]], writes=[pTb[pi]])
                    for c in range(2):
                        C.op("pe", lambda e: e.matmul(PS[0:65, o1 + c, :], vall[:, kt, h * 65:(h + 1) * 65],
                                                      pT[pi][:, c * 512:(c + 1) * 512], start=(kt == 0), stop=(kt == nkt - 1)),
                             reads=[vb_, pTb[pi]], writes=[psb[o1 + c]], signal=(c == 1))
                ob2 = osb[qi]
                C.op("dve", lambda e: e.tensor_copy(out=ob2[:], in_=PS[0:65, o1:o1 + 2, :]), reads=[psb[o1], psb[o2]],
                     writes=[osbb[qi]])
                C.op("dve", lambda e: e.reciprocal(out=ob2[64:65, :, :], in_=ob2[64:65, :, :]), reads=[osbb[qi]], writes=[osbb[qi]])
                for c in range(2):
                    C.op("pe", lambda e: e.matmul(PS[0:64, 6 + c, :], sel_f[0:65, :], ob2[0:65, c, :], start=True, stop=True),
                         reads=[osbb[qi]], writes=[psb[6 + c]], signal=True)
                C.op("dve", lambda e: e.tensor_tensor(out=t1[:], in0=ob2[0:64, 0, :], in1=PS[0:64, 6, :], op=ALU.mult),
                     reads=[osbb[qi], psb[6]], writes=[tb_])
                C.op("dve", lambda e: e.tensor_tensor(out=t2[:], in0=ob2[0:64, 1, :], in1=PS[0:64, 7, :], op=ALU.mult),
                     reads=[osbb[qi], psb[7], tb_], writes=[tb_])
                C.op("dve", lambda e: e.scalar_tensor_tensor(out=t1[:], in0=t2[:], scalar=neglam, in1=t1[:], op0=ALU.mult,
                                                             op1=ALU.add), reads=[tb_, lb], writes=[tb_])
                C.op("dve", lambda e: e.tensor_tensor(out=t2[:], in0=t1[:], in1=t1[:], op=ALU.mult), reads=[tb_], writes=[tb_])
                C.op("pe", lambda e: e.matmul(PS[0:64, 6, :], ones_f[0:64, 0:64], t2[:], start=True, stop=True),
                     reads=[tb_], writes=[psb[6]], signal=True)
                C.op("act", lambda e: e.activation(out=t2[:], in_=PS[0:64, 6, :], func=AF.Ln, bias=float(EPS), scale=float(1.0 / 64)),
                     reads=[psb[6], tb_], writes=[tb_])
                C.op("act", lambda e: e.activation(out=t2[:], in_=t2[:], func=AF.Exp, scale=-0.5), reads=[tb_], writes=[tb_])
                C.op("dve", lambda e: e.scalar_tensor_tensor(out=ost[qi][:], in0=t1[:], scalar=gs[:, 0:1], in1=t2[:],
                                                             op0=ALU.mult, op1=ALU.mult), reads=[tb_, lb], writes=[ostb[qi]])
                C.dma("sp", OT[4 + h // 2, (h % 2) * 64:(h % 2) * 64 + 64, s0 + q * 512:s0 + (q + 1) * 512], ost[qi][:],
                      ostb[qi], reads=[ostb[qi]])
            C.barrier()
        C.sb_base = base0
        C.off = base0

    def dil_phase():
        scale = 64.0 ** -0.5
        for (s0, S) in seqs:
            for h in range(4):
                qn = C.sb("qn", [64, S], BF16)
                kn = C.sb("kn", [64, S], BF16)
                qp = C.sb("qp", [64, S], BF16)
                kp_ = C.sb("kp", [64, S], BF16)
                nb = Buf()
                pb = Buf()
                acc = C.sb("acc", [65, S], F32)
                accb = Buf()
                C.dma("sp", qn[:], QC[h, :, s0:s0 + S], nb, writes=[nb])
                C.dma("sp", kn[:], KC[h, :, s0:s0 + S], nb, writes=[nb])
                pT = [C.sb("pT", [128, 256], BF16) for _ in range(2)]
                pTb = bufs(2)
                ost = [C.sb("ost", [64, 512], BF16) for _ in range(2)]
                ostb = bufs(2)
                cnt = 0
                for d in (1, 4, 16):
                    L = S // d
                    nt = L // 128
                    vp = C.sb("vp%d" % d, [128, d, nt + 1, 65], BF16)
                    vpb = Buf()
                    for r in range(d):
                        def rows(k0, n):
                            tok0 = s0 + k0 * d + r
                            if d == 1:
                                return VC[tok0:tok0 + n, h * 65:(h + 1) * 65]
                            return VC[tok0:tok0 + (n - 1) * d + 1:d, h * 65:(h + 1) * 65]
                        C.dma("sp", vp[64:128, r, 0, :], rows(0, 64), vpb, writes=[vpb])
                        if nt > 1:
                            C.dma("sp", vp[:, r, 1:nt, :], rows(64, (nt - 1) * 128).rearrange("(j p) f -> p j f", p=128),
                                  vpb, writes=[vpb])
                        C.dma("sp", vp[0:64, r, nt, :], rows(L - 64, 64), vpb, writes=[vpb])
                    if d == 1:
                        qd, kd, db = qn, kn, nb
                    else:
                        C.op("pool", lambda e: e.tensor_copy(out=qp[:].rearrange("p (r i) -> p r i", r=d),
                                                             in_=qn[:].rearrange("p (i r) -> p r i", r=d)),
                             reads=[nb], writes=[pb])
                        C.op("pool", lambda e: e.tensor_copy(out=kp_[:].rearrange("p (r i) -> p r i", r=d),
                                                             in_=kn[:].rearrange("p (i r) -> p r i", r=d)),
                             reads=[nb], writes=[pb])
                        qd, kd, db = qp, kp_, pb
                    accv = acc[:].rearrange("p (i r) -> p r i", r=d)
                    for r in range(d):
                        for jp in range(nt + 1):
                            lo = 64 if jp == 0 else 0
                            hi = 64 if jp == nt else 128
                            q_lo = max(jp - 1, 0) * 128
                            q_hi = min(jp + 1, nt) * 128
                            nqc = q_hi - q_lo
                            mc0 = 128 if jp == 0 else 0
                            kbase = r * L + 128 * jp - 64
                            sbk = cnt % 2
                            pi = cnt % 2
                            cnt += 1
                            C.op("pe", lambda e: e.matmul(PS[lo:hi, sbk, 0:nqc], kd[:, kbase + lo:kbase + hi],
                                                          qd[:, r * L + q_lo:r * L + q_hi], start=True, stop=True),
                                 reads=[db], writes=[psb[sbk]], signal=True)
                            C.op("act", lambda e: e.activation(out=pT[pi][lo:hi, 0:nqc], in_=PS[lo:hi, sbk, 0:nqc], func=AF.Exp,
                                                               scale=float(scale)), reads=[psb[sbk]], writes=[pTb[pi]])
                            C.op("pool", lambda e: e.tensor_tensor(out=pT[pi][lo:hi, 0:nqc], in0=pT[pi][lo:hi, 0:nqc],
                                                                   in1=mask[lo:hi, mc0:mc0 + nqc], op=ALU.mult),
                                 reads=[pTb[pi]], writes=[pTb[pi]])
                            col = 0
                            for qt in range(max(jp - 1, 0), min(jp + 1, nt)):
                                first = (qt == jp)
                                obk = 4 + (qt % 2)
                                C.op("pe", lambda e: e.matmul(PS[0:65, obk, 0:128], vp[lo:hi, r, jp, :],
                                                              pT[pi][lo:hi, col:col + 128], start=first, stop=(not first)),
                                     reads=[vpb, pTb[pi]], writes=[psb[obk]], signal=True)
                                if not first:
                                    dst = accv[:, r, qt * 128:(qt + 1) * 128]
                                    if d == 1:
                                        C.op("dve", lambda e: e.tensor_copy(out=dst, in_=PS[0:65, obk, 0:128]),
                                             reads=[psb[obk]], writes=[accb])
                                    else:
                                        C.op("dve", lambda e: e.tensor_tensor(out=dst, in0=dst, in1=PS[0:65, obk, 0:128], op=ALU.add),
                                             reads=[psb[obk], accb], writes=[accb])
                                col += 128
                for q in range(S // 512):
                    qi = q % 2
                    sl = slice(q * 512, (q + 1) * 512)
                    C.op("dve", lambda e: e.reciprocal(out=acc[64:65, sl], in_=acc[64:65, sl]), reads=[accb], writes=[accb])
                    C.op("pe", lambda e: e.matmul(PS[0:64, 6 + qi, :], sel_f[0:65, :], acc[0:65, sl], start=True, stop=True),
                         reads=[accb], writes=[psb[6 + qi]], signal=True)
                    C.op("dve", lambda e: e.tensor_tensor(out=ost[qi][:], in0=acc[0:64, sl], in1=PS[0:64, 6 + qi, :], op=ALU.mult),
                         reads=[accb, psb[6 + qi]], writes=[ostb[qi]])
                    C.dma("sp", OT[6 + h // 2, (h % 2) * 64:(h % 2) * 64 + 64, s0 + q * 512:s0 + (q + 1) * 512], ost[qi][:],
                          ostb[qi], reads=[ostb[qi]])
                C.barrier()

    def out_phase(l, xsrc, xdst, xTdst):
        Wo = C.sb("Wo", [128, 8, D], BF16)
        wb = Buf()
        C.dma("pool", Wo[:], w_out[l].rearrange("(k p) f -> p k f", p=128), wb, writes=[wb])
        oT = [C.sb("oT", [128, 8, 512], BF16) for _ in range(2)]
        oTb = bufs(2)
        L = ln_setup(l, 1)
        ntile = NT // 512
        C.dma("sp", oT[0][:], OT[:, :, 0:512].rearrange("c p t -> p c t"), oTb[0], writes=[oTb[0]])
        for t in range(ntile):
            t0 = t * 512
            i = t % 2
            if t + 1 < ntile:
                C.dma("sp", oT[1 - i][:], OT[:, :, t0 + 512:t0 + 1024].rearrange("c p t -> p c t"), oTb[1 - i], writes=[oTb[1 - i]])
            ln_load_x(L, xsrc, t0)
            for s in range(4):
                for hf in range(2):
                    yb = 4 + hf
                    for c in range(8):
                        C.op("pe", lambda e: e.matmul(bank(yb), oT[i][:, c, s * 128:(s + 1) * 128], Wo[:, c, hf * 512:(hf + 1) * 512],
                                                      start=(c == 0), stop=(c == 7)),
                             reads=[oTb[i], wb], writes=[psb[yb]], signal=(c == 7))
                if s < 3:
                    ln_load_x(L, xsrc, t0 + (s + 1) * 128)
                ln_epilogue(L, (4, 5), ALPHA, 1.0, xdst, xTdst, t0 + s * 128, 6)
        C.barrier()

    phase0(xin, xTa)
    xcur, xTcur = xin, xTa
    xalt = [xa, xb_]
    xTalt = [xTb, xTa]
    step = 0
    for l in range(depth):
        last_layer = (l == depth - 1)
        xd, xTd = xalt[step % 2], xTalt[step % 2]
        ffn_phase(l, 0, xcur, xTcur, xd, xTd)
        xcur, xTcur = xd, xTd
        step += 1
        proj_phase(l, xTcur)
        mla_phase()
        diff_phase(l)
        dil_phase()
        xd, xTd = xalt[step % 2], xTalt[step % 2]
        out_phase(l, xcur, xd, xTd)
        xcur, xTcur = xd, xTd
        step += 1
        xd, xTd = (y, None) if last_layer else (xalt[step % 2], xTalt[step % 2])
        ffn_phase(l, 1, xcur, xTcur, xd, xTd)
        xcur, xTcur = xd, xTd
        step += 1
    return nc


def _swap_cols(w, base, width, dim):
    blk = w[..., base:base + width].reshape(w.shape[:-1] + (width // dim, dim))
    return np.concatenate([blk[..., dim // 2:], blk[..., :dim // 2]], axis=-1).reshape(w.shape[:-1] + (width,))


def _tables():
    def tab(dim):
        half = dim // 2
        inv = (1.0 / (10000.0 ** (np.arange(0, dim, 2, dtype=np.float32) / np.float32(dim)))).astype(np.float32)
        ang = np.arange(MAXPOS, dtype=np.float32)[:, None] * inv[None, :]
        c, s = np.cos(ang).astype(np.float32), np.sin(ang).astype(np.float32)
        rows = np.arange(128)
        i = rows % dim
        f = i % half
        sign = np.where(i < half, -1.0, 1.0).astype(np.float32)
        return np.ascontiguousarray(c[:, f].T), np.ascontiguousarray((s[:, f] * sign[None, :]).T)
    c64, s64 = tab(64)
    c32, s32 = tab(32)
    ident = np.eye(128, dtype=np.float32).astype(ml_dtypes.bfloat16)
    kk = np.arange(128)[:, None]
    qq = np.arange(256)[None, :]
    band = ((qq - kk >= 0) & (qq - kk <= 128)).astype(np.float32).astype(ml_dtypes.bfloat16)
    return c64, s64, c32, s32, ident, band


def make_in_maps(x_prompt, x_sample, ln_g, ln_b, ffn_w_gate, ffn_w_up, ffn_w_down, w_in, mla_q_norm, mla_kv_norm,
                 mla_w_uq, mla_w_ukv, diff_lambda, diff_subln, w_out):
    f = lambda a: np.ascontiguousarray(np.asarray(a, dtype=np.float32))
    w_in = f(w_in)
    w_sw = np.concatenate([_swap_cols(w_in, O_KPE, 32, 32), _swap_cols(w_in, O_QB, 256, 32), _swap_cols(w_in, O_KB, 256, 32),
                           _swap_cols(w_in, O_QC, 256, 64), _swap_cols(w_in, O_KC, 256, 64)], axis=-1)
    w_uq = f(mla_w_uq)
    uq4 = w_uq.reshape(w_uq.shape[0], 384, 8, 96)[..., 64:96]
    w_uqs = np.concatenate([uq4[..., 16:], uq4[..., :16]], axis=-1).reshape(w_uq.shape[0], 384, 256)
    c64, s64, c32, s32, ident, band = _tables()
    shared = dict(ln_g=f(ln_g), ln_b=f(ln_b), wg=f(ffn_w_gate), wu=f(ffn_w_up), wd=f(ffn_w_down), w_in=w_in,
                  w_sw=np.ascontiguousarray(w_sw), q_norm=f(mla_q_norm), kv_norm=f(mla_kv_norm), w_uq=w_uq,
                  w_uqs=np.ascontiguousarray(w_uqs), w_ukv=f(mla_w_ukv), dlam=f(diff_lambda).reshape(-1, 128),
                  dsub=f(diff_subln), w_out=f(w_out), cos64=c64, sin64=s64, cos32=c32, sin32=s32, ident=ident, bandmask=band)
    xp, xs = f(x_prompt), f(x_sample)
    maps = []
    for b in range(xp.shape[0]):
        m = dict(shared)
        m["xin"] = np.ascontiguousarray(np.concatenate([xp[b], xs[b]], axis=0))
        maps.append(m)
    return maps


def kernel(**inputs):
    SP = inputs["x_prompt"].shape[1]
    SS = inputs["x_sample"].shape[1]
    nb = inputs["x_prompt"].shape[0]
    nc = build(SP, SS, DEPTH)
    maps = make_in_maps(**inputs)
    res = run_bass_kernel_spmd(nc, maps, core_ids=list(range(nb)))
    ys = [np.asarray(r["y"], dtype=np.float32) for r in res.results]
    y_prompt = np.stack([yy[:SP] for yy in ys], axis=0)
    y_sample = np.stack([yy[SP:] for yy in ys], axis=0)
    return (y_prompt, y_sample)
```

```python
import math
import numpy as np
import ml_dtypes
import concourse.bass as bass
import concourse.mybir as mybir
from concourse.bass_utils import run_bass_kernel_spmd

F32 = mybir.dt.float32
BF16 = mybir.dt.bfloat16
AF = mybir.ActivationFunctionType
ALU = mybir.AluOpType
AX = mybir.AxisListType

D = 1024
DFF = 2816
NFF = 22
DEPTH = 4
ALPHA = (2 * DEPTH) ** 0.25
EPS = 1e-5
INC = 2208
SWC = 1056
O_CQ, O_CKV, O_KPE, O_QB, O_KB, O_VB, O_QC, O_KC, O_VC = 0, 384, 640, 672, 928, 1184, 1440, 1696, 1952
S_KPE, S_QB, S_KB, S_QC, S_KC = 0, 32, 288, 544, 800
MAXPOS = 8192


class Buf:
    __slots__ = ("w", "r")

    def __init__(self):
        self.w = None
        self.r = {}


def bufs(n):
    return [Buf() for _ in range(n)]


class Ctx:
    SB_LIMIT = 229248

    def __init__(self, nc):
        self.nc = nc
        self.eng = {"pe": nc.tensor, "act": nc.scalar, "dve": nc.vector, "pool": nc.gpsimd, "sp": nc.sync}
        self.sems = {}
        self.nsig = {}
        for k in ("pe", "act", "dve", "pool"):
            self.sems[k] = nc.alloc_semaphore("sem_" + k)
            self.nsig[k] = 0
        self.waited = {k: {} for k in self.eng}
        self.dcount = {}
        self.dma_free = []
        self.slot2sem = {}
        self.sb_base = 16640
        self.off = self.sb_base
        self.uid = 0
        self.keep = []

    def sb(self, name, shape, dtype):
        isz = 2 if dtype == BF16 else 4
        n = 1
        for s in shape[1:]:
            n *= s
        size = (n * isz + 63) // 64 * 64
        off = self.off
        self.off += size
        assert self.off <= self.SB_LIMIT, ("SBUF overflow", name, self.off)
        self.uid += 1
        return self.nc.alloc_sbuf_tensor_at("%s_%d" % (name, self.uid), list(shape), dtype, offset=off)

    def persist(self):
        self.sb_base = self.off

    def _deps(self, reads, writes):
        toks = []
        for b in reads:
            if b.w is not None:
                toks.append(b.w)
        for b in writes:
            if b.w is not None:
                toks.append(b.w)
            toks.extend(b.r.values())
        return toks

    def _wait(self, e, toks):
        need = {}
        for (sn, val, src) in toks:
            if src == "pe" and e == "pe":
                continue
            if val > need.get(sn, 0):
                need[sn] = val
        w = self.waited[e]
        for sn, val in need.items():
            if w.get(sn, 0) >= val:
                continue
            self.eng[e].wait_ge(self.sems[sn], val)
            w[sn] = val

    def _commit(self, tok, reads, writes):
        for b in reads:
            o = b.r.get(tok[0])
            if o is None or o[1] < tok[1]:
                b.r[tok[0]] = tok
        for b in writes:
            b.w = tok
            b.r = {}

    def op(self, e, fn, reads=(), writes=(), signal=True):
        self._wait(e, self._deps(reads, writes))
        ins = fn(self.eng[e])
        if signal:
            self.nsig[e] += 1
            ins.then_inc(self.sems[e], 1)
            tok = (e, self.nsig[e], e)
        else:
            tok = (e, self.nsig[e] + 1, e)
        self._commit(tok, reads, writes)

    def _slot_sem(self, slot):
        if slot not in self.slot2sem:
            if self.dma_free:
                name = self.dma_free.pop()
            else:
                name = "dsem%d" % len(self.dcount)
                self.sems[name] = self.nc.alloc_semaphore(name)
                self.dcount[name] = 0
            self.slot2sem[slot] = name
        return self.slot2sem[slot]

    def dma(self, q, out, in_, slot, reads=(), writes=(), **kw):
        self._wait(q, self._deps(reads, writes))
        sn = self._slot_sem(slot)
        ins = self.eng[q].dma_start(out=out, in_=in_, **kw)
        self.dcount[sn] += 16
        ins.then_inc(self.sems[sn], 16)
        tok = (sn, self.dcount[sn], "dma")
        self._commit(tok, reads, writes)

    def barrier(self):
        for e in self.eng:
            w = self.waited[e]
            for k in ("pe", "act", "dve", "pool"):
                if w.get(k, 0) < self.nsig[k]:
                    self.eng[e].wait_ge(self.sems[k], self.nsig[k])
                    w[k] = self.nsig[k]
            for sn, c in self.dcount.items():
                if w.get(sn, 0) < c:
                    self.eng[e].wait_ge(self.sems[sn], c)
                    w[sn] = c
        self.slot2sem = {}
        self.dma_free = list(self.dcount.keys())
        self.off = self.sb_base


def build(SP, SS, depth, dbg=False):
    NT = SP + SS
    seqs = [(0, SP), (SP, SS)]
    nc = bass.Bass("TRN2", target_bir_lowering=False)

    def din(name, shape, dtype=F32):
        return nc.dram_tensor(name, list(shape), dtype, kind="ExternalInput").ap()

    def dscr(name, shape, dtype):
        return nc.dram_tensor(name, list(shape), dtype, kind=("ExternalOutput" if dbg else "Internal")).ap()

    xin = din("xin", [NT, D])
    ln_g = din("ln_g", [DEPTH, 3, D])
    ln_b = din("ln_b", [DEPTH, 3, D])
    wg = din("wg", [DEPTH, 2, D, DFF])
    wu = din("wu", [DEPTH, 2, D, DFF])
    wd = din("wd", [DEPTH, 2, DFF, D])
    w_in = din("w_in", [DEPTH, D, INC])
    w_sw = din("w_sw", [DEPTH, D, SWC])
    q_norm = din("q_norm", [DEPTH, 384])
    kv_norm = din("kv_norm", [DEPTH, 256])
    w_uq = din("w_uq", [DEPTH, 384, 768])
    w_uqs = din("w_uqs", [DEPTH, 384, 256])
    w_ukv = din("w_ukv", [DEPTH, 256, 1024])
    dlam = din("dlam", [DEPTH, 128])
    dsub = din("dsub", [DEPTH, 64])
    w_out = din("w_out", [DEPTH, D, D])
    cos64 = din("cos64", [128, MAXPOS])
    sin64 = din("sin64", [128, MAXPOS])
    cos32 = din("cos32", [128, MAXPOS])
    sin32 = din("sin32", [128, MAXPOS])
    ident_d = din("ident", [128, 128], BF16)
    mask_d = din("bandmask", [128, 256], BF16)

    y = nc.dram_tensor("y", [NT, D], F32, kind="ExternalOutput").ap()
    xa = dscr("xa", [NT, D], F32)
    xb_ = dscr("xb", [NT, D], F32)
    xTa = dscr("xTa", [8, 128, NT], BF16)
    xTb = dscr("xTb", [8, 128, NT], BF16)
    QA = dscr("QA", [8, 96, NT], BF16)
    KA = dscr("KA", [8, 96, NT], BF16)
    VA = dscr("VA", [NT, 8 * 65], BF16)
    QB = dscr("QB", [4, 64, NT], BF16)
    KB = dscr("KB", [4, 64, NT], BF16)
    VB = dscr("VB", [NT, 4 * 65], BF16)
    QC = dscr("QC", [4, 64, NT], BF16)
    KC = dscr("KC", [4, 64, NT], BF16)
    VC = dscr("VC", [NT, 4 * 65], BF16)
    OT = dscr("OT", [8, 128, NT], BF16)

    C = Ctx(nc)
    PS = nc.alloc_psum_tensor("ps", [128, 8, 512], F32)
    psb = bufs(8)

    def bank(b):
        return PS[:, b, :]

    def bank16(b):
        return PS[:, b, :].bitcast(BF16)

    ident = C.sb("ident", [128, 128], BF16)
    mask = C.sb("mask", [128, 256], BF16)
    ones_f = C.sb("ones_f", [128, 128], F32)
    sel_f = C.sb("sel_f", [128, 64], F32)
    mhalf = C.sb("mhalf", [128, 1], F32)
    C.persist()
    cb = Buf()
    C.dma("sp", ident[:], ident_d[:, :], cb, writes=[cb])
    C.dma("sp", mask[:], mask_d[:, :], cb, writes=[cb])
    C.op("dve", lambda e: e.memset(ones_f[:], 1.0), writes=[cb])
    C.op("dve", lambda e: e.memset(sel_f[:], 0.0), writes=[cb])
    C.op("dve", lambda e: e.memset(sel_f[64:65, :], 1.0), writes=[cb])
    C.op("dve", lambda e: e.memset(mhalf[:], -0.5), writes=[cb])
    C.barrier()

    def ln_setup(l, j):
        g_t = C.sb("g_t", [128, D], F32)
        b_t = C.sb("b_t", [128, D], F32)
        gb = Buf()
        C.dma("sp", g_t[:], ln_g[l, j:j + 1, :].partition_broadcast(128), gb, writes=[gb])
        C.dma("sp", b_t[:], ln_b[l, j:j + 1, :].partition_broadcast(128), gb, writes=[gb])
        xbuf = [C.sb("xbuf", [128, D], F32) for _ in range(2)]
        ob = [C.sb("ob", [128, D], BF16) for _ in range(2)]
        xts = [C.sb("xts", [128, 8, 128], BF16) for _ in range(2)]
        st = C.sb("stats", [128, 2, 6], F32)
        mv = C.sb("mv", [128, 2], F32)
        sm = C.sb("sm", [128, 4], F32)
        return dict(g=g_t, b=b_t, gb=gb, xbuf=xbuf, xbb=bufs(2), ob=ob, obb=bufs(2), xts=xts, xtsb=bufs(2),
                    st=st, stb=Buf(), mv=mv, sm=sm, cnt=0, lcnt=0)

    def ln_load_x(L, xsrc, t0):
        i = L["lcnt"] % 2
        L["lcnt"] += 1
        C.dma("sp", L["xbuf"][i][:], xsrc[t0:t0 + 128, :], L["xbb"][i], writes=[L["xbb"][i]])

    def ln_epilogue(L, ybanks, res_scale, k, xdst, xTdst, t0, trbank):
        i = L["cnt"] % 2
        L["cnt"] += 1
        xbuf = L["xbuf"][i]
        xbb = L["xbb"][i]
        for hf in range(2):
            sl = slice(hf * 512, (hf + 1) * 512)
            C.op("dve", lambda e: e.scalar_tensor_tensor(out=xbuf[:, sl], in0=xbuf[:, sl], scalar=float(res_scale),
                                                         in1=bank(ybanks[hf]), op0=ALU.mult, op1=ALU.add),
                 reads=[xbb, psb[ybanks[hf]]], writes=[xbb])
        stb = L["stb"]
        for hf in range(2):
            sl = slice(hf * 512, (hf + 1) * 512)
            C.op("dve", lambda e: e.bn_stats(out=L["st"][:, hf, :], in_=xbuf[:, sl]), reads=[xbb], writes=[stb])
        C.op("dve", lambda e: e.bn_aggr(out=L["mv"][:], in_=L["st"][:]), reads=[stb], writes=[stb])
        sm = L["sm"]
        C.op("dve", lambda e: e.tensor_scalar(out=sm[:, 0:1], in0=L["mv"][:, 1:2], scalar1=float(1.0 / (k * k)),
                                              scalar2=float(EPS), op0=ALU.mult, op1=ALU.add), reads=[stb], writes=[stb])
        C.op("pool", lambda e: e.tensor_tensor(out=sm[:, 1:2], in0=sm[:, 0:1], in1=mhalf[:], op=ALU.pow),
             reads=[stb], writes=[stb])
        C.op("dve", lambda e: e.tensor_scalar(out=sm[:, 2:3], in0=sm[:, 1:2], scalar1=float(1.0 / k), scalar2=None,
                                              op0=ALU.mult), reads=[stb], writes=[stb])
        C.op("dve", lambda e: e.tensor_scalar(out=sm[:, 3:4], in0=L["mv"][:, 0:1], scalar1=sm[:, 2:3], scalar2=-1.0,
                                              op0=ALU.mult, op1=ALU.mult), reads=[stb], writes=[stb])
        C.op("act", lambda e: e.activation(out=xbuf[:], in_=xbuf[:], func=AF.Identity, bias=sm[:, 3:4], scale=sm[:, 2:3]),
             reads=[xbb, stb], writes=[xbb])
        C.op("pool", lambda e: e.tensor_tensor(out=xbuf[:], in0=xbuf[:], in1=L["g"][:], op=ALU.mult),
             reads=[xbb, L["gb"]], writes=[xbb])
        C.op("pool", lambda e: e.tensor_tensor(out=xbuf[:], in0=xbuf[:], in1=L["b"][:], op=ALU.add),
             reads=[xbb, L["gb"]], writes=[xbb])
        C.dma("sp", xdst[t0:t0 + 128, :], xbuf[:], xbb, reads=[xbb])
        if xTdst is not None:
            obb = L["obb"][i]
            ob = L["ob"][i]
            C.op("act", lambda e: e.copy(out=ob[:], in_=xbuf[:]), reads=[xbb], writes=[obb])
            xts_i, xtsb_i = L["xts"][i], L["xtsb"][i]
            return lambda: transpose_store(ob, obb, xts_i, xtsb_i, xTdst, t0, trbank)
        return None

    def transpose_store(ob, obb, xts, xtsb, xTdst, t0, trbank):
        tb = bank16(trbank)
        for c in range(8):
            C.op("pe", lambda e: e.transpose(out=tb[:, c * 128:(c + 1) * 128], in_=ob[:, c * 128:(c + 1) * 128],
                                             identity=ident[:]),
                 reads=[obb], writes=[psb[trbank]], signal=(c == 7))
        C.op("dve", lambda e: e.tensor_copy(out=xts[:].rearrange("p c t -> p (c t)"), in_=tb[:, :]),
             reads=[psb[trbank]], writes=[xtsb])
        C.dma("sp", xTdst[:, :, t0:t0 + 128].rearrange("c p t -> p c t"), xts[:], xtsb, reads=[xtsb])

    def phase0(xsrc, xTdst):
        xbuf = [C.sb("p0x", [128, D], F32) for _ in range(2)]
        xbb = bufs(2)
        ob = [C.sb("p0o", [128, D], BF16) for _ in range(2)]
        obb = bufs(2)
        xts = [C.sb("p0t", [128, 8, 128], BF16) for _ in range(2)]
        xtsb = bufs(2)
        n = NT // 128
        C.dma("sp", xbuf[0][:], xsrc[0:128, :], xbb[0], writes=[xbb[0]])
        for s in range(n):
            i = s % 2
            if s + 1 < n:
                C.dma("sp", xbuf[1 - i][:], xsrc[(s + 1) * 128:(s + 2) * 128, :], xbb[1 - i], writes=[xbb[1 - i]])
            C.op("act", lambda e: e.copy(out=ob[i][:], in_=xbuf[i][:]), reads=[xbb[i]], writes=[obb[i]])
            transpose_store(ob[i], obb[i], xts[i], xtsb[i], xTdst, s * 128, 6 + i)
        C.barrier()

    def ffn_phase(l, j, xsrc, xTsrc, xdst, xTdst):
        Wg = C.sb("Wg", [128, 8, DFF], BF16)
        Wu = C.sb("Wu", [128, 8, DFF], BF16)
        Wd = C.sb("Wd", [128, NFF, D], BF16)
        wb = Buf()
        for (W, src) in ((Wg, wg), (Wu, wu)):
            for hf in range(2):
                C.dma("pool", W[:, :, hf * 1408:(hf + 1) * 1408],
                      src[l, j, :, hf * 1408:(hf + 1) * 1408].rearrange("(k p) f -> p k f", p=128), wb, writes=[wb])
        for hf in range(2):
            C.dma("pool", Wd[:, hf * 11:(hf + 1) * 11, :],
                  wd[l, j, hf * 1408:(hf + 1) * 1408, :].rearrange("(c p) d -> p c d", p=128), wb, writes=[wb])
        xT = C.sb("xTin", [128, 8, 512], BF16)
        xTb_ = Buf()
        hT = C.sb("hT", [128, NFF, 512], BF16)
        hTb = bufs(NFF)
        sil = [C.sb("sil", [128, 512], F32) for _ in range(2)]
        silb = bufs(2)
        L = ln_setup(l, 0 if j == 0 else 2)
        pend = [None]
        ntile = NT // 512
        C.dma("sp", xT[:], xTsrc[:, :, 0:512].rearrange("c p t -> p c t"), xTb_, writes=[xTb_])
        for t in range(ntile):
            t0 = t * 512
            ln_load_x(L, xsrc, t0)
            for c in range(NFF):
                gb_, ub_ = c % 2, 2 + c % 2
                for k in range(8):
                    C.op("pe", lambda e: e.matmul(bank(gb_), Wg[:, k, c * 128:(c + 1) * 128], xT[:, k, :],
                                                  start=(k == 0), stop=(k == 7)),
                         reads=[wb, xTb_], writes=[psb[gb_]], signal=(k == 7))
                for k in range(8):
                    C.op("pe", lambda e: e.matmul(bank(ub_), Wu[:, k, c * 128:(c + 1) * 128], xT[:, k, :],
                                                  start=(k == 0), stop=(k == 7)),
                         reads=[wb, xTb_], writes=[psb[ub_]], signal=(k == 7))
                C.op("act", lambda e: e.activation(out=sil[c % 2][:], in_=bank(gb_), func=AF.Silu),
                     reads=[psb[gb_]], writes=[silb[c % 2]])
                C.op("dve", lambda e: e.tensor_tensor(out=hT[:, c, :], in0=sil[c % 2][:], in1=bank(ub_), op=ALU.mult),
                     reads=[silb[c % 2], psb[ub_]], writes=[hTb[c]])
                if c == 2 and pend[0] is not None:
                    pend[0]()
                    pend[0] = None
            if t + 1 < ntile:
                C.dma("sp", xT[:], xTsrc[:, :, t0 + 512:t0 + 1024].rearrange("c p t -> p c t"), xTb_, writes=[xTb_])
            for s in range(4):
                for hf in range(2):
                    yb = 4 + hf
                    for c in range(NFF):
                        C.op("pe", lambda e: e.matmul(bank(yb), hT[:, c, s * 128:(s + 1) * 128],
                                                      Wd[:, c, hf * 512:(hf + 1) * 512], start=(c == 0), stop=(c == NFF - 1)),
                             reads=[hTb[c], wb], writes=[psb[yb]], signal=(c == NFF - 1))
                if s < 3:
                    ln_load_x(L, xsrc, t0 + (s + 1) * 128)
                if pend[0] is not None:
                    pend[0]()
                pend[0] = ln_epilogue(L, (4, 5), 2.0 * ALPHA, 2.0, xdst, xTdst, t0 + s * 128, 6)
        if pend[0] is not None:
            pend[0]()
        C.barrier()

    def proj_phase(l, xTsrc):
        Win = C.sb("Win", [128, 8, INC], BF16)
        Wsw = C.sb("Wsw", [128, 8, SWC], BF16)
        Wuq = C.sb("Wuq", [128, 3, 768], BF16)
        Wuqs = C.sb("Wuqs", [128, 3, 256], BF16)
        Wk = C.sb("Wk", [128, 2, 512], BF16)
        Wv = C.sb("Wv", [128, 2, 512], BF16)
        wb = Buf()
        for hf in range(2):
            C.dma("pool", Win[:, :, hf * 1104:(hf + 1) * 1104],
                  w_in[l, :, hf * 1104:(hf + 1) * 1104].rearrange("(k p) f -> p k f", p=128), wb, writes=[wb])
        C.dma("pool", Wsw[:], w_sw[l].rearrange("(k p) f -> p k f", p=128), wb, writes=[wb])
        stg = C.sb("stg", [128, 3, 1024], F32)
        stg2 = C.sb("stg2", [128, 3, 256], F32)
        stg3 = C.sb("stg3", [128, 2, 1024], F32)
        gq = C.sb("gq", [128, 3], F32)
        gkv = C.sb("gkv", [128, 2], F32)
        sb_ = Buf()
        C.dma("sp", stg[:, :, 0:768], w_uq[l].rearrange("(c p) f -> p c f", p=128), sb_, writes=[sb_])
        C.dma("sp", stg2[:], w_uqs[l].rearrange("(c p) f -> p c f", p=128), sb_, writes=[sb_])
        C.dma("sp", stg3[:], w_ukv[l].rearrange("(c p) f -> p c f", p=128), sb_, writes=[sb_])
        C.dma("sp", gq[:], q_norm[l].rearrange("(c p) -> p c", p=128), sb_, writes=[sb_], allow_slow_non_contiguous=True)
        C.dma("sp", gkv[:], kv_norm[l].rearrange("(c p) -> p c", p=128), sb_, writes=[sb_], allow_slow_non_contiguous=True)
        for c in range(3):
            C.op("dve", lambda e: e.tensor_scalar(out=Wuq[:, c, :], in0=stg[:, c, 0:768], scalar1=gq[:, c:c + 1],
                                                  scalar2=None, op0=ALU.mult), reads=[sb_], writes=[wb])
            C.op("dve", lambda e: e.tensor_scalar(out=Wuqs[:, c, :], in0=stg2[:, c, :], scalar1=gq[:, c:c + 1],
                                                  scalar2=None, op0=ALU.mult), reads=[sb_], writes=[wb])
        for c in range(2):
            src = stg3[:, c, :].rearrange("p (h t j) -> p h t j", h=8, t=2)
            C.op("dve", lambda e: e.tensor_scalar(out=Wk[:, c, :].rearrange("p (h j) -> p h j", h=8), in0=src[:, :, 0, :],
                                                  scalar1=gkv[:, c:c + 1], scalar2=None, op0=ALU.mult),
                 reads=[sb_], writes=[wb])
            C.op("dve", lambda e: e.tensor_scalar(out=Wv[:, c, :].rearrange("p (h j) -> p h j", h=8), in0=src[:, :, 1, :],
                                                  scalar1=gkv[:, c:c + 1], scalar2=None, op0=ALU.mult),
                 reads=[sb_], writes=[wb])

        xT = [C.sb("xTin", [128, 8, 512], BF16) for _ in range(2)]
        xTb_ = bufs(2)
        tabs = [C.sb("tabs", [128, 4, 512], F32) for _ in range(2)]
        tabb = bufs(2)
        cqn = C.sb("cqn", [128, 3, 512], BF16)
        cqnb = Buf()
        ckvn = C.sb("ckvn", [128, 2, 512], BF16)
        ckvnb = Buf()
        sq = [C.sb("sq", [128, 512], F32) for _ in range(2)]
        sqb = bufs(2)
        rr = C.sb("rr", [128, 512], F32)
        rrb = Buf()
        qas = C.sb("qas", [128, 8, 512], BF16)
        qasb = bufs(8)
        kas = C.sb("kas", [128, 8, 512], BF16)
        kasb = bufs(8)
        vas = C.sb("vas", [128, 4, 8, 65], BF16)
        vasb = Buf()
        rs = C.sb("rs", [128, 4, 2, 512], BF16)
        rsb = [bufs(2) for _ in range(4)]
        vbs = C.sb("vbs", [128, 4, 4, 65], BF16)
        vbsb = Buf()
        vcs = C.sb("vcs", [128, 4, 4, 65], BF16)
        vcsb = Buf()
        tmp = [C.sb("tmp", [128, 512], F32) for _ in range(2)]
        tmpb = bufs(2)
        C.op("pool", lambda e: e.memset(vas[:], 1.0), writes=[vasb])
        C.op("pool", lambda e: e.memset(vbs[:], 1.0), writes=[vbsb])
        C.op("pool", lambda e: e.memset(vcs[:], 1.0), writes=[vcsb])

        tiles = []
        for (s0, S) in seqs:
            for q in range(S // 512):
                tiles.append((s0 + q * 512, q * 512))

        def load_tile(ti):
            t0, p0 = tiles[ti]
            i = ti % 2
            C.dma("sp", xT[i][:], xTsrc[:, :, t0:t0 + 512].rearrange("c p t -> p c t"), xTb_[i], writes=[xTb_[i]])
            for n_, tab in enumerate((cos64, sin64, cos32, sin32)):
                C.dma("sp", tabs[i][:, n_, :], tab[:, p0:p0 + 512], tabb[i], writes=[tabb[i]])

        def proj_fm(bk, W, col0, ncol, xTi, xb, prow=0):
            for k in range(8):
                C.op("pe", lambda e: e.matmul(PS[prow:prow + ncol, bk, :], W[:, k, col0:col0 + ncol], xTi[:, k, :],
                                              start=(k == 0), stop=(k == 7)),
                     reads=[wb, xb], writes=[psb[bk]], signal=(k == 7))

        tcnt = [0]

        def rope_out(bk_a, bk_s, p0, p1, ctab, stab, out_ap, tb, out_buf, extra_reads=()):
            i0 = tcnt[0] % 2
            tcnt[0] += 1
            t_ = tmp[i0]
            C.op("dve", lambda e: e.tensor_tensor(out=t_[p0:p1, :], in0=ctab[p0:p1, :], in1=PS[p0:p1, bk_a, :], op=ALU.mult),
                 reads=[tb, psb[bk_a]], writes=[tmpb[i0]])
            i1 = tcnt[0] % 2
            tcnt[0] += 1
            t2 = tmp[i1]
            C.op("dve", lambda e: e.tensor_tensor(out=t2[p0:p1, :], in0=stab[p0:p1, :], in1=PS[p0:p1, bk_s, :], op=ALU.mult),
                 reads=[tb, psb[bk_s]], writes=[tmpb[i1]])
            C.op("pool", lambda e: e.tensor_tensor(out=out_ap, in0=t_[p0:p1, :], in1=t2[p0:p1, :], op=ALU.add),
                 reads=[tmpb[i0], tmpb[i1]] + list(extra_reads), writes=[out_buf])

        def rms_norm_fm(nch, col0, nfeat, xTi, xb, dst, dstb):
            for c in range(nch):
                proj_fm(c, Win, col0 + c * 128, 128, xTi, xb)
                C.op("act", lambda e: e.activation(out=sq[c % 2][:], in_=bank(c), func=AF.Square),
                     reads=[psb[c]], writes=[sqb[c % 2]])
                C.op("pe", lambda e: e.matmul(bank(3), ones_f[:, :], sq[c % 2][:], start=(c == 0), stop=(c == nch - 1)),
                     reads=[sqb[c % 2]], writes=[psb[3]], signal=True)
            C.op("act", lambda e: e.activation(out=rr[:], in_=bank(3), func=AF.Ln, bias=float(EPS), scale=float(1.0 / nfeat)),
                 reads=[psb[3]], writes=[rrb])
            C.op("act", lambda e: e.activation(out=rr[:], in_=rr[:], func=AF.Exp, scale=-0.5), reads=[rrb], writes=[rrb])
            for c in range(nch):
                C.op("dve", lambda e: e.tensor_tensor(out=dst[:, c, :], in0=rr[:], in1=bank(c), op=ALU.mult),
                     reads=[rrb, psb[c]], writes=[dstb])

        load_tile(0)
        for ti, (t0, p0) in enumerate(tiles):
            i = ti % 2
            if ti + 1 < len(tiles):
                load_tile(ti + 1)
            xTi, xb, tb = xT[i], xTb_[i], tabb[i]
            c64, s64, c32, s32 = tabs[i][:, 0, :], tabs[i][:, 1, :], tabs[i][:, 2, :], tabs[i][:, 3, :]
            rms_norm_fm(3, O_CQ, 384, xTi, xb, cqn, cqnb)
            for h in range(8):
                bk = 4 + (h % 2) * 2
                for c in range(3):
                    C.op("pe", lambda e: e.matmul(PS[0:96, bk, :], Wuq[:, c, h * 96:(h + 1) * 96], cqn[:, c, :],
                                                  start=(c == 0), stop=(c == 2)),
                         reads=[wb, cqnb], writes=[psb[bk]], signal=(c == 2))
                for c in range(3):
                    C.op("pe", lambda e: e.matmul(PS[64:96, bk + 1, :], Wuqs[:, c, h * 32:(h + 1) * 32], cqn[:, c, :],
                                                  start=(c == 0), stop=(c == 2)),
                         reads=[wb, cqnb], writes=[psb[bk + 1]], signal=(c == 2))
                C.op("act", lambda e: e.copy(out=qas[0:64, h, :], in_=PS[0:64, bk, :]), reads=[psb[bk]], writes=[qasb[h]])
                rope_out(bk, bk + 1, 64, 96, c32, s32, qas[64:96, h, :], tb, qasb[h])
                C.dma("sp", QA[h, :, t0:t0 + 512], qas[0:96, h, :], qasb[h], reads=[qasb[h]])
            rms_norm_fm(2, O_CKV, 256, xTi, xb, ckvn, ckvnb)
            proj_fm(4, Win, O_KPE, 32, xTi, xb, prow=64)
            proj_fm(5, Wsw, S_KPE, 32, xTi, xb, prow=64)
            rope_out(4, 5, 64, 96, c32, s32, kas[64:96, 0, :], tb, kasb[0])
            for h in range(8):
                bk = 6 + (h % 2)
                for c in range(2):
                    C.op("pe", lambda e: e.matmul(PS[0:64, bk, :], Wk[:, c, h * 64:(h + 1) * 64], ckvn[:, c, :],
                                                  start=(c == 0), stop=(c == 1)),
                         reads=[wb, ckvnb], writes=[psb[bk]], signal=(c == 1))
                C.op("act", lambda e: e.copy(out=kas[0:64, h, :], in_=PS[0:64, bk, :]), reads=[psb[bk]], writes=[kasb[h]])
                if h > 0:
                    C.op("pool", lambda e: e.tensor_copy(out=kas[64:96, h, :], in_=kas[64:96, 0, :]),
                         reads=[kasb[0]], writes=[kasb[h]])
                C.dma("sp", KA[h, :, t0:t0 + 512], kas[0:96, h, :], kasb[h], reads=[kasb[h]])
            for s in range(4):
                bk = 4 + (s % 2)
                for c in range(2):
                    C.op("pe", lambda e: e.matmul(bank(bk), ckvn[:, c, s * 128:(s + 1) * 128], Wv[:, c, :],
                                                  start=(c == 0), stop=(c == 1)),
                         reads=[wb, ckvnb], writes=[psb[bk]], signal=(c == 1))
                C.op("act", lambda e: e.copy(out=vas[:, s, :, 0:64], in_=bank(bk).rearrange("p (h j) -> p h j", h=8)),
                     reads=[psb[bk]], writes=[vasb])
            C.dma("sp", VA[t0:t0 + 512, :].rearrange("(s p) f -> p s f", p=128), vas[:].rearrange("p s h j -> p s (h j)"),
                  vasb, reads=[vasb])
            for gi, (oc, osw, ct, st_, dst) in enumerate(((O_QB, S_QB, c32, s32, QB), (O_KB, S_KB, c32, s32, KB),
                                                           (O_QC, S_QC, c64, s64, QC), (O_KC, S_KC, c64, s64, KC))):
                for ch in range(2):
                    ba = (gi * 2 + ch) % 2 * 2
                    proj_fm(ba, Win, oc + ch * 128, 128, xTi, xb)
                    proj_fm(ba + 1, Wsw, osw + ch * 128, 128, xTi, xb)
                    rope_out(ba, ba + 1, 0, 128, ct, st_, rs[:, gi, ch, :], tb, rsb[gi][ch])
                    C.dma("sp", dst[2 * ch:2 * ch + 2, :, t0:t0 + 512].rearrange("h p t -> (h p) t"), rs[:, gi, ch, :],
                          rsb[gi][ch], reads=[rsb[gi][ch]])
            for s in range(4):
                bk = 6 + (s % 2)
                for (col, off_) in ((O_VB, 0), (O_VC, 256)):
                    for k in range(8):
                        C.op("pe", lambda e: e.matmul(PS[:, bk, off_:off_ + 256], xTi[:, k, s * 128:(s + 1) * 128],
                                                      Win[:, k, col:col + 256], start=(k == 0), stop=(k == 7)),
                             reads=[wb, xb], writes=[psb[bk]], signal=(k == 7))
                C.op("act", lambda e: e.copy(out=vbs[:, s, :, 0:64], in_=PS[:, bk, 0:256].rearrange("p (h j) -> p h j", h=4)),
                     reads=[psb[bk]], writes=[vbsb])
                C.op("act", lambda e: e.copy(out=vcs[:, s, :, 0:64], in_=PS[:, bk, 256:512].rearrange("p (h j) -> p h j", h=4)),
                     reads=[psb[bk]], writes=[vcsb])
            C.dma("sp", VB[t0:t0 + 512, :].rearrange("(s p) f -> p s f", p=128), vbs[:].rearrange("p s h j -> p s (h j)"),
                  vbsb, reads=[vbsb])
            C.dma("sp", VC[t0:t0 + 512, :].rearrange("(s p) f -> p s f", p=128), vcs[:].rearrange("p s h j -> p s (h j)"),
                  vcsb, reads=[vcsb])
        C.barrier()

    def finalize_norm(osb, osbb, nbank):
        C.op("dve", lambda e: e.reciprocal(out=osb[64:65, :], in_=osb[64:65, :]), reads=[osbb], writes=[osbb])
        C.op("pe", lambda e: e.matmul(PS[0:64, nbank, :], sel_f[0:65, :], osb[0:65, :], start=True, stop=True),
             reads=[osbb], writes=[psb[nbank]], signal=True)

    class Deferred:
        def __init__(self):
            self.q = []

        def add(self, n, fn):
            self.q.append([n, fn])

        def tick(self):
            for it in self.q:
                it[0] -= 1
            while self.q and self.q[0][0] <= 0:
                self.q.pop(0)[1]()

        def flush(self):
            while self.q:
                self.q.pop(0)[1]()

    def mla_phase():
        for (s0, S) in seqs:
            nkt = S // 128
            vall = C.sb("vall", [128, nkt, 520], BF16)
            vb_ = Buf()
            C.dma("sp", vall[:], VA[s0:s0 + S, :].rearrange("(k p) f -> p k f", p=128), vb_, writes=[vb_])
            kT = [C.sb("kT", [96, S], BF16) for _ in range(2)]
            kTb = bufs(2)
            qT = [C.sb("qT", [96, 512], BF16) for _ in range(2)]
            qTb = bufs(2)
            pT = [C.sb("pT", [128, 1024], BF16) for _ in range(2)]
            pTb = bufs(2)
            osb = [C.sb("osb", [65, 512], F32) for _ in range(2)]
            osbb = bufs(2)
            ost = [C.sb("ost", [64, 512], BF16) for _ in range(2)]
            ostb = bufs(2)
            nq = S // 512
            work = [(h, q) for h in range(8) for q in range(nq)]
            C.dma("sp", kT[0][:], KA[0, :, s0:s0 + S], kTb[0], writes=[kTb[0]])
            C.dma("sp", qT[0][:], QA[0, :, s0:s0 + 512], qTb[0], writes=[qTb[0]])
            scale = 96.0 ** -0.5
            npair = nkt // 2
            dq = Deferred()
            OB, NB = 6, 7
            items = [(wi, kp) for wi in range(len(work)) for kp in range(npair)]

            def scores(idx):
                wi, kp = items[idx]
                h = work[wi][0]
                ki, qi = h % 2, wi % 2
                b0 = (idx % 3) * 2
                for j in range(2):
                    kt = kp * 2 + j
                    C.op("pe", lambda e: e.matmul(bank(b0 + j), kT[ki][:, kt * 128:(kt + 1) * 128], qT[qi][:, :],
                                                  start=True, stop=True),
                         reads=[kTb[ki], qTb[qi]], writes=[psb[b0 + j]], signal=(j == 1))

            def prefetch(wi):
                if wi >= len(work):
                    return
                h2, q2 = work[wi]
                if wi == 0 or work[wi - 1][0] != h2:
                    if wi > 0:
                        C.dma("sp", kT[h2 % 2][:], KA[h2, :, s0:s0 + S], kTb[h2 % 2], writes=[kTb[h2 % 2]])
                if wi > 0:
                    C.dma("sp", qT[wi % 2][:], QA[h2, :, s0 + q2 * 512:s0 + (q2 + 1) * 512], qTb[wi % 2], writes=[qTb[wi % 2]])

            prefetch(1)
            scores(0)
            if len(items) > 1:
                scores(1)
            for idx, (wi, kp) in enumerate(items):
                h, q = work[wi]
                qi = wi % 2
                if idx + 2 < len(items):
                    if items[idx + 2][1] == 0 and items[idx + 2][0] + 1 < len(work):
                        pass
                    scores(idx + 2)
                b0 = (idx % 3) * 2
                pi = idx % 2
                C.op("act", lambda e: e.activation(out=pT[pi][:], in_=PS[:, b0:b0 + 2, :].rearrange("p b n -> p (b n)"),
                                                   func=AF.Exp, scale=float(scale)),
                     reads=[psb[b0], psb[b0 + 1]], writes=[pTb[pi]])
                for j in range(2):
                    kt = kp * 2 + j
                    last = (kp == npair - 1 and j == 1)
                    C.op("pe", lambda e: e.matmul(PS[0:65, OB, :], vall[:, kt, h * 65:(h + 1) * 65],
                                                  pT[pi][:, j * 512:(j + 1) * 512], start=(kt == 0), stop=last),
                         reads=[vb_, pTb[pi]], writes=[psb[OB]], signal=(j == 1))
                dq.tick()
                if kp == 2 or (npair <= 2 and kp == npair - 1):
                    prefetch(wi + 2) if False else None
                if kp == npair - 1:
                    C.op("dve", lambda e: e.tensor_copy(out=osb[qi][:], in_=PS[0:65, OB, :]), reads=[psb[OB]], writes=[osbb[qi]])
                    C.op("dve", lambda e: e.reciprocal(out=osb[qi][64:65, :], in_=osb[qi][64:65, :]), reads=[osbb[qi]],
                         writes=[osbb[qi]])

                    def fin(qi=qi, h=h, q=q):
                        C.op("pe", lambda e: e.matmul(PS[0:64, NB, :], sel_f[0:65, :], osb[qi][0:65, :], start=True, stop=True),
                             reads=[osbb[qi]], writes=[psb[NB]], signal=True)
                        C.op("dve", lambda e: e.tensor_tensor(out=ost[qi][:], in0=osb[qi][0:64, :], in1=PS[0:64, NB, :], op=ALU.mult),
                             reads=[osbb[qi], psb[NB]], writes=[ostb[qi]])
                        C.dma("sp", OT[h // 2, (h % 2) * 64:(h % 2) * 64 + 64, s0 + q * 512:s0 + (q + 1) * 512], ost[qi][:],
                              ostb[qi], reads=[ostb[qi]])
                    dq.add(3, fin)
                    prefetch(wi + 2)
            dq.flush()
            C.barrier()

    def diff_phase(l):
        lambda_init = 0.8 - 0.6 * math.exp(-0.3 * l)
        base0 = C.sb_base
        lt = C.sb("lt", [128, 128], F32)
        lsm = C.sb("lsm", [128, 8], F32)
        gs = C.sb("gs", [64, 1], F32)
        C.sb_base = C.off
        lb = Buf()
        C.dma("sp", lt[:], dlam[l:l + 1, :].partition_broadcast(128), lb, writes=[lb])
        C.dma("sp", gs[:], dsub[l].rearrange("(p o) -> p o", o=1), lb, writes=[lb])
        for i in range(2):
            C.op("dve", lambda e: e.tensor_tensor(out=lt[:, i * 64:i * 64 + 32], in0=lt[:, i * 64:i * 64 + 32],
                                                  in1=lt[:, i * 64 + 32:i * 64 + 64], op=ALU.mult), reads=[lb], writes=[lb])
            C.op("dve", lambda e: e.reduce_sum(out=lsm[:, i:i + 1], in_=lt[:, i * 64:i * 64 + 32], axis=AX.X),
                 reads=[lb], writes=[lb])
        C.op("act", lambda e: e.activation(out=lsm[:, 2:4], in_=lsm[:, 0:2], func=AF.Exp), reads=[lb], writes=[lb])
        C.op("dve", lambda e: e.tensor_tensor(out=lsm[:, 4:5], in0=lsm[:, 3:4], in1=lsm[:, 2:3], op=ALU.subtract),
             reads=[lb], writes=[lb])
        C.op("dve", lambda e: e.tensor_scalar(out=lsm[:, 5:6], in0=lsm[:, 4:5], scalar1=float(-lambda_init), scalar2=None,
                                              op0=ALU.add), reads=[lb], writes=[lb])
        C.op("dve", lambda e: e.tensor_scalar(out=gs[:], in0=gs[:], scalar1=float(1.0 - lambda_init), scalar2=None,
                                              op0=ALU.mult), reads=[lb], writes=[lb])
        neglam = lsm[0:64, 5:6]
        scale = 32.0 ** -0.5
        for (s0, S) in seqs:
            nkt = S // 128
            vall = C.sb("vall", [128, nkt, 260], BF16)
            vb_ = Buf()
            C.dma("sp", vall[:], VB[s0:s0 + S, :].rearrange("(k p) f -> p k f", p=128), vb_, writes=[vb_])
            kT = [C.sb("kT", [64, S], BF16) for _ in range(2)]
            kTb = bufs(2)
            qT = [C.sb("qT", [64, 512], BF16) for _ in range(2)]
            qTb = bufs(2)
            pT = [C.sb("pT", [128, 1024], BF16) for _ in range(2)]
            pTb = bufs(2)
            osb = [C.sb("osb", [65, 2, 512], F32) for _ in range(2)]
            osbb = bufs(2)
            t1 = C.sb("t1", [64, 512], F32)
            t2 = C.sb("t2", [64, 512], F32)
            tb_ = Buf()
            ost = [C.sb("ost", [64, 512], BF16) for _ in range(2)]
            ostb = bufs(2)
            nq = S // 512
            work = [(h, q) for h in range(4) for q in range(nq)]
            C.dma("sp", kT[0][:], KB[0, :, s0:s0 + S], kTb[0], writes=[kTb[0]])
            C.dma("sp", qT[0][:], QB[0, :, s0:s0 + 512], qTb[0], writes=[qTb[0]])
            dq = Deferred()
            for wi, (h, q) in enumerate(work):
                ki = h % 2
                qi = wi % 2
                if wi + 1 < len(work):
                    h2, q2 = work[wi + 1]
                    if h2 != h:
                        C.dma("sp", kT[h2 % 2][:], KB[h2, :, s0:s0 + S], kTb[h2 % 2], writes=[kTb[h2 % 2]])
                    C.dma("sp", qT[1 - qi][:], QB[h2, :, s0 + q2 * 512:s0 + (q2 + 1) * 512], qTb[1 - qi], writes=[qTb[1 - qi]])
                o1, o2 = 4, 5

                def scores(kt):
                    b0 = (kt % 2) * 2
                    for c in range(2):
                        C.op("pe", lambda e: e.matmul(bank(b0 + c), kT[ki][c * 32:(c + 1) * 32, kt * 128:(kt + 1) * 128],
                                                      qT[qi][c * 32:(c + 1) * 32, :], start=True, stop=True),
                             reads=[kTb[ki], qTb[qi]], writes=[psb[b0 + c]], signal=(c == 1))

                scores(0)
                for kt in range(nkt):
                    if kt + 1 < nkt:
                        scores(kt + 1)
                    b0 = (kt % 2) * 2
                    pi = kt % 2
                    C.op("act", lambda e: e.activation(out=pT[pi][:], in_=PS[:, b0:b0 + 2, :].rearrange("p b n -> p (b n)"),
                                                       func=AF.Exp, scale=float(scale)),
                         reads=[psb[b0], psb[b0 + 1]], writes=[pTb[pi]])
                    for c in range(2):
                        C.op("pe", lambda e: e.matmul(PS[0:65, o1 + c, :], vall[:, kt, h * 65:(h + 1) * 65],
                                                      pT[pi][:, c * 512:(c + 1) * 512], start=(kt == 0), stop=(kt == nkt - 1)),
                             reads=[vb_, pTb[pi]], writes=[psb[o1 + c]], signal=(c == 1))
                    dq.tick()
                ob2 = osb[qi]
                C.op("dve", lambda e: e.tensor_copy(out=ob2[:], in_=PS[0:65, o1:o1 + 2, :]), reads=[psb[o1], psb[o2]],
                     writes=[osbb[qi]])
                C.op("dve", lambda e: e.reciprocal(out=ob2[64:65, :, :], in_=ob2[64:65, :, :]), reads=[osbb[qi]], writes=[osbb[qi]])

                def f1(ob2=ob2, qi=qi):
                    for c in range(2):
                        C.op("pe", lambda e: e.matmul(PS[0:64, 6 + c, :], sel_f[0:65, :], ob2[0:65, c, :], start=True, stop=True),
                             reads=[osbb[qi]], writes=[psb[6 + c]], signal=True)
                    C.op("dve", lambda e: e.tensor_tensor(out=t1[:], in0=ob2[0:64, 0, :], in1=PS[0:64, 6, :], op=ALU.mult),
                         reads=[osbb[qi], psb[6]], writes=[tb_])
                    C.op("dve", lambda e: e.tensor_tensor(out=t2[:], in0=ob2[0:64, 1, :], in1=PS[0:64, 7, :], op=ALU.mult),
                         reads=[osbb[qi], psb[7], tb_], writes=[tb_])
                    C.op("dve", lambda e: e.scalar_tensor_tensor(out=t1[:], in0=t2[:], scalar=neglam, in1=t1[:], op0=ALU.mult,
                                                                 op1=ALU.add), reads=[tb_, lb], writes=[tb_])
                    C.op("dve", lambda e: e.tensor_tensor(out=t2[:], in0=t1[:], in1=t1[:], op=ALU.mult), reads=[tb_], writes=[tb_])

                def f2():
                    C.op("pe", lambda e: e.matmul(PS[0:64, 6, :], ones_f[0:64, 0:64], t2[:], start=True, stop=True),
                         reads=[tb_], writes=[psb[6]], signal=True)

                def f3(qi=qi, h=h, q=q):
                    C.op("act", lambda e: e.activation(out=t2[:], in_=PS[0:64, 6, :], func=AF.Ln, bias=float(EPS), scale=float(1.0 / 64)),
                         reads=[psb[6], tb_], writes=[tb_])
                    C.op("act", lambda e: e.activation(out=t2[:], in_=t2[:], func=AF.Exp, scale=-0.5), reads=[tb_], writes=[tb_])
                    C.op("dve", lambda e: e.scalar_tensor_tensor(out=ost[qi][:], in0=t1[:], scalar=gs[:, 0:1], in1=t2[:],
                                                                 op0=ALU.mult, op1=ALU.mult), reads=[tb_, lb], writes=[ostb[qi]])
                    C.dma("sp", OT[4 + h // 2, (h % 2) * 64:(h % 2) * 64 + 64, s0 + q * 512:s0 + (q + 1) * 512], ost[qi][:],
                          ostb[qi], reads=[ostb[qi]])
                dq.add(3, f1)
                dq.add(6, f2)
                dq.add(9, f3)
            dq.flush()
            C.barrier()
        C.sb_base = base0
        C.off = base0

    def dil_phase():
        scale = 64.0 ** -0.5
        for (s0, S) in seqs:
            for h in range(4):
                qn = C.sb("qn", [64, S], BF16)
                kn = C.sb("kn", [64, S], BF16)
                qp = C.sb("qp", [64, S], BF16)
                kp_ = C.sb("kp", [64, S], BF16)
                nb = Buf()
                pb = Buf()
                acc = C.sb("acc", [65, S], F32)
                accb = Buf()
                C.dma("sp", qn[:], QC[h, :, s0:s0 + S], nb, writes=[nb])
                C.dma("sp", kn[:], KC[h, :, s0:s0 + S], nb, writes=[nb])
                pT = [C.sb("pT", [128, 256], BF16) for _ in range(2)]
                pTb = bufs(2)
                ost = [C.sb("ost", [64, 512], BF16) for _ in range(2)]
                ostb = bufs(2)
                cnt = 0
                for d in (1, 4, 16):
                    L = S // d
                    nt = L // 128
                    vp = C.sb("vp%d" % d, [128, d, nt + 1, 65], BF16)
                    vpb = Buf()
                    for r in range(d):
                        def rows(k0, n):
                            tok0 = s0 + k0 * d + r
                            if d == 1:
                                return VC[tok0:tok0 + n, h * 65:(h + 1) * 65]
                            return VC[tok0:tok0 + (n - 1) * d + 1:d, h * 65:(h + 1) * 65]
                        C.dma("sp", vp[64:128, r, 0, :], rows(0, 64), vpb, writes=[vpb])
                        if nt > 1:
                            C.dma("sp", vp[:, r, 1:nt, :], rows(64, (nt - 1) * 128).rearrange("(j p) f -> p j f", p=128),
                                  vpb, writes=[vpb])
                        C.dma("sp", vp[0:64, r, nt, :], rows(L - 64, 64), vpb, writes=[vpb])
                    if d == 1:
                        qd, kd, db = qn, kn, nb
                    else:
                        C.op("pool", lambda e: e.tensor_copy(out=qp[:].rearrange("p (r i) -> p r i", r=d),
                                                             in_=qn[:].rearrange("p (i r) -> p r i", r=d)),
                             reads=[nb], writes=[pb])
                        C.op("pool", lambda e: e.tensor_copy(out=kp_[:].rearrange("p (r i) -> p r i", r=d),
                                                             in_=kn[:].rearrange("p (i r) -> p r i", r=d)),
                             reads=[nb], writes=[pb])
                        qd, kd, db = qp, kp_, pb
                    accv = acc[:].rearrange("p (i r) -> p r i", r=d)
                    for r in range(d):
                        for jp in range(nt + 1):
                            lo = 64 if jp == 0 else 0
                            hi = 64 if jp == nt else 128
                            q_lo = max(jp - 1, 0) * 128
                            q_hi = min(jp + 1, nt) * 128
                            nqc = q_hi - q_lo
                            mc0 = 128 if jp == 0 else 0
                            kbase = r * L + 128 * jp - 64
                            sbk = cnt % 2
                            pi = cnt % 2
                            cnt += 1
                            C.op("pe", lambda e: e.matmul(PS[lo:hi, sbk, 0:nqc], kd[:, kbase + lo:kbase + hi],
                                                          qd[:, r * L + q_lo:r * L + q_hi], start=True, stop=True),
                                 reads=[db], writes=[psb[sbk]], signal=True)
                            C.op("act", lambda e: e.activation(out=pT[pi][lo:hi, 0:nqc], in_=PS[lo:hi, sbk, 0:nqc], func=AF.Exp,
                                                               scale=float(scale)), reads=[psb[sbk]], writes=[pTb[pi]])
                            C.op("pool", lambda e: e.tensor_tensor(out=pT[pi][lo:hi, 0:nqc], in0=pT[pi][lo:hi, 0:nqc],
                                                                   in1=mask[lo:hi, mc0:mc0 + nqc], op=ALU.mult),
                                 reads=[pTb[pi]], writes=[pTb[pi]])
                            col = 0
                            for qt in range(max(jp - 1, 0), min(jp + 1, nt)):
                                first = (qt == jp)
                                obk = 4 + (qt % 2)
                                C.op("pe", lambda e: e.matmul(PS[0:65, obk, 0:128], vp[lo:hi, r, jp, :],
                                                              pT[pi][lo:hi, col:col + 128], start=first, stop=(not first)),
                                     reads=[vpb, pTb[pi]], writes=[psb[obk]], signal=True)
                                if not first:
                                    dst = accv[:, r, qt * 128:(qt + 1) * 128]
                                    if d == 1:
                                        C.op("dve", lambda e: e.tensor_copy(out=dst, in_=PS[0:65, obk, 0:128]),
                                             reads=[psb[obk]], writes=[accb])
                                    else:
                                        C.op("dve", lambda e: e.tensor_tensor(out=dst, in0=dst, in1=PS[0:65, obk, 0:128], op=ALU.add),
                                             reads=[psb[obk], accb], writes=[accb])
                                col += 128
                for q in range(S // 512):
                    qi = q % 2
                    sl = slice(q * 512, (q + 1) * 512)
                    C.op("dve", lambda e: e.reciprocal(out=acc[64:65, sl], in_=acc[64:65, sl]), reads=[accb], writes=[accb])
                    C.op("pe", lambda e: e.matmul(PS[0:64, 6 + qi, :], sel_f[0:65, :], acc[0:65, sl], start=True, stop=True),
                         reads=[accb], writes=[psb[6 + qi]], signal=True)
                    C.op("dve", lambda e: e.tensor_tensor(out=ost[qi][:], in0=acc[0:64, sl], in1=PS[0:64, 6 + qi, :], op=ALU.mult),
                         reads=[accb, psb[6 + qi]], writes=[ostb[qi]])
                    C.dma("sp", OT[6 + h // 2, (h % 2) * 64:(h % 2) * 64 + 64, s0 + q * 512:s0 + (q + 1) * 512], ost[qi][:],
                          ostb[qi], reads=[ostb[qi]])
                C.barrier()

    def out_phase(l, xsrc, xdst, xTdst):
        Wo = C.sb("Wo", [128, 8, D], BF16)
        wb = Buf()
        C.dma("pool", Wo[:], w_out[l].rearrange("(k p) f -> p k f", p=128), wb, writes=[wb])
        oT = [C.sb("oT", [128, 8, 512], BF16) for _ in range(2)]
        oTb = bufs(2)
        L = ln_setup(l, 1)
        pend = [None]
        ntile = NT // 512
        C.dma("sp", oT[0][:], OT[:, :, 0:512].rearrange("c p t -> p c t"), oTb[0], writes=[oTb[0]])
        for t in range(ntile):
            t0 = t * 512
            i = t % 2
            if t + 1 < ntile:
                C.dma("sp", oT[1 - i][:], OT[:, :, t0 + 512:t0 + 1024].rearrange("c p t -> p c t"), oTb[1 - i], writes=[oTb[1 - i]])
            ln_load_x(L, xsrc, t0)
            for s in range(4):
                for hf in range(2):
                    yb = 4 + hf
                    for c in range(8):
                        C.op("pe", lambda e: e.matmul(bank(yb), oT[i][:, c, s * 128:(s + 1) * 128], Wo[:, c, hf * 512:(hf + 1) * 512],
                                                      start=(c == 0), stop=(c == 7)),
                             reads=[oTb[i], wb], writes=[psb[yb]], signal=(c == 7))
                if s < 3:
                    ln_load_x(L, xsrc, t0 + (s + 1) * 128)
                if pend[0] is not None:
                    pend[0]()
                pend[0] = ln_epilogue(L, (4, 5), ALPHA, 1.0, xdst, xTdst, t0 + s * 128, 6)
        if pend[0] is not None:
            pend[0]()
        C.barrier()

    phase0(xin, xTa)
    xcur, xTcur = xin, xTa
    xalt = [xa, xb_]
    xTalt = [xTb, xTa]
    step = 0
    for l in range(depth):
        last_layer = (l == depth - 1)
        xd, xTd = xalt[step % 2], xTalt[step % 2]
        ffn_phase(l, 0, xcur, xTcur, xd, xTd)
        xcur, xTcur = xd, xTd
        step += 1
        proj_phase(l, xTcur)
        mla_phase()
        diff_phase(l)
        dil_phase()
        xd, xTd = xalt[step % 2], xTalt[step % 2]
        out_phase(l, xcur, xd, xTd)
        xcur, xTcur = xd, xTd
        step += 1
        xd, xTd = (y, None) if last_layer else (xalt[step % 2], xTalt[step % 2])
        ffn_phase(l, 1, xcur, xTcur, xd, xTd)
        xcur, xTcur = xd, xTd
        step += 1
    return nc


def _swap_cols(w, base, width, dim):
    blk = w[..., base:base + width].reshape(w.shape[:-1] + (width // dim, dim))
    return np.concatenate([blk[..., dim // 2:], blk[..., :dim // 2]], axis=-1).reshape(w.shape[:-1] + (width,))


def _tables():
    def tab(dim):
        half = dim // 2
        inv = (1.0 / (10000.0 ** (np.arange(0, dim, 2, dtype=np.float32) / np.float32(dim)))).astype(np.float32)
        ang = np.arange(MAXPOS, dtype=np.float32)[:, None] * inv[None, :]
        c, s = np.cos(ang).astype(np.float32), np.sin(ang).astype(np.float32)
        rows = np.arange(128)
        i = rows % dim
        f = i % half
        sign = np.where(i < half, -1.0, 1.0).astype(np.float32)
        return np.ascontiguousarray(c[:, f].T), np.ascontiguousarray((s[:, f] * sign[None, :]).T)
    c64, s64 = tab(64)
    c32, s32 = tab(32)
    ident = np.eye(128, dtype=np.float32).astype(ml_dtypes.bfloat16)
    kk = np.arange(128)[:, None]
    qq = np.arange(256)[None, :]
    band = ((qq - kk >= 0) & (qq - kk <= 128)).astype(np.float32).astype(ml_dtypes.bfloat16)
    return c64, s64, c32, s32, ident, band


def make_in_maps(x_prompt, x_sample, ln_g, ln_b, ffn_w_gate, ffn_w_up, ffn_w_down, w_in, mla_q_norm, mla_kv_norm,
                 mla_w_uq, mla_w_ukv, diff_lambda, diff_subln, w_out):
    f = lambda a: np.ascontiguousarray(np.asarray(a, dtype=np.float32))
    w_in = f(w_in)
    w_sw = np.concatenate([_swap_cols(w_in, O_KPE, 32, 32), _swap_cols(w_in, O_QB, 256, 32), _swap_cols(w_in, O_KB, 256, 32),
                           _swap_cols(w_in, O_QC, 256, 64), _swap_cols(w_in, O_KC, 256, 64)], axis=-1)
    w_uq = f(mla_w_uq)
    uq4 = w_uq.reshape(w_uq.shape[0], 384, 8, 96)[..., 64:96]
    w_uqs = np.concatenate([uq4[..., 16:], uq4[..., :16]], axis=-1).reshape(w_uq.shape[0], 384, 256)
    c64, s64, c32, s32, ident, band = _tables()
    shared = dict(ln_g=f(ln_g), ln_b=f(ln_b), wg=f(ffn_w_gate), wu=f(ffn_w_up), wd=f(ffn_w_down), w_in=w_in,
                  w_sw=np.ascontiguousarray(w_sw), q_norm=f(mla_q_norm), kv_norm=f(mla_kv_norm), w_uq=w_uq,
                  w_uqs=np.ascontiguousarray(w_uqs), w_ukv=f(mla_w_ukv), dlam=f(diff_lambda).reshape(-1, 128),
                  dsub=f(diff_subln), w_out=f(w_out), cos64=c64, sin64=s64, cos32=c32, sin32=s32, ident=ident, bandmask=band)
    xp, xs = f(x_prompt), f(x_sample)
    maps = []
    for b in range(xp.shape[0]):
        m = dict(shared)
        m["xin"] = np.ascontiguousarray(np.concatenate([xp[b], xs[b]], axis=0))
        maps.append(m)
    return maps


def kernel(**inputs):
    SP = inputs["x_prompt"].shape[1]
    SS = inputs["x_sample"].shape[1]
    nb = inputs["x_prompt"].shape[0]
    nc = build(SP, SS, DEPTH)
    maps = make_in_maps(**inputs)
    res = run_bass_kernel_spmd(nc, maps, core_ids=list(range(nb)))
    ys = [np.asarray(r["y"], dtype=np.float32) for r in res.results]
    y_prompt = np.stack([yy[:SP] for yy in ys], axis=0)
    y_sample = np.stack([yy[SP:] for yy in ys], axis=0)
    return (y_prompt, y_sample)
```

```python
import math
import numpy as np
import ml_dtypes
import concourse.bass as bass
import concourse.mybir as mybir
from concourse.bass_utils import run_bass_kernel_spmd

F32 = mybir.dt.float32
BF16 = mybir.dt.bfloat16
AF = mybir.ActivationFunctionType
ALU = mybir.AluOpType
AX = mybir.AxisListType

D = 1024
DFF = 2816
NFF = 22
DEPTH = 4
ALPHA = (2 * DEPTH) ** 0.25
EPS = 1e-5
INC = 2208
SWC = 1056
O_CQ, O_CKV, O_KPE, O_QB, O_KB, O_VB, O_QC, O_KC, O_VC = 0, 384, 640, 672, 928, 1184, 1440, 1696, 1952
S_QB, S_KB, S_QC, S_KC, S_KPE = 0, 256, 512, 768, 1024
MAXPOS = 8192


class Buf:
    __slots__ = ("w", "r")

    def __init__(self):
        self.w = None
        self.r = {}


def bufs(n):
    return [Buf() for _ in range(n)]


class Ctx:
    SB_LIMIT = 229248

    def __init__(self, nc):
        self.nc = nc
        self.eng = {"pe": nc.tensor, "act": nc.scalar, "dve": nc.vector, "pool": nc.gpsimd, "sp": nc.sync}
        self.sems = {}
        self.nsig = {}
        for k in ("pe", "act", "dve", "pool"):
            self.sems[k] = nc.alloc_semaphore("sem_" + k)
            self.nsig[k] = 0
        self.waited = {k: {} for k in self.eng}
        self.dcount = {}
        self.dma_free = []
        self.slot2sem = {}
        self.sb_base = 16640
        self.off = self.sb_base
        self.uid = 0
        self.keep = []

    def sb(self, name, shape, dtype):
        isz = 2 if dtype == BF16 else 4
        n = 1
        for s in shape[1:]:
            n *= s
        size = (n * isz + 63) // 64 * 64
        off = self.off
        self.off += size
        assert self.off <= self.SB_LIMIT, ("SBUF overflow", name, self.off)
        self.uid += 1
        return self.nc.alloc_sbuf_tensor_at("%s_%d" % (name, self.uid), list(shape), dtype, offset=off)

    def persist(self):
        self.sb_base = self.off

    def _deps(self, reads, writes):
        toks = []
        for b in reads:
            if b.w is not None:
                toks.append(b.w)
        for b in writes:
            if b.w is not None:
                toks.append(b.w)
            toks.extend(b.r.values())
        return toks

    def _wait(self, e, toks):
        need = {}
        for (sn, val, src) in toks:
            if src == "pe" and e == "pe":
                continue
            if val > need.get(sn, 0):
                need[sn] = val
        w = self.waited[e]
        for sn, val in need.items():
            if w.get(sn, 0) >= val:
                continue
            self.eng[e].wait_ge(self.sems[sn], val)
            w[sn] = val

    def _commit(self, tok, reads, writes):
        for b in reads:
            o = b.r.get(tok[0])
            if o is None or o[1] < tok[1]:
                b.r[tok[0]] = tok
        for b in writes:
            b.w = tok
            b.r = {}

    def op(self, e, fn, reads=(), writes=(), signal=True):
        self._wait(e, self._deps(reads, writes))
        ins = fn(self.eng[e])
        if signal:
            self.nsig[e] += 1
            ins.then_inc(self.sems[e], 1)
            tok = (e, self.nsig[e], e)
        else:
            tok = (e, self.nsig[e] + 1, e)
        self._commit(tok, reads, writes)

    def _slot_sem(self, slot):
        if slot not in self.slot2sem:
            if self.dma_free:
                name = self.dma_free.pop()
            else:
                name = "dsem%d" % len(self.dcount)
                self.sems[name] = self.nc.alloc_semaphore(name)
                self.dcount[name] = 0
            self.slot2sem[slot] = name
        return self.slot2sem[slot]

    def dma(self, q, out, in_, slot, reads=(), writes=(), **kw):
        self._wait(q, self._deps(reads, writes))
        sn = self._slot_sem(slot)
        ins = self.eng[q].dma_start(out=out, in_=in_, **kw)
        self.dcount[sn] += 16
        ins.then_inc(self.sems[sn], 16)
        tok = (sn, self.dcount[sn], "dma")
        self._commit(tok, reads, writes)

    def barrier(self):
        for e in self.eng:
            w = self.waited[e]
            for k in ("pe", "act", "dve", "pool"):
                if w.get(k, 0) < self.nsig[k]:
                    self.eng[e].wait_ge(self.sems[k], self.nsig[k])
                    w[k] = self.nsig[k]
            for sn, c in self.dcount.items():
                if w.get(sn, 0) < c:
                    self.eng[e].wait_ge(self.sems[sn], c)
                    w[sn] = c
        self.slot2sem = {}
        self.dma_free = list(self.dcount.keys())
        self.off = self.sb_base


def build(SP, SS, depth, dbg=False, phases=None):
    NT = SP + SS
    seqs = [(0, SP), (SP, SS)]
    nc = bass.Bass("TRN2", target_bir_lowering=False)

    def din(name, shape, dtype=F32):
        return nc.dram_tensor(name, list(shape), dtype, kind="ExternalInput").ap()

    def dscr(name, shape, dtype):
        return nc.dram_tensor(name, list(shape), dtype, kind=("ExternalOutput" if dbg else "Internal")).ap()

    xin = din("xin", [NT, D])
    ln_g = din("ln_g", [DEPTH, 3, D])
    ln_b = din("ln_b", [DEPTH, 3, D])
    wg = din("wg", [DEPTH, 2, D, DFF])
    wu = din("wu", [DEPTH, 2, D, DFF])
    wd = din("wd", [DEPTH, 2, DFF, D])
    w_in = din("w_in", [DEPTH, D, INC])
    w_sw = din("w_sw", [DEPTH, D, SWC])
    q_norm = din("q_norm", [DEPTH, 384])
    kv_norm = din("kv_norm", [DEPTH, 256])
    w_uq = din("w_uq", [DEPTH, 384, 768])
    w_uqs = din("w_uqs", [DEPTH, 384, 768])
    w_ukv = din("w_ukv", [DEPTH, 256, 1024])
    dlam = din("dlam", [DEPTH, 128])
    dsub = din("dsub", [DEPTH, 64])
    w_out = din("w_out", [DEPTH, D, D])
    cos64 = din("cos64", [128, MAXPOS])
    sin64 = din("sin64", [128, MAXPOS])
    cos32 = din("cos32", [128, MAXPOS])
    sin32 = din("sin32", [128, MAXPOS])
    ident_d = din("ident", [128, 128], BF16)
    mask_d = din("bandmask", [128, 256], BF16)

    y = nc.dram_tensor("y", [NT, D], F32, kind="ExternalOutput").ap()
    xa = dscr("xa", [NT, D], F32)
    xb_ = dscr("xb", [NT, D], F32)
    xTa = dscr("xTa", [8, 128, NT], BF16)
    xTb = dscr("xTb", [8, 128, NT], BF16)
    QA = dscr("QA", [8, 96, NT], BF16)
    KA = dscr("KA", [8, 96, NT], BF16)
    VA = dscr("VA", [NT, 8 * 65], BF16)
    QB = dscr("QB", [4, 64, NT], BF16)
    KB = dscr("KB", [4, 64, NT], BF16)
    VB = dscr("VB", [NT, 4 * 65], BF16)
    QC = dscr("QC", [4, 64, NT], BF16)
    KC = dscr("KC", [4, 64, NT], BF16)
    VC = dscr("VC", [NT, 4 * 65], BF16)
    OT = dscr("OT", [8, 128, NT], BF16)

    C = Ctx(nc)
    PS = nc.alloc_psum_tensor("ps", [128, 8, 512], F32)
    psb = bufs(8)

    def bank(b):
        return PS[:, b, :]

    def bank16(b):
        return PS[:, b, :].bitcast(BF16)

    ident = C.sb("ident", [128, 128], BF16)
    mask = C.sb("mask", [128, 256], BF16)
    ones_f = C.sb("ones_f", [128, 128], F32)
    sel_f = C.sb("sel_f", [128, 128], F32)
    mhalf = C.sb("mhalf", [128, 1], F32)
    C.persist()
    cb = Buf()
    C.dma("sp", ident[:], ident_d[:, :], cb, writes=[cb])
    C.dma("sp", mask[:], mask_d[:, :], cb, writes=[cb])
    C.op("dve", lambda e: e.memset(ones_f[:], 1.0), writes=[cb])
    C.op("dve", lambda e: e.memset(sel_f[:], 0.0), writes=[cb])
    C.op("dve", lambda e: e.memset(sel_f[64:65, :], 1.0), writes=[cb])
    C.op("dve", lambda e: e.memset(mhalf[:], -0.5), writes=[cb])
    C.barrier()

    def ln_setup(l, j):
        g_t = C.sb("g_t", [128, D], F32)
        b_t = C.sb("b_t", [128, D], F32)
        gb = Buf()
        C.dma("sp", g_t[:], ln_g[l, j:j + 1, :].partition_broadcast(128), gb, writes=[gb])
        C.dma("sp", b_t[:], ln_b[l, j:j + 1, :].partition_broadcast(128), gb, writes=[gb])
        xbuf = [C.sb("xbuf", [128, D], F32) for _ in range(2)]
        ob = [C.sb("ob", [128, D], BF16) for _ in range(2)]
        xts = [C.sb("xts", [128, 8, 128], BF16) for _ in range(2)]
        st = C.sb("stats", [128, 2, 6], F32)
        mv = C.sb("mv", [128, 2], F32)
        sm = C.sb("sm", [128, 4], F32)
        return dict(g=g_t, b=b_t, gb=gb, xbuf=xbuf, xbb=bufs(2), ob=ob, obb=bufs(2), xts=xts, xtsb=bufs(2),
                    st=st, stb=Buf(), mv=mv, sm=sm, cnt=0, lcnt=0)

    def ln_load_x(L, xsrc, t0):
        i = L["lcnt"] % 2
        L["lcnt"] += 1
        C.dma("sp", L["xbuf"][i][:], xsrc[t0:t0 + 128, :], L["xbb"][i], writes=[L["xbb"][i]])

    def ln_epilogue(L, ybanks, res_scale, k, xdst, xTdst, t0, trbank):
        i = L["cnt"] % 2
        L["cnt"] += 1
        xbuf = L["xbuf"][i]
        xbb = L["xbb"][i]
        for hf in range(2):
            sl = slice(hf * 512, (hf + 1) * 512)
            C.op("dve", lambda e: e.scalar_tensor_tensor(out=xbuf[:, sl], in0=xbuf[:, sl], scalar=float(res_scale),
                                                         in1=bank(ybanks[hf]), op0=ALU.mult, op1=ALU.add),
                 reads=[xbb, psb[ybanks[hf]]], writes=[xbb])
        stb = L["stb"]
        for hf in range(2):
            sl = slice(hf * 512, (hf + 1) * 512)
            C.op("dve", lambda e: e.bn_stats(out=L["st"][:, hf, :], in_=xbuf[:, sl]), reads=[xbb], writes=[stb])
        C.op("dve", lambda e: e.bn_aggr(out=L["mv"][:], in_=L["st"][:]), reads=[stb], writes=[stb])
        sm = L["sm"]
        C.op("dve", lambda e: e.tensor_scalar(out=sm[:, 0:1], in0=L["mv"][:, 1:2], scalar1=float(1.0 / (k * k)),
                                              scalar2=float(EPS), op0=ALU.mult, op1=ALU.add), reads=[stb], writes=[stb])
        C.op("pool", lambda e: e.tensor_tensor(out=sm[:, 1:2], in0=sm[:, 0:1], in1=mhalf[:], op=ALU.pow),
             reads=[stb], writes=[stb])
        C.op("dve", lambda e: e.tensor_scalar(out=sm[:, 2:3], in0=sm[:, 1:2], scalar1=float(1.0 / k), scalar2=None,
                                              op0=ALU.mult), reads=[stb], writes=[stb])
        C.op("dve", lambda e: e.tensor_scalar(out=sm[:, 3:4], in0=L["mv"][:, 0:1], scalar1=sm[:, 2:3], scalar2=-1.0,
                                              op0=ALU.mult, op1=ALU.mult), reads=[stb], writes=[stb])
        C.op("act", lambda e: e.activation(out=xbuf[:], in_=xbuf[:], func=AF.Identity, bias=sm[:, 3:4], scale=sm[:, 2:3]),
             reads=[xbb, stb], writes=[xbb])
        C.op("dve", lambda e: e.tensor_tensor(out=xbuf[:], in0=xbuf[:], in1=L["g"][:], op=ALU.mult),
             reads=[xbb, L["gb"]], writes=[xbb])
        C.op("dve", lambda e: e.tensor_tensor(out=xbuf[:], in0=xbuf[:], in1=L["b"][:], op=ALU.add),
             reads=[xbb, L["gb"]], writes=[xbb])
        C.dma("sp", xdst[t0:t0 + 128, :], xbuf[:], xbb, reads=[xbb])
        if xTdst is not None:
            obb = L["obb"][i]
            ob = L["ob"][i]
            C.op("act", lambda e: e.copy(out=ob[:], in_=xbuf[:]), reads=[xbb], writes=[obb])
            xts_i, xtsb_i = L["xts"][i], L["xtsb"][i]
            return lambda: transpose_store(ob, obb, xts_i, xtsb_i, xTdst, t0, trbank)
        return None

    def transpose_store(ob, obb, xts, xtsb, xTdst, t0, trbank):
        tb = bank16(trbank)
        for c in range(8):
            C.op("pe", lambda e: e.transpose(out=tb[:, c * 128:(c + 1) * 128], in_=ob[:, c * 128:(c + 1) * 128],
                                             identity=ident[:]),
                 reads=[obb], writes=[psb[trbank]], signal=(c == 7))
        C.op("dve", lambda e: e.tensor_copy(out=xts[:].rearrange("p c t -> p (c t)"), in_=tb[:, :]),
             reads=[psb[trbank]], writes=[xtsb])
        C.dma("sp", xTdst[:, :, t0:t0 + 128].rearrange("c p t -> p c t"), xts[:], xtsb, reads=[xtsb])

    def phase0(xsrc, xTdst):
        xbuf = [C.sb("p0x", [128, D], F32) for _ in range(2)]
        xbb = bufs(2)
        ob = [C.sb("p0o", [128, D], BF16) for _ in range(2)]
        obb = bufs(2)
        xts = [C.sb("p0t", [128, 8, 128], BF16) for _ in range(2)]
        xtsb = bufs(2)
        n = NT // 128
        C.dma("sp", xbuf[0][:], xsrc[0:128, :], xbb[0], writes=[xbb[0]])
        for s in range(n):
            i = s % 2
            if s + 1 < n:
                C.dma("sp", xbuf[1 - i][:], xsrc[(s + 1) * 128:(s + 2) * 128, :], xbb[1 - i], writes=[xbb[1 - i]])
            C.op("act", lambda e: e.copy(out=ob[i][:], in_=xbuf[i][:]), reads=[xbb[i]], writes=[obb[i]])
            transpose_store(ob[i], obb[i], xts[i], xtsb[i], xTdst, s * 128, 6 + i)
        C.barrier()

    def ffn_phase(l, j, xsrc, xTsrc, xdst, xTdst):
        Wg = C.sb("Wg", [128, 8, DFF], BF16)
        Wu = C.sb("Wu", [128, 8, DFF], BF16)
        Wd = C.sb("Wd", [128, NFF, D], BF16)
        wb = Buf()
        for (W, src) in ((Wg, wg), (Wu, wu)):
            for hf in range(2):
                C.dma("pool", W[:, :, hf * 1408:(hf + 1) * 1408],
                      src[l, j, :, hf * 1408:(hf + 1) * 1408].rearrange("(k p) f -> p k f", p=128), wb, writes=[wb])
        for hf in range(2):
            C.dma("pool", Wd[:, hf * 11:(hf + 1) * 11, :],
                  wd[l, j, hf * 1408:(hf + 1) * 1408, :].rearrange("(c p) d -> p c d", p=128), wb, writes=[wb])
        xT = C.sb("xTin", [128, 8, 512], BF16)
        xTb_ = Buf()
        hT = C.sb("hT", [128, NFF, 512], BF16)
        hTb = bufs(NFF)
        sil = [C.sb("sil", [128, 512], F32) for _ in range(2)]
        silb = bufs(2)
        L = ln_setup(l, 0 if j == 0 else 2)
        pend = [None]
        ntile = NT // 512
        C.dma("sp", xT[:], xTsrc[:, :, 0:512].rearrange("c p t -> p c t"), xTb_, writes=[xTb_])
        for t in range(ntile):
            t0 = t * 512
            ln_load_x(L, xsrc, t0)
            for c in range(NFF):
                gb_, ub_ = c % 2, 2 + c % 2
                for k in range(8):
                    C.op("pe", lambda e: e.matmul(bank(gb_), Wg[:, k, c * 128:(c + 1) * 128], xT[:, k, :],
                                                  start=(k == 0), stop=(k == 7)),
                         reads=[wb, xTb_], writes=[psb[gb_]], signal=(k == 7))
                for k in range(8):
                    C.op("pe", lambda e: e.matmul(bank(ub_), Wu[:, k, c * 128:(c + 1) * 128], xT[:, k, :],
                                                  start=(k == 0), stop=(k == 7)),
                         reads=[wb, xTb_], writes=[psb[ub_]], signal=(k == 7))
                C.op("act", lambda e: e.activation(out=sil[c % 2][:], in_=bank(gb_), func=AF.Silu),
                     reads=[psb[gb_]], writes=[silb[c % 2]])
                C.op("dve", lambda e: e.tensor_tensor(out=hT[:, c, :], in0=sil[c % 2][:], in1=bank(ub_), op=ALU.mult),
                     reads=[silb[c % 2], psb[ub_]], writes=[hTb[c]])
                if c == 2 and pend[0] is not None:
                    pend[0]()
                    pend[0] = None
            if t + 1 < ntile:
                C.dma("sp", xT[:], xTsrc[:, :, t0 + 512:t0 + 1024].rearrange("c p t -> p c t"), xTb_, writes=[xTb_])
            for s in range(4):
                for hf in range(2):
                    yb = 4 + hf
                    for c in range(NFF):
                        C.op("pe", lambda e: e.matmul(bank(yb), hT[:, c, s * 128:(s + 1) * 128],
                                                      Wd[:, c, hf * 512:(hf + 1) * 512], start=(c == 0), stop=(c == NFF - 1)),
                             reads=[hTb[c], wb], writes=[psb[yb]], signal=(c == NFF - 1))
                if s < 3:
                    ln_load_x(L, xsrc, t0 + (s + 1) * 128)
                if pend[0] is not None:
                    pend[0]()
                pend[0] = ln_epilogue(L, (4, 5), 2.0 * ALPHA, 2.0, xdst, xTdst, t0 + s * 128, 6)
        if pend[0] is not None:
            pend[0]()
        C.barrier()

    def proj_phase(l, xTsrc):
        Win = C.sb("Win", [128, 8, INC], BF16)
        Wsw = C.sb("Wsw", [128, 8, SWC], BF16)
        Wuq = C.sb("Wuq", [128, 3, 768], BF16)
        Wuqs = C.sb("Wuqs", [128, 3, 768], BF16)
        Wk = C.sb("Wk", [128, 2, 512], BF16)
        Wv = C.sb("Wv", [128, 2, 512], BF16)
        wb = Buf()
        for hf in range(2):
            C.dma("pool", Win[:, :, hf * 1104:(hf + 1) * 1104],
                  w_in[l, :, hf * 1104:(hf + 1) * 1104].rearrange("(k p) f -> p k f", p=128), wb, writes=[wb])
        C.dma("pool", Wsw[:], w_sw[l].rearrange("(k p) f -> p k f", p=128), wb, writes=[wb])
        stg = C.sb("stg", [128, 3, 1024], F32)
        stg2 = C.sb("stg2", [128, 3, 768], F32)
        stg3 = C.sb("stg3", [128, 2, 1024], F32)
        gq = C.sb("gq", [128, 3], F32)
        gkv = C.sb("gkv", [128, 2], F32)
        sb_ = Buf()
        C.dma("sp", stg[:, :, 0:768], w_uq[l].rearrange("(c p) f -> p c f", p=128), sb_, writes=[sb_])
        C.dma("sp", stg2[:], w_uqs[l].rearrange("(c p) f -> p c f", p=128), sb_, writes=[sb_])
        C.dma("sp", stg3[:], w_ukv[l].rearrange("(c p) f -> p c f", p=128), sb_, writes=[sb_])
        C.dma("sp", gq[:], q_norm[l].rearrange("(c p) -> p c", p=128), sb_, writes=[sb_], allow_slow_non_contiguous=True)
        C.dma("sp", gkv[:], kv_norm[l].rearrange("(c p) -> p c", p=128), sb_, writes=[sb_], allow_slow_non_contiguous=True)
        for c in range(3):
            C.op("dve", lambda e: e.tensor_scalar(out=Wuq[:, c, :], in0=stg[:, c, 0:768], scalar1=gq[:, c:c + 1],
                                                  scalar2=None, op0=ALU.mult), reads=[sb_], writes=[wb])
            C.op("dve", lambda e: e.tensor_scalar(out=Wuqs[:, c, :], in0=stg2[:, c, :], scalar1=gq[:, c:c + 1],
                                                  scalar2=None, op0=ALU.mult), reads=[sb_], writes=[wb])
        for c in range(2):
            src = stg3[:, c, :].rearrange("p (h t j) -> p h t j", h=8, t=2)
            C.op("dve", lambda e: e.tensor_scalar(out=Wk[:, c, :].rearrange("p (h j) -> p h j", h=8), in0=src[:, :, 0, :],
                                                  scalar1=gkv[:, c:c + 1], scalar2=None, op0=ALU.mult),
                 reads=[sb_], writes=[wb])
            C.op("dve", lambda e: e.tensor_scalar(out=Wv[:, c, :].rearrange("p (h j) -> p h j", h=8), in0=src[:, :, 1, :],
                                                  scalar1=gkv[:, c:c + 1], scalar2=None, op0=ALU.mult),
                 reads=[sb_], writes=[wb])

        xT = [C.sb("xTin", [128, 8, 512], BF16) for _ in range(2)]
        xTb_ = bufs(2)
        tabs = [C.sb("tabs", [128, 4, 512], F32) for _ in range(2)]
        tabb = bufs(2)
        cqn = C.sb("cqn", [128, 3, 512], BF16)
        cqnb = Buf()
        ckvn = C.sb("ckvn", [128, 2, 512], BF16)
        ckvnb = Buf()
        sq = [C.sb("sq", [128, 512], F32) for _ in range(2)]
        sqb = bufs(2)
        rr = C.sb("rr", [128, 512], F32)
        rrb = Buf()
        qas = C.sb("qas", [128, 8, 512], BF16)
        qasb = bufs(8)
        kas = C.sb("kas", [128, 8, 512], BF16)
        kasb = bufs(8)
        vas = C.sb("vas", [128, 4, 8, 65], BF16)
        vasb = Buf()
        rs = C.sb("rs", [128, 4, 2, 512], BF16)
        rsb = [bufs(2) for _ in range(4)]
        vbs = C.sb("vbs", [128, 4, 4, 65], BF16)
        vbsb = Buf()
        vcs = C.sb("vcs", [128, 4, 4, 65], BF16)
        vcsb = Buf()
        tmp = [C.sb("tmp", [128, 512], F32) for _ in range(2)]
        tmpb = bufs(2)
        C.op("pool", lambda e: e.memset(vas[:], 1.0), writes=[vasb])
        C.op("pool", lambda e: e.memset(vbs[:], 1.0), writes=[vbsb])
        C.op("pool", lambda e: e.memset(vcs[:], 1.0), writes=[vcsb])

        tiles = []
        for (s0, S) in seqs:
            for q in range(S // 512):
                tiles.append((s0 + q * 512, q * 512))

        def load_tile(ti):
            t0, p0 = tiles[ti]
            i = ti % 2
            C.dma("sp", xT[i][:], xTsrc[:, :, t0:t0 + 512].rearrange("c p t -> p c t"), xTb_[i], writes=[xTb_[i]])
            for n_, tab in enumerate((cos64, sin64, cos32, sin32)):
                C.dma("sp", tabs[i][:, n_, :], tab[:, p0:p0 + 512], tabb[i], writes=[tabb[i]])

        def proj_fm(bk, W, col0, ncol, xTi, xb, prow=0):
            for k in range(8):
                C.op("pe", lambda e: e.matmul(PS[prow:prow + ncol, bk, :], W[:, k, col0:col0 + ncol], xTi[:, k, :],
                                              start=(k == 0), stop=(k == 7)),
                     reads=[wb, xb], writes=[psb[bk]], signal=(k == 7))

        tcnt = [0]

        def rope_out(bk_a, bk_s, p0, p1, ctab, stab, out_ap, tb, out_buf, extra_reads=()):
            i0 = tcnt[0] % 2
            tcnt[0] += 1
            t_ = tmp[i0]
            C.op("dve", lambda e: e.tensor_tensor(out=t_[p0:p1, :], in0=ctab[p0:p1, :], in1=PS[p0:p1, bk_a, :], op=ALU.mult),
                 reads=[tb, psb[bk_a]], writes=[tmpb[i0]])
            i1 = tcnt[0] % 2
            tcnt[0] += 1
            t2 = tmp[i1]
            C.op("dve", lambda e: e.tensor_tensor(out=t2[p0:p1, :], in0=stab[p0:p1, :], in1=PS[p0:p1, bk_s, :], op=ALU.mult),
                 reads=[tb, psb[bk_s]], writes=[tmpb[i1]])
            C.op("dve", lambda e: e.tensor_tensor(out=out_ap, in0=t_[p0:p1, :], in1=t2[p0:p1, :], op=ALU.add),
                 reads=[tmpb[i0], tmpb[i1]] + list(extra_reads), writes=[out_buf])

        def rms_norm_fm(nch, col0, nfeat, xTi, xb, dst, dstb):
            for c in range(nch):
                proj_fm(c, Win, col0 + c * 128, 128, xTi, xb)
                C.op("act", lambda e: e.activation(out=sq[c % 2][:], in_=bank(c), func=AF.Square),
                     reads=[psb[c]], writes=[sqb[c % 2]])
                C.op("pe", lambda e: e.matmul(bank(3), ones_f[:, :], sq[c % 2][:], start=(c == 0), stop=(c == nch - 1)),
                     reads=[sqb[c % 2]], writes=[psb[3]], signal=True)
            C.op("act", lambda e: e.activation(out=rr[:], in_=bank(3), func=AF.Ln, bias=float(EPS), scale=float(1.0 / nfeat)),
                 reads=[psb[3]], writes=[rrb])
            C.op("act", lambda e: e.activation(out=rr[:], in_=rr[:], func=AF.Exp, scale=-0.5), reads=[rrb], writes=[rrb])
            for c in range(nch):
                C.op("dve", lambda e: e.tensor_tensor(out=dst[:, c, :], in0=rr[:], in1=bank(c), op=ALU.mult),
                     reads=[rrb, psb[c]], writes=[dstb])

        load_tile(0)
        for ti, (t0, p0) in enumerate(tiles):
            i = ti % 2
            if ti + 1 < len(tiles):
                load_tile(ti + 1)
            xTi, xb, tb = xT[i], xTb_[i], tabb[i]
            c64, s64, c32, s32 = tabs[i][:, 0, :], tabs[i][:, 1, :], tabs[i][:, 2, :], tabs[i][:, 3, :]
            rms_norm_fm(3, O_CQ, 384, xTi, xb, cqn, cqnb)
            for h in range(8):
                bk = 4 + (h % 2) * 2
                for c in range(3):
                    C.op("pe", lambda e: e.matmul(PS[0:96, bk, :], Wuq[:, c, h * 96:(h + 1) * 96], cqn[:, c, :],
                                                  start=(c == 0), stop=(c == 2)),
                         reads=[wb, cqnb], writes=[psb[bk]], signal=(c == 2))
                for c in range(3):
                    C.op("pe", lambda e: e.matmul(PS[0:96, bk + 1, :], Wuqs[:, c, h * 96:(h + 1) * 96], cqn[:, c, :],
                                                  start=(c == 0), stop=(c == 2)),
                         reads=[wb, cqnb], writes=[psb[bk + 1]], signal=(c == 2))
                C.op("act", lambda e: e.copy(out=qas[0:64, h, :], in_=PS[0:64, bk, :]), reads=[psb[bk]], writes=[qasb[h]])
                rope_out(bk, bk + 1, 64, 96, c32, s32, qas[64:96, h, :], tb, qasb[h])
                C.dma("sp", QA[h, :, t0:t0 + 512], qas[0:96, h, :], qasb[h], reads=[qasb[h]])
            rms_norm_fm(2, O_CKV, 256, xTi, xb, ckvn, ckvnb)
            proj_fm(4, Win, O_KPE - 64, 96, xTi, xb)
            proj_fm(5, Wsw, S_KPE - 64, 96, xTi, xb)
            rope_out(4, 5, 64, 96, c32, s32, kas[64:96, 0, :], tb, kasb[0])
            for h in range(8):
                bk = 6 + (h % 2)
                for c in range(2):
                    C.op("pe", lambda e: e.matmul(PS[0:64, bk, :], Wk[:, c, h * 64:(h + 1) * 64], ckvn[:, c, :],
                                                  start=(c == 0), stop=(c == 1)),
                         reads=[wb, ckvnb], writes=[psb[bk]], signal=(c == 1))
                C.op("act", lambda e: e.copy(out=kas[0:64, h, :], in_=PS[0:64, bk, :]), reads=[psb[bk]], writes=[kasb[h]])
                C.dma("sp", KA[h, 0:64, t0:t0 + 512], kas[0:64, h, :], kasb[h], reads=[kasb[h]])
                C.dma("sp", KA[h, 64:96, t0:t0 + 512], kas[64:96, 0, :], kasb[0], reads=[kasb[0]])
            for s in range(4):
                bk = 4 + (s % 2)
                for c in range(2):
                    C.op("pe", lambda e: e.matmul(bank(bk), ckvn[:, c, s * 128:(s + 1) * 128], Wv[:, c, :],
                                                  start=(c == 0), stop=(c == 1)),
                         reads=[wb, ckvnb], writes=[psb[bk]], signal=(c == 1))
                C.op("act", lambda e: e.copy(out=vas[:, s, :, 0:64], in_=bank(bk).rearrange("p (h j) -> p h j", h=8)),
                     reads=[psb[bk]], writes=[vasb])
            C.dma("sp", VA[t0:t0 + 512, :].rearrange("(s p) f -> p s f", p=128), vas[:].rearrange("p s h j -> p s (h j)"),
                  vasb, reads=[vasb])
            for gi, (oc, osw, ct, st_, dst) in enumerate(((O_QB, S_QB, c32, s32, QB), (O_KB, S_KB, c32, s32, KB),
                                                           (O_QC, S_QC, c64, s64, QC), (O_KC, S_KC, c64, s64, KC))):
                for ch in range(2):
                    ba = (gi * 2 + ch) % 2 * 2
                    proj_fm(ba, Win, oc + ch * 128, 128, xTi, xb)
                    proj_fm(ba + 1, Wsw, osw + ch * 128, 128, xTi, xb)
                    rope_out(ba, ba + 1, 0, 128, ct, st_, rs[:, gi, ch, :], tb, rsb[gi][ch])
                    C.dma("sp", dst[2 * ch:2 * ch + 2, :, t0:t0 + 512].rearrange("h p t -> (h p) t"), rs[:, gi, ch, :],
                          rsb[gi][ch], reads=[rsb[gi][ch]])
            for s in range(4):
                bk = 6 + (s % 2)
                for (col, off_) in ((O_VB, 0), (O_VC, 256)):
                    for k in range(8):
                        C.op("pe", lambda e: e.matmul(PS[:, bk, off_:off_ + 256], xTi[:, k, s * 128:(s + 1) * 128],
                                                      Win[:, k, col:col + 256], start=(k == 0), stop=(k == 7)),
                             reads=[wb, xb], writes=[psb[bk]], signal=(k == 7))
                C.op("act", lambda e: e.copy(out=vbs[:, s, :, 0:64], in_=PS[:, bk, 0:256].rearrange("p (h j) -> p h j", h=4)),
                     reads=[psb[bk]], writes=[vbsb])
                C.op("act", lambda e: e.copy(out=vcs[:, s, :, 0:64], in_=PS[:, bk, 256:512].rearrange("p (h j) -> p h j", h=4)),
                     reads=[psb[bk]], writes=[vcsb])
            C.dma("sp", VB[t0:t0 + 512, :].rearrange("(s p) f -> p s f", p=128), vbs[:].rearrange("p s h j -> p s (h j)"),
                  vbsb, reads=[vbsb])
            C.dma("sp", VC[t0:t0 + 512, :].rearrange("(s p) f -> p s f", p=128), vcs[:].rearrange("p s h j -> p s (h j)"),
                  vcsb, reads=[vcsb])
        C.barrier()

    def finalize_norm(osb, osbb, nbank):
        C.op("dve", lambda e: e.reciprocal(out=osb[64:65, :], in_=osb[64:65, :]), reads=[osbb], writes=[osbb])
        C.op("pe", lambda e: e.matmul(PS[0:64, nbank, :], sel_f[0:65, :], osb[0:65, :], start=True, stop=True),
             reads=[osbb], writes=[psb[nbank]], signal=True)

    class Deferred:
        def __init__(self):
            self.q = []

        def add(self, n, fn):
            self.q.append([n, fn])

        def tick(self):
            for it in self.q:
                it[0] -= 1
            while self.q and self.q[0][0] <= 0:
                self.q.pop(0)[1]()

        def flush(self):
            while self.q:
                self.q.pop(0)[1]()

    def mla_phase():
        for (s0, S) in seqs:
            nkt = S // 128
            vall = C.sb("vall", [128, nkt, 520], BF16)
            vb_ = Buf()
            C.dma("sp", vall[:], VA[s0:s0 + S, :].rearrange("(k p) f -> p k f", p=128), vb_, writes=[vb_])
            kT = [C.sb("kT", [96, S], BF16) for _ in range(2)]
            kTb = bufs(2)
            qT = [C.sb("qT", [96, 512], BF16) for _ in range(2)]
            qTb = bufs(2)
            pT = [C.sb("pT", [128, 1024], BF16) for _ in range(3)]
            pTb = bufs(3)
            osb = [C.sb("osb", [65, 512], F32) for _ in range(2)]
            osbb = bufs(2)
            ost = [C.sb("ost", [64, 512], BF16) for _ in range(2)]
            ostb = bufs(2)
            nq = S // 512
            work = [(h, q) for h in range(8) for q in range(nq)]
            C.dma("sp", kT[0][:], KA[0, :, s0:s0 + S], kTb[0], writes=[kTb[0]])
            C.dma("sp", qT[0][:], QA[0, :, s0:s0 + 512], qTb[0], writes=[qTb[0]])
            scale = 96.0 ** -0.5
            npair = nkt // 2
            dq = Deferred()
            OB, NB = 6, 7
            items = [(wi, kp) for wi in range(len(work)) for kp in range(npair)]

            def scores(idx):
                wi, kp = items[idx]
                h = work[wi][0]
                ki, qi = h % 2, wi % 2
                b0 = (idx % 3) * 2
                for j in range(2):
                    kt = kp * 2 + j
                    C.op("pe", lambda e: e.matmul(bank(b0 + j), kT[ki][:, kt * 128:(kt + 1) * 128], qT[qi][:, :],
                                                  start=True, stop=True),
                         reads=[kTb[ki], qTb[qi]], writes=[psb[b0 + j]], signal=(j == 1))

            def prefetch(wi):
                if wi >= len(work):
                    return
                h2, q2 = work[wi]
                if wi == 0 or work[wi - 1][0] != h2:
                    if wi > 0:
                        C.dma("sp", kT[h2 % 2][:], KA[h2, :, s0:s0 + S], kTb[h2 % 2], writes=[kTb[h2 % 2]])
                if wi > 0:
                    C.dma("sp", qT[wi % 2][:], QA[h2, :, s0 + q2 * 512:s0 + (q2 + 1) * 512], qTb[wi % 2], writes=[qTb[wi % 2]])

            prefetch(1)
            scores(0)
            scores(1)
            scores(2)
            for idx, (wi, kp) in enumerate(items):
                h, q = work[wi]
                qi = wi % 2
                b0 = (idx % 3) * 2
                pi = idx % 3
                C.op("act", lambda e: e.activation(out=pT[pi][:], in_=PS[:, b0:b0 + 2, :].rearrange("p b n -> p (b n)"),
                                                   func=AF.Exp, scale=float(scale)),
                     reads=[psb[b0], psb[b0 + 1]], writes=[pTb[pi]])
                if idx + 3 < len(items):
                    scores(idx + 3)
                for j in range(2):
                    kt = kp * 2 + j
                    last = (kp == npair - 1 and j == 1)
                    C.op("pe", lambda e: e.matmul(PS[0:65, OB, :], vall[:, kt, h * 65:(h + 1) * 65],
                                                  pT[pi][:, j * 512:(j + 1) * 512], start=(kt == 0), stop=last),
                         reads=[vb_, pTb[pi]], writes=[psb[OB]], signal=(j == 1))
                dq.tick()
                if kp == 2 or (npair <= 2 and kp == npair - 1):
                    prefetch(wi + 2) if False else None
                if kp == npair - 1:
                    C.op("dve", lambda e: e.tensor_copy(out=osb[qi][:], in_=PS[0:65, OB, :]), reads=[psb[OB]], writes=[osbb[qi]])
                    C.op("dve", lambda e: e.reciprocal(out=osb[qi][64:65, :], in_=osb[qi][64:65, :]), reads=[osbb[qi]],
                         writes=[osbb[qi]])

                    def fin(qi=qi, h=h, q=q):
                        C.op("pe", lambda e: e.matmul(PS[:, NB, :], sel_f[0:65, :], osb[qi][0:65, :], start=True, stop=True),
                             reads=[osbb[qi]], writes=[psb[NB]], signal=True)
                        C.op("dve", lambda e: e.tensor_tensor(out=ost[qi][:], in0=osb[qi][0:64, :], in1=PS[0:64, NB, :], op=ALU.mult),
                             reads=[osbb[qi], psb[NB]], writes=[ostb[qi]])
                        C.dma("sp", OT[h // 2, (h % 2) * 64:(h % 2) * 64 + 64, s0 + q * 512:s0 + (q + 1) * 512], ost[qi][:],
                              ostb[qi], reads=[ostb[qi]])
                    dq.add(3, fin)
                    prefetch(wi + 2)
            dq.flush()
            C.barrier()

    def diff_phase(l):
        lambda_init = 0.8 - 0.6 * math.exp(-0.3 * l)
        base0 = C.sb_base
        lt = C.sb("lt", [128, 128], F32)
        lsm = C.sb("lsm", [128, 8], F32)
        gs = C.sb("gs", [64, 1], F32)
        C.sb_base = C.off
        lb = Buf()
        C.dma("sp", lt[:], dlam[l:l + 1, :].partition_broadcast(128), lb, writes=[lb])
        C.dma("sp", gs[:], dsub[l].rearrange("(p o) -> p o", o=1), lb, writes=[lb])
        for i in range(2):
            C.op("dve", lambda e: e.tensor_tensor(out=lt[:, i * 64:i * 64 + 32], in0=lt[:, i * 64:i * 64 + 32],
                                                  in1=lt[:, i * 64 + 32:i * 64 + 64], op=ALU.mult), reads=[lb], writes=[lb])
            C.op("dve", lambda e: e.reduce_sum(out=lsm[:, i:i + 1], in_=lt[:, i * 64:i * 64 + 32], axis=AX.X),
                 reads=[lb], writes=[lb])
        C.op("act", lambda e: e.activation(out=lsm[:, 2:4], in_=lsm[:, 0:2], func=AF.Exp), reads=[lb], writes=[lb])
        C.op("dve", lambda e: e.tensor_tensor(out=lsm[:, 4:5], in0=lsm[:, 3:4], in1=lsm[:, 2:3], op=ALU.subtract),
             reads=[lb], writes=[lb])
        C.op("dve", lambda e: e.tensor_scalar(out=lsm[:, 5:6], in0=lsm[:, 4:5], scalar1=float(-lambda_init), scalar2=None,
                                              op0=ALU.add), reads=[lb], writes=[lb])
        C.op("dve", lambda e: e.tensor_scalar(out=gs[:], in0=gs[:], scalar1=float(1.0 - lambda_init), scalar2=None,
                                              op0=ALU.mult), reads=[lb], writes=[lb])
        neglam = lsm[0:64, 5:6]
        scale = 32.0 ** -0.5
        for (s0, S) in seqs:
            nkt = S // 128
            vall = C.sb("vall", [128, nkt, 260], BF16)
            vb_ = Buf()
            C.dma("sp", vall[:], VB[s0:s0 + S, :].rearrange("(k p) f -> p k f", p=128), vb_, writes=[vb_])
            kT = [[C.sb("kT", [128, S], BF16) for _c in range(2)] for _ in range(2)]
            kTb = bufs(2)
            qT = [C.sb("qT", [128, 512], BF16) for _ in range(2)]
            qTb = bufs(2)
            for b_ in range(2):
                for c_ in range(2):
                    C.op("pool", lambda e: e.memset(kT[b_][c_][:], 0.0), writes=[kTb[b_]])
                C.op("pool", lambda e: e.memset(qT[b_][:], 0.0), writes=[qTb[b_]])

            def load_k(h_, b_):
                for c_ in range(2):
                    C.dma("sp", kT[b_][c_][c_ * 32:(c_ + 1) * 32, :], KB[h_, c_ * 32:(c_ + 1) * 32, s0:s0 + S], kTb[b_],
                          writes=[kTb[b_]])
            pT = [C.sb("pT", [128, 1024], BF16) for _ in range(3)]
            pTb = bufs(3)
            osb = [C.sb("osb", [65, 2, 512], F32) for _ in range(2)]
            osbb = bufs(2)
            t1 = C.sb("t1", [64, 512], F32)
            t2 = C.sb("t2", [64, 512], F32)
            tb_ = Buf()
            ost = [C.sb("ost", [64, 512], BF16) for _ in range(2)]
            ostb = bufs(2)
            nq = S // 512
            work = [(h, q) for h in range(4) for q in range(nq)]
            load_k(0, 0)
            C.dma("sp", qT[0][0:64, :], QB[0, :, s0:s0 + 512], qTb[0], writes=[qTb[0]])
            dq = Deferred()
            for wi, (h, q) in enumerate(work):
                ki = h % 2
                qi = wi % 2
                if wi + 1 < len(work):
                    h2, q2 = work[wi + 1]
                    if h2 != h:
                        load_k(h2, h2 % 2)
                    C.dma("sp", qT[1 - qi][0:64, :], QB[h2, :, s0 + q2 * 512:s0 + (q2 + 1) * 512], qTb[1 - qi], writes=[qTb[1 - qi]])
                o1, o2 = 4, 5

                def scores(kt):
                    b0 = (kt % 2) * 2
                    for c in range(2):
                        C.op("pe", lambda e: e.matmul(bank(b0 + c), kT[ki][c][:, kt * 128:(kt + 1) * 128],
                                                      qT[qi][:, :], start=True, stop=True),
                             reads=[kTb[ki], qTb[qi]], writes=[psb[b0 + c]], signal=(c == 1))

                scores(0)
                scores(1)
                for kt in range(nkt):
                    b0 = (kt % 2) * 2
                    pi = kt % 3
                    C.op("act", lambda e: e.activation(out=pT[pi][:], in_=PS[:, b0:b0 + 2, :].rearrange("p b n -> p (b n)"),
                                                       func=AF.Exp, scale=float(scale)),
                         reads=[psb[b0], psb[b0 + 1]], writes=[pTb[pi]])
                    if kt + 2 < nkt:
                        scores(kt + 2)
                    for c in range(2):
                        C.op("pe", lambda e: e.matmul(PS[0:65, o1 + c, :], vall[:, kt, h * 65:(h + 1) * 65],
                                                      pT[pi][:, c * 512:(c + 1) * 512], start=(kt == 0), stop=(kt == nkt - 1)),
                             reads=[vb_, pTb[pi]], writes=[psb[o1 + c]], signal=(c == 1))
                    dq.tick()
                ob2 = osb[qi]
                C.op("dve", lambda e: e.tensor_copy(out=ob2[:], in_=PS[0:65, o1:o1 + 2, :]), reads=[psb[o1], psb[o2]],
                     writes=[osbb[qi]])
                C.op("dve", lambda e: e.reciprocal(out=ob2[64:65, :, :], in_=ob2[64:65, :, :]), reads=[osbb[qi]], writes=[osbb[qi]])

                def f1(ob2=ob2, qi=qi):
                    for c in range(2):
                        C.op("pe", lambda e: e.matmul(PS[:, 6 + c, :], sel_f[0:65, :], ob2[0:65, c, :], start=True, stop=True),
                             reads=[osbb[qi]], writes=[psb[6 + c]], signal=True)
                    C.op("dve", lambda e: e.tensor_tensor(out=t1[:], in0=ob2[0:64, 0, :], in1=PS[0:64, 6, :], op=ALU.mult),
                         reads=[osbb[qi], psb[6]], writes=[tb_])
                    C.op("dve", lambda e: e.tensor_tensor(out=t2[:], in0=ob2[0:64, 1, :], in1=PS[0:64, 7, :], op=ALU.mult),
                         reads=[osbb[qi], psb[7], tb_], writes=[tb_])
                    C.op("dve", lambda e: e.scalar_tensor_tensor(out=t1[:], in0=t2[:], scalar=neglam, in1=t1[:], op0=ALU.mult,
                                                                 op1=ALU.add), reads=[tb_, lb], writes=[tb_])
                    C.op("dve", lambda e: e.tensor_tensor(out=t2[:], in0=t1[:], in1=t1[:], op=ALU.mult), reads=[tb_], writes=[tb_])

                def f2():
                    C.op("pe", lambda e: e.matmul(PS[0:64, 6, :], ones_f[0:64, 0:64], t2[:], start=True, stop=True),
                         reads=[tb_], writes=[psb[6]], signal=True)

                def f3(qi=qi, h=h, q=q):
                    C.op("act", lambda e: e.activation(out=t2[:], in_=PS[0:64, 6, :], func=AF.Ln, bias=float(EPS), scale=float(1.0 / 64)),
                         reads=[psb[6], tb_], writes=[tb_])
                    C.op("act", lambda e: e.activation(out=t2[:], in_=t2[:], func=AF.Exp, scale=-0.5), reads=[tb_], writes=[tb_])
                    C.op("dve", lambda e: e.scalar_tensor_tensor(out=ost[qi][:], in0=t1[:], scalar=gs[:, 0:1], in1=t2[:],
                                                                 op0=ALU.mult, op1=ALU.mult), reads=[tb_, lb], writes=[ostb[qi]])
                    C.dma("sp", OT[4 + h // 2, (h % 2) * 64:(h % 2) * 64 + 64, s0 + q * 512:s0 + (q + 1) * 512], ost[qi][:],
                          ostb[qi], reads=[ostb[qi]])
                dq.add(3, f1)
                dq.add(6, f2)
                dq.add(9, f3)
            dq.flush()
            C.barrier()
        C.sb_base = base0
        C.off = base0

    def dil_phase():
        scale = 64.0 ** -0.5
        for (s0, S) in seqs:
            for h in range(4):
                qn = C.sb("qn", [128, S], BF16)
                kn = C.sb("kn", [128, S], BF16)
                qp = C.sb("qp", [128, S], BF16)
                kp_ = C.sb("kp", [128, S], BF16)
                nb = Buf()
                pb = Buf()
                for t_ in (qn, kn):
                    C.op("pool", lambda e: e.memset(t_[64:128, :], 0.0), writes=[nb])
                for t_ in (qp, kp_):
                    C.op("pool", lambda e: e.memset(t_[64:128, :], 0.0), writes=[pb])
                acc = C.sb("acc", [65, S], F32)
                accb = Buf()
                C.dma("sp", qn[0:64, :], QC[h, :, s0:s0 + S], nb, writes=[nb])
                C.dma("sp", kn[0:64, :], KC[h, :, s0:s0 + S], nb, writes=[nb])
                pT = [C.sb("pT", [128, 256], BF16) for _ in range(2)]
                pTb = bufs(2)
                ost = [C.sb("ost", [64, 512], BF16) for _ in range(2)]
                ostb = bufs(2)
                cnt = 0
                for d in (1, 4, 16):
                    L = S // d
                    nt = L // 128
                    vp = C.sb("vp%d" % d, [128, d, nt + 1, 65], BF16)
                    vpb = Buf()
                    for r in range(d):
                        def rows(k0, n):
                            tok0 = s0 + k0 * d + r
                            if d == 1:
                                return VC[tok0:tok0 + n, h * 65:(h + 1) * 65]
                            return VC[tok0:tok0 + (n - 1) * d + 1:d, h * 65:(h + 1) * 65]
                        C.dma("sp", vp[64:128, r, 0, :], rows(0, 64), vpb, writes=[vpb])
                        if nt > 1:
                            C.dma("sp", vp[:, r, 1:nt, :], rows(64, (nt - 1) * 128).rearrange("(j p) f -> p j f", p=128),
                                  vpb, writes=[vpb])
                        C.dma("sp", vp[0:64, r, nt, :], rows(L - 64, 64), vpb, writes=[vpb])
                    if d == 1:
                        qd, kd, db = qn, kn, nb
                    else:
                        C.op("act", lambda e: e.copy(out=qp[0:64, :].rearrange("p (r i) -> p r i", r=d),
                                                             in_=qn[0:64, :].rearrange("p (i r) -> p r i", r=d)),
                             reads=[nb], writes=[pb])
                        C.op("dve", lambda e: e.tensor_copy(out=kp_[0:64, :].rearrange("p (r i) -> p r i", r=d),
                                                             in_=kn[0:64, :].rearrange("p (i r) -> p r i", r=d)),
                             reads=[nb], writes=[pb])
                        qd, kd, db = qp, kp_, pb
                    accv = acc[:].rearrange("p (i r) -> p r i", r=d)
                    for r in range(d):
                        for jp in range(nt + 1):
                            lo = 64 if jp == 0 else 0
                            hi = 64 if jp == nt else 128
                            q_lo = max(jp - 1, 0) * 128
                            q_hi = min(jp + 1, nt) * 128
                            nqc = q_hi - q_lo
                            mc0 = 128 if jp == 0 else 0
                            kbase = r * L + 128 * jp - 64
                            sbk = cnt % 2
                            pi = cnt % 2
                            cnt += 1
                            C.op("pe", lambda e: e.matmul(PS[lo:hi, sbk, 0:nqc], kd[:, kbase + lo:kbase + hi],
                                                          qd[:, r * L + q_lo:r * L + q_hi], start=True, stop=True),
                                 reads=[db], writes=[psb[sbk]], signal=True)
                            C.op("act", lambda e: e.activation(out=pT[pi][lo:hi, 0:nqc], in_=PS[lo:hi, sbk, 0:nqc], func=AF.Exp,
                                                               scale=float(scale)), reads=[psb[sbk]], writes=[pTb[pi]])
                            C.op("dve", lambda e: e.tensor_tensor(out=pT[pi][lo:hi, 0:nqc], in0=pT[pi][lo:hi, 0:nqc],
                                                                   in1=mask[lo:hi, mc0:mc0 + nqc], op=ALU.mult),
                                 reads=[pTb[pi]], writes=[pTb[pi]])
                            col = 0
                            for qt in range(max(jp - 1, 0), min(jp + 1, nt)):
                                first = (qt == jp)
                                obk = 4 + (qt % 2)
                                C.op("pe", lambda e: e.matmul(PS[0:65, obk, 0:128], vp[lo:hi, r, jp, :],
                                                              pT[pi][lo:hi, col:col + 128], start=first, stop=(not first)),
                                     reads=[vpb, pTb[pi]], writes=[psb[obk]], signal=True)
                                if not first:
                                    dst = accv[:, r, qt * 128:(qt + 1) * 128]
                                    if d == 1:
                                        C.op("dve", lambda e: e.tensor_copy(out=dst, in_=PS[0:65, obk, 0:128]),
                                             reads=[psb[obk]], writes=[accb])
                                    else:
                                        C.op("dve", lambda e: e.tensor_tensor(out=dst, in0=dst, in1=PS[0:65, obk, 0:128], op=ALU.add),
                                             reads=[psb[obk], accb], writes=[accb])
                                col += 128
                for q in range(S // 512):
                    qi = q % 2
                    sl = slice(q * 512, (q + 1) * 512)
                    C.op("dve", lambda e: e.reciprocal(out=acc[64:65, sl], in_=acc[64:65, sl]), reads=[accb], writes=[accb])
                    C.op("pe", lambda e: e.matmul(PS[:, 6 + qi, :], sel_f[0:65, :], acc[0:65, sl], start=True, stop=True),
                         reads=[accb], writes=[psb[6 + qi]], signal=True)
                    C.op("dve", lambda e: e.tensor_tensor(out=ost[qi][:], in0=acc[0:64, sl], in1=PS[0:64, 6 + qi, :], op=ALU.mult),
                         reads=[accb, psb[6 + qi]], writes=[ostb[qi]])
                    C.dma("sp", OT[6 + h // 2, (h % 2) * 64:(h % 2) * 64 + 64, s0 + q * 512:s0 + (q + 1) * 512], ost[qi][:],
                          ostb[qi], reads=[ostb[qi]])
                C.barrier()

    def out_phase(l, xsrc, xdst, xTdst):
        Wo = C.sb("Wo", [128, 8, D], BF16)
        wb = Buf()
        C.dma("pool", Wo[:], w_out[l].rearrange("(k p) f -> p k f", p=128), wb, writes=[wb])
        oT = [C.sb("oT", [128, 8, 512], BF16) for _ in range(2)]
        oTb = bufs(2)
        L = ln_setup(l, 1)
        pend = [None]
        ntile = NT // 512
        C.dma("sp", oT[0][:], OT[:, :, 0:512].rearrange("c p t -> p c t"), oTb[0], writes=[oTb[0]])
        for t in range(ntile):
            t0 = t * 512
            i = t % 2
            if t + 1 < ntile:
                C.dma("sp", oT[1 - i][:], OT[:, :, t0 + 512:t0 + 1024].rearrange("c p t -> p c t"), oTb[1 - i], writes=[oTb[1 - i]])
            ln_load_x(L, xsrc, t0)
            for s in range(4):
                for hf in range(2):
                    yb = 4 + hf
                    for c in range(8):
                        C.op("pe", lambda e: e.matmul(bank(yb), oT[i][:, c, s * 128:(s + 1) * 128], Wo[:, c, hf * 512:(hf + 1) * 512],
                                                      start=(c == 0), stop=(c == 7)),
                             reads=[oTb[i], wb], writes=[psb[yb]], signal=(c == 7))
                if s < 3:
                    ln_load_x(L, xsrc, t0 + (s + 1) * 128)
                if pend[0] is not None:
                    pend[0]()
                pend[0] = ln_epilogue(L, (4, 5), ALPHA, 1.0, xdst, xTdst, t0 + s * 128, 6)
        if pend[0] is not None:
            pend[0]()
        C.barrier()

    def on(p):
        return phases is None or p in phases

    if on("p0"):
        phase0(xin, xTa)
    xcur, xTcur = xin, xTa
    xalt = [xa, xb_]
    xTalt = [xTb, xTa]
    step = 0
    for l in range(depth):
        last_layer = (l == depth - 1)
        xd, xTd = xalt[step % 2], xTalt[step % 2]
        if on("ffn"):
            ffn_phase(l, 0, xcur, xTcur, xd, xTd)
        xcur, xTcur = xd, xTd
        step += 1
        if on("proj"):
            proj_phase(l, xTcur)
        if on("mla"):
            mla_phase()
        if on("diff"):
            diff_phase(l)
        if on("dil"):
            dil_phase()
        xd, xTd = xalt[step % 2], xTalt[step % 2]
        if on("out"):
            out_phase(l, xcur, xd, xTd)
        xcur, xTcur = xd, xTd
        step += 1
        xd, xTd = (y, None) if last_layer else (xalt[step % 2], xTalt[step % 2])
        if on("ffn"):
            ffn_phase(l, 1, xcur, xTcur, xd, xTd)
        xcur, xTcur = xd, xTd
        step += 1
    return nc


def _swap_cols(w, base, width, dim):
    blk = w[..., base:base + width].reshape(w.shape[:-1] + (width // dim, dim))
    return np.concatenate([blk[..., dim // 2:], blk[..., :dim // 2]], axis=-1).reshape(w.shape[:-1] + (width,))


def _tables():
    def tab(dim):
        half = dim // 2
        inv = (1.0 / (10000.0 ** (np.arange(0, dim, 2, dtype=np.float32) / np.float32(dim)))).astype(np.float32)
        ang = np.arange(MAXPOS, dtype=np.float32)[:, None] * inv[None, :]
        c, s = np.cos(ang).astype(np.float32), np.sin(ang).astype(np.float32)
        rows = np.arange(128)
        i = rows % dim
        f = i % half
        sign = np.where(i < half, -1.0, 1.0).astype(np.float32)
        return np.ascontiguousarray(c[:, f].T), np.ascontiguousarray((s[:, f] * sign[None, :]).T)
    c64, s64 = tab(64)
    c32, s32 = tab(32)
    ident = np.eye(128, dtype=np.float32).astype(ml_dtypes.bfloat16)
    kk = np.arange(128)[:, None]
    qq = np.arange(256)[None, :]
    band = ((qq - kk >= 0) & (qq - kk <= 128)).astype(np.float32).astype(ml_dtypes.bfloat16)
    return c64, s64, c32, s32, ident, band


def make_in_maps(x_prompt, x_sample, ln_g, ln_b, ffn_w_gate, ffn_w_up, ffn_w_down, w_in, mla_q_norm, mla_kv_norm,
                 mla_w_uq, mla_w_ukv, diff_lambda, diff_subln, w_out):
    f = lambda a: np.ascontiguousarray(np.asarray(a, dtype=np.float32))
    w_in = f(w_in)
    w_sw = np.concatenate([_swap_cols(w_in, O_QB, 256, 32), _swap_cols(w_in, O_KB, 256, 32),
                           _swap_cols(w_in, O_QC, 256, 64), _swap_cols(w_in, O_KC, 256, 64), _swap_cols(w_in, O_KPE, 32, 32)], axis=-1)
    w_uq = f(mla_w_uq)
    uq4 = w_uq.reshape(w_uq.shape[0], 384, 8, 96)
    w_uqs = np.concatenate([uq4[..., :64], uq4[..., 80:96], uq4[..., 64:80]], axis=-1).reshape(w_uq.shape[0], 384, 768)
    c64, s64, c32, s32, ident, band = _tables()
    shared = dict(ln_g=f(ln_g), ln_b=f(ln_b), wg=f(ffn_w_gate), wu=f(ffn_w_up), wd=f(ffn_w_down), w_in=w_in,
                  w_sw=np.ascontiguousarray(w_sw), q_norm=f(mla_q_norm), kv_norm=f(mla_kv_norm), w_uq=w_uq,
                  w_uqs=np.ascontiguousarray(w_uqs), w_ukv=f(mla_w_ukv), dlam=f(diff_lambda).reshape(-1, 128),
                  dsub=f(diff_subln), w_out=f(w_out), cos64=c64, sin64=s64, cos32=c32, sin32=s32, ident=ident, bandmask=band)
    xp, xs = f(x_prompt), f(x_sample)
    maps = []
    for b in range(xp.shape[0]):
        m = dict(shared)
        m["xin"] = np.ascontiguousarray(np.concatenate([xp[b], xs[b]], axis=0))
        maps.append(m)
    return maps


def kernel(**inputs):
    SP = inputs["x_prompt"].shape[1]
    SS = inputs["x_sample"].shape[1]
    nb = inputs["x_prompt"].shape[0]
    nc = build(SP, SS, DEPTH)
    maps = make_in_maps(**inputs)
    res = run_bass_kernel_spmd(nc, maps, core_ids=list(range(nb)))
    ys = [np.asarray(r["y"], dtype=np.float32) for r in res.results]
    y_prompt = np.stack([yy[:SP] for yy in ys], axis=0)
    y_sample = np.stack([yy[SP:] for yy in ys], axis=0)
    return (y_prompt, y_sample)
```

```python
import math
import numpy as np
import ml_dtypes
import concourse.bass as bass
import concourse.mybir as mybir
from concourse.bass_utils import run_bass_kernel_spmd

F32 = mybir.dt.float32
BF16 = mybir.dt.bfloat16
AF = mybir.ActivationFunctionType
ALU = mybir.AluOpType
AX = mybir.AxisListType

D = 1024
DFF = 2816
NFF = 22
DEPTH = 4
ALPHA = (2 * DEPTH) ** 0.25
EPS = 1e-5
INC = 2208
SWC = 1056
O_CQ, O_CKV, O_KPE, O_QB, O_KB, O_VB, O_QC, O_KC, O_VC = 0, 384, 640, 672, 928, 1184, 1440, 1696, 1952
S_QB, S_KB, S_QC, S_KC, S_KPE = 0, 256, 512, 768, 1024
MAXPOS = 8192


class Buf:
    __slots__ = ("w", "r")

    def __init__(self):
        self.w = None
        self.r = {}


def bufs(n):
    return [Buf() for _ in range(n)]


class Ctx:
    SB_LIMIT = 229248

    def __init__(self, nc):
        self.nc = nc
        self.eng = {"pe": nc.tensor, "act": nc.scalar, "dve": nc.vector, "pool": nc.gpsimd, "sp": nc.sync}
        self.sems = {}
        self.nsig = {}
        for k in ("pe", "act", "dve", "pool"):
            self.sems[k] = nc.alloc_semaphore("sem_" + k)
            self.nsig[k] = 0
        self.waited = {k: {} for k in self.eng}
        self.dcount = {}
        self.dma_free = []
        self.slot2sem = {}
        self.sb_base = 16640
        self.off = self.sb_base
        self.uid = 0
        self.keep = []

    def sb(self, name, shape, dtype):
        isz = 2 if dtype == BF16 else 4
        n = 1
        for s in shape[1:]:
            n *= s
        size = (n * isz + 63) // 64 * 64
        off = self.off
        self.off += size
        assert self.off <= self.SB_LIMIT, ("SBUF overflow", name, self.off)
        self.uid += 1
        return self.nc.alloc_sbuf_tensor_at("%s_%d" % (name, self.uid), list(shape), dtype, offset=off)

    def persist(self):
        self.sb_base = self.off

    def _deps(self, reads, writes):
        toks = []
        for b in reads:
            if b.w is not None:
                toks.append(b.w)
        for b in writes:
            if b.w is not None:
                toks.append(b.w)
            toks.extend(b.r.values())
        return toks

    def _wait(self, e, toks):
        need = {}
        for (sn, val, src) in toks:
            if src == "pe" and e == "pe":
                continue
            if val > need.get(sn, 0):
                need[sn] = val
        w = self.waited[e]
        for sn, val in need.items():
            if w.get(sn, 0) >= val:
                continue
            self.eng[e].wait_ge(self.sems[sn], val)
            w[sn] = val

    def _commit(self, tok, reads, writes):
        for b in reads:
            o = b.r.get(tok[0])
            if o is None or o[1] < tok[1]:
                b.r[tok[0]] = tok
        for b in writes:
            b.w = tok
            b.r = {}

    def op(self, e, fn, reads=(), writes=(), signal=True):
        self._wait(e, self._deps(reads, writes))
        ins = fn(self.eng[e])
        if signal:
            self.nsig[e] += 1
            ins.then_inc(self.sems[e], 1)
            tok = (e, self.nsig[e], e)
        else:
            tok = (e, self.nsig[e] + 1, e)
        self._commit(tok, reads, writes)

    def _slot_sem(self, slot):
        if slot not in self.slot2sem:
            if self.dma_free:
                name = self.dma_free.pop()
            else:
                name = "dsem%d" % len(self.dcount)
                self.sems[name] = self.nc.alloc_semaphore(name)
                self.dcount[name] = 0
            self.slot2sem[slot] = name
        return self.slot2sem[slot]

    def dma(self, q, out, in_, slot, reads=(), writes=(), **kw):
        self._wait(q, self._deps(reads, writes))
        sn = self._slot_sem(slot)
        ins = self.eng[q].dma_start(out=out, in_=in_, **kw)
        self.dcount[sn] += 16
        ins.then_inc(self.sems[sn], 16)
        tok = (sn, self.dcount[sn], "dma")
        self._commit(tok, reads, writes)

    def barrier(self):
        for e in self.eng:
            w = self.waited[e]
            for k in ("pe", "act", "dve", "pool"):
                if w.get(k, 0) < self.nsig[k]:
                    self.eng[e].wait_ge(self.sems[k], self.nsig[k])
                    w[k] = self.nsig[k]
            for sn, c in self.dcount.items():
                if w.get(sn, 0) < c:
                    self.eng[e].wait_ge(self.sems[sn], c)
                    w[sn] = c
        self.slot2sem = {}
        self.dma_free = list(self.dcount.keys())
        self.off = self.sb_base


def build(SP, SS, depth, dbg=False, phases=None):
    NT = SP + SS
    seqs = [(0, SP), (SP, SS)]
    nc = bass.Bass("TRN2", target_bir_lowering=False)

    def din(name, shape, dtype=F32):
        return nc.dram_tensor(name, list(shape), dtype, kind="ExternalInput").ap()

    def dscr(name, shape, dtype):
        return nc.dram_tensor(name, list(shape), dtype, kind=("ExternalOutput" if dbg else "Internal")).ap()

    xin = din("xin", [NT, D])
    ln_g = din("ln_g", [DEPTH, 3, D])
    ln_b = din("ln_b", [DEPTH, 3, D])
    wg = din("wg", [DEPTH, 2, D, DFF])
    wu = din("wu", [DEPTH, 2, D, DFF])
    wd = din("wd", [DEPTH, 2, DFF, D])
    w_in = din("w_in", [DEPTH, D, INC])
    w_sw = din("w_sw", [DEPTH, D, SWC])
    q_norm = din("q_norm", [DEPTH, 384])
    kv_norm = din("kv_norm", [DEPTH, 256])
    w_uq = din("w_uq", [DEPTH, 384, 768])
    w_uqs = din("w_uqs", [DEPTH, 384, 768])
    w_ukv = din("w_ukv", [DEPTH, 256, 1024])
    dlam = din("dlam", [DEPTH, 128])
    dsub = din("dsub", [DEPTH, 64])
    w_out = din("w_out", [DEPTH, D, D])
    cos64 = din("cos64", [128, MAXPOS])
    sin64 = din("sin64", [128, MAXPOS])
    cos32 = din("cos32", [128, MAXPOS])
    sin32 = din("sin32", [128, MAXPOS])
    ident_d = din("ident", [128, 128], BF16)
    mask_d = din("bandmask", [128, 256], BF16)

    y = nc.dram_tensor("y", [NT, D], F32, kind="ExternalOutput").ap()
    xa = dscr("xa", [NT, D], F32)
    xb_ = dscr("xb", [NT, D], F32)
    xTa = dscr("xTa", [8, 128, NT], BF16)
    xTb = dscr("xTb", [8, 128, NT], BF16)
    QA = dscr("QA", [8, 96, NT], BF16)
    KA = dscr("KA", [8, 96, NT], BF16)
    VA = dscr("VA", [NT, 8 * 65], BF16)
    QB = dscr("QB", [4, 64, NT], BF16)
    KB = dscr("KB", [4, 64, NT], BF16)
    VB = dscr("VB", [NT, 4 * 65], BF16)
    QC = dscr("QC", [4, 64, NT], BF16)
    KC = dscr("KC", [4, 64, NT], BF16)
    VC = dscr("VC", [NT, 4 * 65], BF16)
    OT = dscr("OT", [8, 128, NT], BF16)

    C = Ctx(nc)
    PS = nc.alloc_psum_tensor("ps", [128, 8, 512], F32)
    psb = bufs(8)

    def bank(b):
        return PS[:, b, :]

    def bank16(b):
        return PS[:, b, :].bitcast(BF16)

    ident = C.sb("ident", [128, 128], BF16)
    mask = C.sb("mask", [128, 256], BF16)
    ones_f = C.sb("ones_f", [128, 128], F32)
    sel_f = C.sb("sel_f", [128, 128], F32)
    mhalf = C.sb("mhalf", [128, 1], F32)
    C.persist()
    cb = Buf()
    C.dma("sp", ident[:], ident_d[:, :], cb, writes=[cb])
    C.dma("sp", mask[:], mask_d[:, :], cb, writes=[cb])
    C.op("dve", lambda e: e.memset(ones_f[:], 1.0), writes=[cb])
    C.op("dve", lambda e: e.memset(sel_f[:], 0.0), writes=[cb])
    C.op("dve", lambda e: e.memset(sel_f[64:65, :], 1.0), writes=[cb])
    C.op("dve", lambda e: e.memset(mhalf[:], -0.5), writes=[cb])
    C.barrier()

    def ln_setup(l, j):
        g_t = C.sb("g_t", [128, D], F32)
        b_t = C.sb("b_t", [128, D], F32)
        gb = Buf()
        C.dma("sp", g_t[:], ln_g[l, j:j + 1, :].partition_broadcast(128), gb, writes=[gb])
        C.dma("sp", b_t[:], ln_b[l, j:j + 1, :].partition_broadcast(128), gb, writes=[gb])
        xbuf = [C.sb("xbuf", [128, D], F32) for _ in range(2)]
        ob = [C.sb("ob", [128, D], BF16) for _ in range(2)]
        xts = [C.sb("xts", [128, 8, 128], BF16) for _ in range(2)]
        st = C.sb("stats", [128, 2, 6], F32)
        mv = C.sb("mv", [128, 2], F32)
        sm = C.sb("sm", [128, 4], F32)
        return dict(g=g_t, b=b_t, gb=gb, xbuf=xbuf, xbb=bufs(2), ob=ob, obb=bufs(2), xts=xts, xtsb=bufs(2),
                    st=st, stb=Buf(), mv=mv, sm=sm, cnt=0, lcnt=0)

    def ln_load_x(L, xsrc, t0):
        i = L["lcnt"] % 2
        L["lcnt"] += 1
        C.dma("sp", L["xbuf"][i][:], xsrc[t0:t0 + 128, :], L["xbb"][i], writes=[L["xbb"][i]])

    def ln_epilogue(L, ybanks, res_scale, k, xdst, xTdst, t0, trbank):
        i = L["cnt"] % 2
        L["cnt"] += 1
        xbuf = L["xbuf"][i]
        xbb = L["xbb"][i]
        for hf in range(2):
            sl = slice(hf * 512, (hf + 1) * 512)
            C.op("dve", lambda e: e.scalar_tensor_tensor(out=xbuf[:, sl], in0=xbuf[:, sl], scalar=float(res_scale),
                                                         in1=bank(ybanks[hf]), op0=ALU.mult, op1=ALU.add),
                 reads=[xbb, psb[ybanks[hf]]], writes=[xbb])
        stb = L["stb"]
        for hf in range(2):
            sl = slice(hf * 512, (hf + 1) * 512)
            C.op("dve", lambda e: e.bn_stats(out=L["st"][:, hf, :], in_=xbuf[:, sl]), reads=[xbb], writes=[stb])
        C.op("dve", lambda e: e.bn_aggr(out=L["mv"][:], in_=L["st"][:]), reads=[stb], writes=[stb])
        sm = L["sm"]
        C.op("dve", lambda e: e.tensor_scalar(out=sm[:, 0:1], in0=L["mv"][:, 1:2], scalar1=float(1.0 / (k * k)),
                                              scalar2=float(EPS), op0=ALU.mult, op1=ALU.add), reads=[stb], writes=[stb])
        C.op("pool", lambda e: e.tensor_tensor(out=sm[:, 1:2], in0=sm[:, 0:1], in1=mhalf[:], op=ALU.pow),
             reads=[stb], writes=[stb])
        C.op("dve", lambda e: e.tensor_scalar(out=sm[:, 2:3], in0=sm[:, 1:2], scalar1=float(1.0 / k), scalar2=None,
                                              op0=ALU.mult), reads=[stb], writes=[stb])
        C.op("dve", lambda e: e.tensor_scalar(out=sm[:, 3:4], in0=L["mv"][:, 0:1], scalar1=sm[:, 2:3], scalar2=-1.0,
                                              op0=ALU.mult, op1=ALU.mult), reads=[stb], writes=[stb])
        C.op("act", lambda e: e.activation(out=xbuf[:], in_=xbuf[:], func=AF.Identity, bias=sm[:, 3:4], scale=sm[:, 2:3]),
             reads=[xbb, stb], writes=[xbb])
        C.op("dve", lambda e: e.tensor_tensor(out=xbuf[:], in0=xbuf[:], in1=L["g"][:], op=ALU.mult),
             reads=[xbb, L["gb"]], writes=[xbb])
        C.op("dve", lambda e: e.tensor_tensor(out=xbuf[:], in0=xbuf[:], in1=L["b"][:], op=ALU.add),
             reads=[xbb, L["gb"]], writes=[xbb])
        C.dma("sp", xdst[t0:t0 + 128, :], xbuf[:], xbb, reads=[xbb])
        if xTdst is not None:
            obb = L["obb"][i]
            ob = L["ob"][i]
            C.op("act", lambda e: e.copy(out=ob[:], in_=xbuf[:]), reads=[xbb], writes=[obb])
            xts_i, xtsb_i = L["xts"][i], L["xtsb"][i]
            return lambda: transpose_store(ob, obb, xts_i, xtsb_i, xTdst, t0, trbank)
        return None

    def transpose_store(ob, obb, xts, xtsb, xTdst, t0, trbank):
        tb = bank16(trbank)
        for c in range(8):
            C.op("pe", lambda e: e.transpose(out=tb[:, c * 128:(c + 1) * 128], in_=ob[:, c * 128:(c + 1) * 128],
                                             identity=ident[:]),
                 reads=[obb], writes=[psb[trbank]], signal=(c == 7))
        C.op("dve", lambda e: e.tensor_copy(out=xts[:].rearrange("p c t -> p (c t)"), in_=tb[:, :]),
             reads=[psb[trbank]], writes=[xtsb])
        C.dma("sp", xTdst[:, :, t0:t0 + 128].rearrange("c p t -> p c t"), xts[:], xtsb, reads=[xtsb])

    def phase0(xsrc, xTdst):
        xbuf = [C.sb("p0x", [128, D], F32) for _ in range(2)]
        xbb = bufs(2)
        ob = [C.sb("p0o", [128, D], BF16) for _ in range(2)]
        obb = bufs(2)
        xts = [C.sb("p0t", [128, 8, 128], BF16) for _ in range(2)]
        xtsb = bufs(2)
        n = NT // 128
        C.dma("sp", xbuf[0][:], xsrc[0:128, :], xbb[0], writes=[xbb[0]])
        for s in range(n):
            i = s % 2
            if s + 1 < n:
                C.dma("sp", xbuf[1 - i][:], xsrc[(s + 1) * 128:(s + 2) * 128, :], xbb[1 - i], writes=[xbb[1 - i]])
            C.op("act", lambda e: e.copy(out=ob[i][:], in_=xbuf[i][:]), reads=[xbb[i]], writes=[obb[i]])
            transpose_store(ob[i], obb[i], xts[i], xtsb[i], xTdst, s * 128, 6 + i)
        C.barrier()

    def ffn_phase(l, j, xsrc, xTsrc, xdst, xTdst):
        Wg = C.sb("Wg", [128, 8, DFF], BF16)
        Wu = C.sb("Wu", [128, 8, DFF], BF16)
        Wd = C.sb("Wd", [128, NFF, D], BF16)
        wb = Buf()
        for (W, src) in ((Wg, wg), (Wu, wu)):
            for hf in range(2):
                C.dma("pool", W[:, :, hf * 1408:(hf + 1) * 1408],
                      src[l, j, :, hf * 1408:(hf + 1) * 1408].rearrange("(k p) f -> p k f", p=128), wb, writes=[wb])
        for hf in range(2):
            C.dma("pool", Wd[:, hf * 11:(hf + 1) * 11, :],
                  wd[l, j, hf * 1408:(hf + 1) * 1408, :].rearrange("(c p) d -> p c d", p=128), wb, writes=[wb])
        xT = C.sb("xTin", [128, 8, 512], BF16)
        xTb_ = Buf()
        hT = C.sb("hT", [128, NFF, 512], BF16)
        hTb = bufs(NFF)
        sil = [C.sb("sil", [128, 512], F32) for _ in range(2)]
        silb = bufs(2)
        L = ln_setup(l, 0 if j == 0 else 2)
        pend = [None]
        ntile = NT // 512
        C.dma("sp", xT[:], xTsrc[:, :, 0:512].rearrange("c p t -> p c t"), xTb_, writes=[xTb_])
        for t in range(ntile):
            t0 = t * 512
            ln_load_x(L, xsrc, t0)
            for c in range(NFF):
                gb_, ub_ = c % 2, 2 + c % 2
                for k in range(8):
                    C.op("pe", lambda e: e.matmul(bank(gb_), Wg[:, k, c * 128:(c + 1) * 128], xT[:, k, :],
                                                  start=(k == 0), stop=(k == 7)),
                         reads=[wb, xTb_], writes=[psb[gb_]], signal=(k == 7))
                for k in range(8):
                    C.op("pe", lambda e: e.matmul(bank(ub_), Wu[:, k, c * 128:(c + 1) * 128], xT[:, k, :],
                                                  start=(k == 0), stop=(k == 7)),
                         reads=[wb, xTb_], writes=[psb[ub_]], signal=(k == 7))
                C.op("act", lambda e: e.activation(out=sil[c % 2][:], in_=bank(gb_), func=AF.Silu),
                     reads=[psb[gb_]], writes=[silb[c % 2]])
                C.op("dve", lambda e: e.tensor_tensor(out=hT[:, c, :], in0=sil[c % 2][:], in1=bank(ub_), op=ALU.mult),
                     reads=[silb[c % 2], psb[ub_]], writes=[hTb[c]])
                if c == 2 and pend[0] is not None:
                    pend[0]()
                    pend[0] = None
            if t + 1 < ntile:
                C.dma("sp", xT[:], xTsrc[:, :, t0 + 512:t0 + 1024].rearrange("c p t -> p c t"), xTb_, writes=[xTb_])
            for s in range(4):
                for hf in range(2):
                    yb = 4 + hf
                    for c in range(NFF):
                        C.op("pe", lambda e: e.matmul(bank(yb), hT[:, c, s * 128:(s + 1) * 128],
                                                      Wd[:, c, hf * 512:(hf + 1) * 512], start=(c == 0), stop=(c == NFF - 1)),
                             reads=[hTb[c], wb], writes=[psb[yb]], signal=(c == NFF - 1))
                if s < 3:
                    ln_load_x(L, xsrc, t0 + (s + 1) * 128)
                if pend[0] is not None:
                    pend[0]()
                pend[0] = ln_epilogue(L, (4, 5), 2.0 * ALPHA, 2.0, xdst, xTdst, t0 + s * 128, 6)
        if pend[0] is not None:
            pend[0]()
        C.barrier()

    def proj_phase(l, xTsrc):
        Win = C.sb("Win", [128, 8, INC], BF16)
        Wsw = C.sb("Wsw", [128, 8, SWC], BF16)
        Wuq = C.sb("Wuq", [128, 3, 768], BF16)
        Wuqs = C.sb("Wuqs", [128, 3, 768], BF16)
        Wk = C.sb("Wk", [128, 2, 512], BF16)
        Wv = C.sb("Wv", [128, 2, 512], BF16)
        wb = Buf()
        for hf in range(2):
            C.dma("pool", Win[:, :, hf * 1104:(hf + 1) * 1104],
                  w_in[l, :, hf * 1104:(hf + 1) * 1104].rearrange("(k p) f -> p k f", p=128), wb, writes=[wb])
        C.dma("pool", Wsw[:], w_sw[l].rearrange("(k p) f -> p k f", p=128), wb, writes=[wb])
        stg = C.sb("stg", [128, 3, 1024], F32)
        stg2 = C.sb("stg2", [128, 3, 768], F32)
        stg3 = C.sb("stg3", [128, 2, 1024], F32)
        gq = C.sb("gq", [128, 3], F32)
        gkv = C.sb("gkv", [128, 2], F32)
        sb_ = Buf()
        C.dma("sp", stg[:, :, 0:768], w_uq[l].rearrange("(c p) f -> p c f", p=128), sb_, writes=[sb_])
        C.dma("sp", stg2[:], w_uqs[l].rearrange("(c p) f -> p c f", p=128), sb_, writes=[sb_])
        C.dma("sp", stg3[:], w_ukv[l].rearrange("(c p) f -> p c f", p=128), sb_, writes=[sb_])
        C.dma("sp", gq[:], q_norm[l].rearrange("(c p) -> p c", p=128), sb_, writes=[sb_], allow_slow_non_contiguous=True)
        C.dma("sp", gkv[:], kv_norm[l].rearrange("(c p) -> p c", p=128), sb_, writes=[sb_], allow_slow_non_contiguous=True)
        for c in range(3):
            C.op("dve", lambda e: e.tensor_scalar(out=Wuq[:, c, :], in0=stg[:, c, 0:768], scalar1=gq[:, c:c + 1],
                                                  scalar2=None, op0=ALU.mult), reads=[sb_], writes=[wb])
            C.op("dve", lambda e: e.tensor_scalar(out=Wuqs[:, c, :], in0=stg2[:, c, :], scalar1=gq[:, c:c + 1],
                                                  scalar2=None, op0=ALU.mult), reads=[sb_], writes=[wb])
        for c in range(2):
            src = stg3[:, c, :].rearrange("p (h t j) -> p h t j", h=8, t=2)
            C.op("dve", lambda e: e.tensor_scalar(out=Wk[:, c, :].rearrange("p (h j) -> p h j", h=8), in0=src[:, :, 0, :],
                                                  scalar1=gkv[:, c:c + 1], scalar2=None, op0=ALU.mult),
                 reads=[sb_], writes=[wb])
            C.op("dve", lambda e: e.tensor_scalar(out=Wv[:, c, :].rearrange("p (h j) -> p h j", h=8), in0=src[:, :, 1, :],
                                                  scalar1=gkv[:, c:c + 1], scalar2=None, op0=ALU.mult),
                 reads=[sb_], writes=[wb])

        xT = [C.sb("xTin", [128, 8, 512], BF16) for _ in range(2)]
        xTb_ = bufs(2)
        tabs = [C.sb("tabs", [128, 4, 512], F32) for _ in range(2)]
        tabb = bufs(2)
        cqn = C.sb("cqn", [128, 3, 512], BF16)
        cqnb = Buf()
        ckvn = C.sb("ckvn", [128, 2, 512], BF16)
        ckvnb = Buf()
        sq = [C.sb("sq", [128, 512], F32) for _ in range(2)]
        sqb = bufs(2)
        rr = C.sb("rr", [128, 512], F32)
        rrb = Buf()
        qas = C.sb("qas", [128, 8, 512], BF16)
        qasb = bufs(8)
        kas = C.sb("kas", [128, 8, 512], BF16)
        kasb = bufs(8)
        vas = C.sb("vas", [128, 4, 8, 65], BF16)
        vasb = Buf()
        rs = C.sb("rs", [128, 4, 2, 512], BF16)
        rsb = [bufs(2) for _ in range(4)]
        vbs = C.sb("vbs", [128, 4, 4, 65], BF16)
        vbsb = Buf()
        vcs = C.sb("vcs", [128, 4, 4, 65], BF16)
        vcsb = Buf()
        tmp = [C.sb("tmp", [128, 512], F32) for _ in range(2)]
        tmpb = bufs(2)
        C.op("pool", lambda e: e.memset(vas[:], 1.0), writes=[vasb])
        C.op("pool", lambda e: e.memset(vbs[:], 1.0), writes=[vbsb])
        C.op("pool", lambda e: e.memset(vcs[:], 1.0), writes=[vcsb])

        tiles = []
        for (s0, S) in seqs:
            for q in range(S // 512):
                tiles.append((s0 + q * 512, q * 512))

        def load_tile(ti):
            t0, p0 = tiles[ti]
            i = ti % 2
            C.dma("sp", xT[i][:], xTsrc[:, :, t0:t0 + 512].rearrange("c p t -> p c t"), xTb_[i], writes=[xTb_[i]])
            for n_, tab in enumerate((cos64, sin64, cos32, sin32)):
                C.dma("sp", tabs[i][:, n_, :], tab[:, p0:p0 + 512], tabb[i], writes=[tabb[i]])

        def proj_fm(bk, W, col0, ncol, xTi, xb, prow=0):
            for k in range(8):
                C.op("pe", lambda e: e.matmul(PS[prow:prow + ncol, bk, :], W[:, k, col0:col0 + ncol], xTi[:, k, :],
                                              start=(k == 0), stop=(k == 7)),
                     reads=[wb, xb], writes=[psb[bk]], signal=(k == 7))

        tcnt = [0]

        def rope_out(bk_a, bk_s, p0, p1, ctab, stab, out_ap, tb, out_buf, extra_reads=()):
            i0 = tcnt[0] % 2
            tcnt[0] += 1
            t_ = tmp[i0]
            C.op("dve", lambda e: e.tensor_tensor(out=t_[p0:p1, :], in0=ctab[p0:p1, :], in1=PS[p0:p1, bk_a, :], op=ALU.mult),
                 reads=[tb, psb[bk_a]], writes=[tmpb[i0]])
            i1 = tcnt[0] % 2
            tcnt[0] += 1
            t2 = tmp[i1]
            C.op("dve", lambda e: e.tensor_tensor(out=t2[p0:p1, :], in0=stab[p0:p1, :], in1=PS[p0:p1, bk_s, :], op=ALU.mult),
                 reads=[tb, psb[bk_s]], writes=[tmpb[i1]])
            C.op("dve", lambda e: e.tensor_tensor(out=out_ap, in0=t_[p0:p1, :], in1=t2[p0:p1, :], op=ALU.add),
                 reads=[tmpb[i0], tmpb[i1]] + list(extra_reads), writes=[out_buf])

        def rms_norm_fm(nch, col0, nfeat, xTi, xb, dst, dstb):
            for c in range(nch):
                proj_fm(c, Win, col0 + c * 128, 128, xTi, xb)
                C.op("act", lambda e: e.activation(out=sq[c % 2][:], in_=bank(c), func=AF.Square),
                     reads=[psb[c]], writes=[sqb[c % 2]])
                C.op("pe", lambda e: e.matmul(bank(3), ones_f[:, :], sq[c % 2][:], start=(c == 0), stop=(c == nch - 1)),
                     reads=[sqb[c % 2]], writes=[psb[3]], signal=True)
            C.op("act", lambda e: e.activation(out=rr[:], in_=bank(3), func=AF.Ln, bias=float(EPS), scale=float(1.0 / nfeat)),
                 reads=[psb[3]], writes=[rrb])
            C.op("act", lambda e: e.activation(out=rr[:], in_=rr[:], func=AF.Exp, scale=-0.5), reads=[rrb], writes=[rrb])
            for c in range(nch):
                C.op("dve", lambda e: e.tensor_tensor(out=dst[:, c, :], in0=rr[:], in1=bank(c), op=ALU.mult),
                     reads=[rrb, psb[c]], writes=[dstb])

        load_tile(0)
        for ti, (t0, p0) in enumerate(tiles):
            i = ti % 2
            if ti + 1 < len(tiles):
                load_tile(ti + 1)
            xTi, xb, tb = xT[i], xTb_[i], tabb[i]
            c64, s64, c32, s32 = tabs[i][:, 0, :], tabs[i][:, 1, :], tabs[i][:, 2, :], tabs[i][:, 3, :]
            rms_norm_fm(3, O_CQ, 384, xTi, xb, cqn, cqnb)
            for h in range(8):
                bk = 4 + (h % 2) * 2
                for c in range(3):
                    C.op("pe", lambda e: e.matmul(PS[0:96, bk, :], Wuq[:, c, h * 96:(h + 1) * 96], cqn[:, c, :],
                                                  start=(c == 0), stop=(c == 2)),
                         reads=[wb, cqnb], writes=[psb[bk]], signal=(c == 2))
                for c in range(3):
                    C.op("pe", lambda e: e.matmul(PS[0:96, bk + 1, :], Wuqs[:, c, h * 96:(h + 1) * 96], cqn[:, c, :],
                                                  start=(c == 0), stop=(c == 2)),
                         reads=[wb, cqnb], writes=[psb[bk + 1]], signal=(c == 2))
                C.op("act", lambda e: e.copy(out=qas[0:64, h, :], in_=PS[0:64, bk, :]), reads=[psb[bk]], writes=[qasb[h]])
                rope_out(bk, bk + 1, 64, 96, c32, s32, qas[64:96, h, :], tb, qasb[h])
                C.dma("sp", QA[h, :, t0:t0 + 512], qas[0:96, h, :], qasb[h], reads=[qasb[h]])
            rms_norm_fm(2, O_CKV, 256, xTi, xb, ckvn, ckvnb)
            proj_fm(4, Win, O_KPE - 64, 96, xTi, xb)
            proj_fm(5, Wsw, S_KPE - 64, 96, xTi, xb)
            rope_out(4, 5, 64, 96, c32, s32, kas[64:96, 0, :], tb, kasb[0])
            for h in range(8):
                bk = 6 + (h % 2)
                for c in range(2):
                    C.op("pe", lambda e: e.matmul(PS[0:64, bk, :], Wk[:, c, h * 64:(h + 1) * 64], ckvn[:, c, :],
                                                  start=(c == 0), stop=(c == 1)),
                         reads=[wb, ckvnb], writes=[psb[bk]], signal=(c == 1))
                C.op("act", lambda e: e.copy(out=kas[0:64, h, :], in_=PS[0:64, bk, :]), reads=[psb[bk]], writes=[kasb[h]])
                C.dma("sp", KA[h, 0:64, t0:t0 + 512], kas[0:64, h, :], kasb[h], reads=[kasb[h]])
                C.dma("sp", KA[h, 64:96, t0:t0 + 512], kas[64:96, 0, :], kasb[0], reads=[kasb[0]])
            for s in range(4):
                bk = 4 + (s % 2)
                for c in range(2):
                    C.op("pe", lambda e: e.matmul(bank(bk), ckvn[:, c, s * 128:(s + 1) * 128], Wv[:, c, :],
                                                  start=(c == 0), stop=(c == 1)),
                         reads=[wb, ckvnb], writes=[psb[bk]], signal=(c == 1))
                C.op("act", lambda e: e.copy(out=vas[:, s, :, 0:64], in_=bank(bk).rearrange("p (h j) -> p h j", h=8)),
                     reads=[psb[bk]], writes=[vasb])
            C.dma("sp", VA[t0:t0 + 512, :].rearrange("(s p) f -> p s f", p=128), vas[:].rearrange("p s h j -> p s (h j)"),
                  vasb, reads=[vasb])
            for gi, (oc, osw, ct, st_, dst) in enumerate(((O_QB, S_QB, c32, s32, QB), (O_KB, S_KB, c32, s32, KB),
                                                           (O_QC, S_QC, c64, s64, QC), (O_KC, S_KC, c64, s64, KC))):
                for ch in range(2):
                    ba = (gi * 2 + ch) % 2 * 2
                    proj_fm(ba, Win, oc + ch * 128, 128, xTi, xb)
                    proj_fm(ba + 1, Wsw, osw + ch * 128, 128, xTi, xb)
                    rope_out(ba, ba + 1, 0, 128, ct, st_, rs[:, gi, ch, :], tb, rsb[gi][ch])
                    C.dma("sp", dst[2 * ch:2 * ch + 2, :, t0:t0 + 512].rearrange("h p t -> (h p) t"), rs[:, gi, ch, :],
                          rsb[gi][ch], reads=[rsb[gi][ch]])
            for s in range(4):
                bk = 6 + (s % 2)
                for (col, off_) in ((O_VB, 0), (O_VC, 256)):
                    for k in range(8):
                        C.op("pe", lambda e: e.matmul(PS[:, bk, off_:off_ + 256], xTi[:, k, s * 128:(s + 1) * 128],
                                                      Win[:, k, col:col + 256], start=(k == 0), stop=(k == 7)),
                             reads=[wb, xb], writes=[psb[bk]], signal=(k == 7))
                C.op("act", lambda e: e.copy(out=vbs[:, s, :, 0:64], in_=PS[:, bk, 0:256].rearrange("p (h j) -> p h j", h=4)),
                     reads=[psb[bk]], writes=[vbsb])
                C.op("act", lambda e: e.copy(out=vcs[:, s, :, 0:64], in_=PS[:, bk, 256:512].rearrange("p (h j) -> p h j", h=4)),
                     reads=[psb[bk]], writes=[vcsb])
            C.dma("sp", VB[t0:t0 + 512, :].rearrange("(s p) f -> p s f", p=128), vbs[:].rearrange("p s h j -> p s (h j)"),
                  vbsb, reads=[vbsb])
            C.dma("sp", VC[t0:t0 + 512, :].rearrange("(s p) f -> p s f", p=128), vcs[:].rearrange("p s h j -> p s (h j)"),
                  vcsb, reads=[vcsb])
        C.barrier()

    def finalize_norm(osb, osbb, nbank):
        C.op("dve", lambda e: e.reciprocal(out=osb[64:65, :], in_=osb[64:65, :]), reads=[osbb], writes=[osbb])
        C.op("pe", lambda e: e.matmul(PS[0:64, nbank, :], sel_f[0:65, :], osb[0:65, :], start=True, stop=True),
             reads=[osbb], writes=[psb[nbank]], signal=True)

    class Deferred:
        def __init__(self):
            self.q = []

        def add(self, n, fn):
            self.q.append([n, fn])

        def tick(self):
            for it in self.q:
                it[0] -= 1
            while self.q and self.q[0][0] <= 0:
                self.q.pop(0)[1]()

        def flush(self):
            while self.q:
                self.q.pop(0)[1]()

    def mla_phase():
        for (s0, S) in seqs:
            nkt = S // 128
            vall = C.sb("vall", [128, nkt, 520], BF16)
            vb_ = Buf()
            C.dma("sp", vall[:], VA[s0:s0 + S, :].rearrange("(k p) f -> p k f", p=128), vb_, writes=[vb_])
            kT = [C.sb("kT", [96, S], BF16) for _ in range(2)]
            kTb = bufs(2)
            qT = [C.sb("qT", [96, 512], BF16) for _ in range(2)]
            qTb = bufs(2)
            pT = [C.sb("pT", [128, 1024], BF16) for _ in range(3)]
            pTb = bufs(3)
            osb = [C.sb("osb", [65, 512], F32) for _ in range(2)]
            osbb = bufs(2)
            ost = [C.sb("ost", [64, 512], BF16) for _ in range(2)]
            ostb = bufs(2)
            nq = S // 512
            work = [(h, q) for h in range(8) for q in range(nq)]
            C.dma("sp", kT[0][:], KA[0, :, s0:s0 + S], kTb[0], writes=[kTb[0]])
            C.dma("sp", qT[0][:], QA[0, :, s0:s0 + 512], qTb[0], writes=[qTb[0]])
            scale = 96.0 ** -0.5
            npair = nkt // 2
            dq = Deferred()
            OB, NB = 6, 7
            items = [(wi, kp) for wi in range(len(work)) for kp in range(npair)]

            def scores(idx):
                wi, kp = items[idx]
                h = work[wi][0]
                ki, qi = h % 2, wi % 2
                b0 = (idx % 3) * 2
                for j in range(2):
                    kt = kp * 2 + j
                    C.op("pe", lambda e: e.matmul(bank(b0 + j), kT[ki][:, kt * 128:(kt + 1) * 128], qT[qi][:, :],
                                                  start=True, stop=True),
                         reads=[kTb[ki], qTb[qi]], writes=[psb[b0 + j]], signal=(j == 1))

            def prefetch(wi):
                if wi >= len(work):
                    return
                h2, q2 = work[wi]
                if wi == 0 or work[wi - 1][0] != h2:
                    if wi > 0:
                        C.dma("sp", kT[h2 % 2][:], KA[h2, :, s0:s0 + S], kTb[h2 % 2], writes=[kTb[h2 % 2]])
                if wi > 0:
                    C.dma("sp", qT[wi % 2][:], QA[h2, :, s0 + q2 * 512:s0 + (q2 + 1) * 512], qTb[wi % 2], writes=[qTb[wi % 2]])

            prefetch(1)
            scores(0)
            scores(1)
            scores(2)
            for idx, (wi, kp) in enumerate(items):
                h, q = work[wi]
                qi = wi % 2
                b0 = (idx % 3) * 2
                pi = idx % 3
                C.op("act", lambda e: e.activation(out=pT[pi][:], in_=PS[:, b0:b0 + 2, :].rearrange("p b n -> p (b n)"),
                                                   func=AF.Exp, scale=float(scale)),
                     reads=[psb[b0], psb[b0 + 1]], writes=[pTb[pi]])
                if idx + 3 < len(items):
                    scores(idx + 3)
                for j in range(2):
                    kt = kp * 2 + j
                    last = (kp == npair - 1 and j == 1)
                    C.op("pe", lambda e: e.matmul(PS[0:65, OB, :], vall[:, kt, h * 65:(h + 1) * 65],
                                                  pT[pi][:, j * 512:(j + 1) * 512], start=(kt == 0), stop=last),
                         reads=[vb_, pTb[pi]], writes=[psb[OB]], signal=(j == 1))
                dq.tick()
                if kp == 2 or (npair <= 2 and kp == npair - 1):
                    prefetch(wi + 2) if False else None
                if kp == npair - 1:
                    C.op("dve", lambda e: e.tensor_copy(out=osb[qi][:], in_=PS[0:65, OB, :]), reads=[psb[OB]], writes=[osbb[qi]])
                    C.op("dve", lambda e: e.reciprocal(out=osb[qi][64:65, :], in_=osb[qi][64:65, :]), reads=[osbb[qi]],
                         writes=[osbb[qi]])

                    def fin(qi=qi, h=h, q=q):
                        C.op("pe", lambda e: e.matmul(PS[:, NB, :], sel_f[0:65, :], osb[qi][0:65, :], start=True, stop=True),
                             reads=[osbb[qi]], writes=[psb[NB]], signal=True)
                        C.op("dve", lambda e: e.tensor_tensor(out=ost[qi][:], in0=osb[qi][0:64, :], in1=PS[0:64, NB, :], op=ALU.mult),
                             reads=[osbb[qi], psb[NB]], writes=[ostb[qi]])
                        C.dma("sp", OT[h // 2, (h % 2) * 64:(h % 2) * 64 + 64, s0 + q * 512:s0 + (q + 1) * 512], ost[qi][:],
                              ostb[qi], reads=[ostb[qi]])
                    dq.add(3, fin)
                    prefetch(wi + 2)
            dq.flush()
            C.barrier()

    def diff_phase(l):
        lambda_init = 0.8 - 0.6 * math.exp(-0.3 * l)
        base0 = C.sb_base
        lt = C.sb("lt", [128, 128], F32)
        lsm = C.sb("lsm", [128, 8], F32)
        gs = C.sb("gs", [64, 1], F32)
        C.sb_base = C.off
        lb = Buf()
        C.dma("sp", lt[:], dlam[l:l + 1, :].partition_broadcast(128), lb, writes=[lb])
        C.dma("sp", gs[:], dsub[l].rearrange("(p o) -> p o", o=1), lb, writes=[lb])
        for i in range(2):
            C.op("dve", lambda e: e.tensor_tensor(out=lt[:, i * 64:i * 64 + 32], in0=lt[:, i * 64:i * 64 + 32],
                                                  in1=lt[:, i * 64 + 32:i * 64 + 64], op=ALU.mult), reads=[lb], writes=[lb])
            C.op("dve", lambda e: e.reduce_sum(out=lsm[:, i:i + 1], in_=lt[:, i * 64:i * 64 + 32], axis=AX.X),
                 reads=[lb], writes=[lb])
        C.op("act", lambda e: e.activation(out=lsm[:, 2:4], in_=lsm[:, 0:2], func=AF.Exp), reads=[lb], writes=[lb])
        C.op("dve", lambda e: e.tensor_tensor(out=lsm[:, 4:5], in0=lsm[:, 3:4], in1=lsm[:, 2:3], op=ALU.subtract),
             reads=[lb], writes=[lb])
        C.op("dve", lambda e: e.tensor_scalar(out=lsm[:, 5:6], in0=lsm[:, 4:5], scalar1=float(-lambda_init), scalar2=None,
                                              op0=ALU.add), reads=[lb], writes=[lb])
        C.op("dve", lambda e: e.tensor_scalar(out=gs[:], in0=gs[:], scalar1=float(1.0 - lambda_init), scalar2=None,
                                              op0=ALU.mult), reads=[lb], writes=[lb])
        neglam = lsm[0:64, 5:6]
        scale = 32.0 ** -0.5
        for (s0, S) in seqs:
            nkt = S // 128
            vall = C.sb("vall", [128, nkt, 260], BF16)
            vb_ = Buf()
            C.dma("sp", vall[:], VB[s0:s0 + S, :].rearrange("(k p) f -> p k f", p=128), vb_, writes=[vb_])
            kT = [[C.sb("kT", [128, S], BF16) for _c in range(2)] for _ in range(2)]
            kTb = bufs(2)
            qT = [C.sb("qT", [128, 512], BF16) for _ in range(2)]
            qTb = bufs(2)
            for b_ in range(2):
                for c_ in range(2):
                    C.op("pool", lambda e: e.memset(kT[b_][c_][:], 0.0), writes=[kTb[b_]])
                C.op("pool", lambda e: e.memset(qT[b_][:], 0.0), writes=[qTb[b_]])

            def load_k(h_, b_):
                for c_ in range(2):
                    C.dma("sp", kT[b_][c_][c_ * 32:(c_ + 1) * 32, :], KB[h_, c_ * 32:(c_ + 1) * 32, s0:s0 + S], kTb[b_],
                          writes=[kTb[b_]])
            pT = [C.sb("pT", [128, 1024], BF16) for _ in range(3)]
            pTb = bufs(3)
            osb = [C.sb("osb", [65, 2, 512], F32) for _ in range(2)]
            osbb = bufs(2)
            t1 = C.sb("t1", [64, 512], F32)
            t2 = C.sb("t2", [64, 512], F32)
            tb_ = Buf()
            ost = [C.sb("ost", [64, 512], BF16) for _ in range(2)]
            ostb = bufs(2)
            nq = S // 512
            work = [(h, q) for h in range(4) for q in range(nq)]
            load_k(0, 0)
            C.dma("sp", qT[0][0:64, :], QB[0, :, s0:s0 + 512], qTb[0], writes=[qTb[0]])
            dq = Deferred()
            for wi, (h, q) in enumerate(work):
                ki = h % 2
                qi = wi % 2
                if wi + 1 < len(work):
                    h2, q2 = work[wi + 1]
                    if h2 != h:
                        load_k(h2, h2 % 2)
                    C.dma("sp", qT[1 - qi][0:64, :], QB[h2, :, s0 + q2 * 512:s0 + (q2 + 1) * 512], qTb[1 - qi], writes=[qTb[1 - qi]])
                o1, o2 = 4, 5

                def scores(kt):
                    b0 = (kt % 2) * 2
                    for c in range(2):
                        C.op("pe", lambda e: e.matmul(bank(b0 + c), kT[ki][c][:, kt * 128:(kt + 1) * 128],
                                                      qT[qi][:, :], start=True, stop=True),
                             reads=[kTb[ki], qTb[qi]], writes=[psb[b0 + c]], signal=(c == 1))

                scores(0)
                scores(1)
                for kt in range(nkt):
                    b0 = (kt % 2) * 2
                    pi = kt % 3
                    C.op("act", lambda e: e.activation(out=pT[pi][:], in_=PS[:, b0:b0 + 2, :].rearrange("p b n -> p (b n)"),
                                                       func=AF.Exp, scale=float(scale)),
                         reads=[psb[b0], psb[b0 + 1]], writes=[pTb[pi]])
                    if kt + 2 < nkt:
                        scores(kt + 2)
                    for c in range(2):
                        C.op("pe", lambda e: e.matmul(PS[0:65, o1 + c, :], vall[:, kt, h * 65:(h + 1) * 65],
                                                      pT[pi][:, c * 512:(c + 1) * 512], start=(kt == 0), stop=(kt == nkt - 1)),
                             reads=[vb_, pTb[pi]], writes=[psb[o1 + c]], signal=(c == 1))
                    dq.tick()
                ob2 = osb[qi]
                C.op("dve", lambda e: e.tensor_copy(out=ob2[:], in_=PS[0:65, o1:o1 + 2, :]), reads=[psb[o1], psb[o2]],
                     writes=[osbb[qi]])
                C.op("dve", lambda e: e.reciprocal(out=ob2[64:65, :, :], in_=ob2[64:65, :, :]), reads=[osbb[qi]], writes=[osbb[qi]])

                def f1(ob2=ob2, qi=qi):
                    for c in range(2):
                        C.op("pe", lambda e: e.matmul(PS[:, 6 + c, :], sel_f[0:65, :], ob2[0:65, c, :], start=True, stop=True),
                             reads=[osbb[qi]], writes=[psb[6 + c]], signal=True)
                    C.op("dve", lambda e: e.tensor_tensor(out=t1[:], in0=ob2[0:64, 0, :], in1=PS[0:64, 6, :], op=ALU.mult),
                         reads=[osbb[qi], psb[6]], writes=[tb_])
                    C.op("dve", lambda e: e.tensor_tensor(out=t2[:], in0=ob2[0:64, 1, :], in1=PS[0:64, 7, :], op=ALU.mult),
                         reads=[osbb[qi], psb[7], tb_], writes=[tb_])
                    C.op("dve", lambda e: e.scalar_tensor_tensor(out=t1[:], in0=t2[:], scalar=neglam, in1=t1[:], op0=ALU.mult,
                                                                 op1=ALU.add), reads=[tb_, lb], writes=[tb_])
                    C.op("dve", lambda e: e.tensor_tensor(out=t2[:], in0=t1[:], in1=t1[:], op=ALU.mult), reads=[tb_], writes=[tb_])

                def f2():
                    C.op("pe", lambda e: e.matmul(PS[0:64, 6, :], ones_f[0:64, 0:64], t2[:], start=True, stop=True),
                         reads=[tb_], writes=[psb[6]], signal=True)

                def f3(qi=qi, h=h, q=q):
                    C.op("act", lambda e: e.activation(out=t2[:], in_=PS[0:64, 6, :], func=AF.Ln, bias=float(EPS), scale=float(1.0 / 64)),
                         reads=[psb[6], tb_], writes=[tb_])
                    C.op("act", lambda e: e.activation(out=t2[:], in_=t2[:], func=AF.Exp, scale=-0.5), reads=[tb_], writes=[tb_])
                    C.op("dve", lambda e: e.scalar_tensor_tensor(out=ost[qi][:], in0=t1[:], scalar=gs[:, 0:1], in1=t2[:],
                                                                 op0=ALU.mult, op1=ALU.mult), reads=[tb_, lb], writes=[ostb[qi]])
                    C.dma("sp", OT[4 + h // 2, (h % 2) * 64:(h % 2) * 64 + 64, s0 + q * 512:s0 + (q + 1) * 512], ost[qi][:],
                          ostb[qi], reads=[ostb[qi]])
                dq.add(3, f1)
                dq.add(6, f2)
                dq.add(9, f3)
            dq.flush()
            C.barrier()
        C.sb_base = base0
        C.off = base0

    def dil_phase():
        scale = 64.0 ** -0.5
        for (s0, S) in seqs:
            for h in range(4):
                qn = C.sb("qn", [128, S], BF16)
                kn = C.sb("kn", [128, S], BF16)
                qp = C.sb("qp", [128, S], BF16)
                kp_ = C.sb("kp", [128, S], BF16)
                nb = Buf()
                pb = Buf()
                for t_ in (qn, kn):
                    C.op("pool", lambda e: e.memset(t_[64:128, :], 0.0), writes=[nb])
                for t_ in (qp, kp_):
                    C.op("pool", lambda e: e.memset(t_[64:128, :], 0.0), writes=[pb])
                acc = C.sb("acc", [65, S], F32)
                accb = Buf()
                C.dma("sp", qn[0:64, :], QC[h, :, s0:s0 + S], nb, writes=[nb])
                C.dma("sp", kn[0:64, :], KC[h, :, s0:s0 + S], nb, writes=[nb])
                pT = [C.sb("pT", [128, 256], BF16) for _ in range(4)]
                pTb = bufs(4)
                ost = [C.sb("ost", [64, 512], BF16) for _ in range(2)]
                ostb = bufs(2)
                cnt = 0
                for d in (1, 4, 16):
                    L = S // d
                    nt = L // 128
                    vp = C.sb("vp%d" % d, [128, d, nt + 1, 65], BF16)
                    vpb = Buf()
                    for r in range(d):
                        def rows(k0, n):
                            tok0 = s0 + k0 * d + r
                            if d == 1:
                                return VC[tok0:tok0 + n, h * 65:(h + 1) * 65]
                            return VC[tok0:tok0 + (n - 1) * d + 1:d, h * 65:(h + 1) * 65]
                        C.dma("sp", vp[64:128, r, 0, :], rows(0, 64), vpb, writes=[vpb])
                        if nt > 1:
                            C.dma("sp", vp[:, r, 1:nt, :], rows(64, (nt - 1) * 128).rearrange("(j p) f -> p j f", p=128),
                                  vpb, writes=[vpb])
                        C.dma("sp", vp[0:64, r, nt, :], rows(L - 64, 64), vpb, writes=[vpb])
                    if d == 1:
                        qd, kd, db = qn, kn, nb
                    else:
                        C.op("act", lambda e: e.copy(out=qp[0:64, :].rearrange("p (r i) -> p r i", r=d),
                                                             in_=qn[0:64, :].rearrange("p (i r) -> p r i", r=d)),
                             reads=[nb], writes=[pb])
                        C.op("dve", lambda e: e.tensor_copy(out=kp_[0:64, :].rearrange("p (r i) -> p r i", r=d),
                                                             in_=kn[0:64, :].rearrange("p (i r) -> p r i", r=d)),
                             reads=[nb], writes=[pb])
                        qd, kd, db = qp, kp_, pb
                    accv = acc[:].rearrange("p (i r) -> p r i", r=d)
                    tl = [(r, jp) for r in range(d) for jp in range(nt + 1)]
                    base = cnt
                    cnt += len(tl)

                    def geom(ti):
                        r, jp = tl[ti]
                        lo = 64 if jp == 0 else 0
                        hi = 64 if jp == nt else 128
                        q_lo = max(jp - 1, 0) * 128
                        q_hi = min(jp + 1, nt) * 128
                        mc0 = 128 if jp == 0 else 0
                        kbase = r * L + 128 * jp - 64
                        return r, jp, lo, hi, q_lo, q_hi - q_lo, mc0, kbase, (base + ti) % 4

                    def emit_score(ti):
                        r, jp, lo, hi, q_lo, nqc, mc0, kbase, sl_ = geom(ti)
                        C.op("pe", lambda e: e.matmul(PS[lo:hi, sl_, 0:nqc], kd[:, kbase + lo:kbase + hi],
                                                      qd[:, r * L + q_lo:r * L + q_lo + nqc], start=True, stop=True),
                             reads=[db], writes=[psb[sl_]], signal=True)

                    def emit_exp(ti):
                        r, jp, lo, hi, q_lo, nqc, mc0, kbase, sl_ = geom(ti)
                        C.op("act", lambda e: e.activation(out=pT[sl_][lo:hi, 0:nqc], in_=PS[lo:hi, sl_, 0:nqc], func=AF.Exp,
                                                           scale=float(scale)), reads=[psb[sl_]], writes=[pTb[sl_]])
                        C.op("dve", lambda e: e.tensor_tensor(out=pT[sl_][lo:hi, 0:nqc], in0=pT[sl_][lo:hi, 0:nqc],
                                                               in1=mask[lo:hi, mc0:mc0 + nqc], op=ALU.mult),
                             reads=[pTb[sl_]], writes=[pTb[sl_]])

                    def emit_pv(ti):
                        r, jp, lo, hi, q_lo, nqc, mc0, kbase, sl_ = geom(ti)
                        col = 0
                        for qt in range(max(jp - 1, 0), min(jp + 1, nt)):
                            first = (qt == jp)
                            obk = 4 + (qt % 2)
                            C.op("pe", lambda e: e.matmul(PS[0:65, obk, 0:128], vp[lo:hi, r, jp, :],
                                                          pT[sl_][lo:hi, col:col + 128], start=first, stop=(not first)),
                                 reads=[vpb, pTb[sl_]], writes=[psb[obk]], signal=True)
                            if not first:
                                dst = accv[:, r, qt * 128:(qt + 1) * 128]
                                if d == 1:
                                    C.op("dve", lambda e: e.tensor_copy(out=dst, in_=PS[0:65, obk, 0:128]),
                                         reads=[psb[obk]], writes=[accb])
                                else:
                                    C.op("dve", lambda e: e.tensor_tensor(out=dst, in0=dst, in1=PS[0:65, obk, 0:128], op=ALU.add),
                                         reads=[psb[obk], accb], writes=[accb])
                            col += 128

                    emit_score(0)
                    if len(tl) > 1:
                        emit_score(1)
                    for ti in range(len(tl)):
                        emit_exp(ti)
                        if ti + 2 < len(tl):
                            emit_score(ti + 2)
                        emit_pv(ti)
                for q in range(S // 512):
                    qi = q % 2
                    sl = slice(q * 512, (q + 1) * 512)
                    C.op("dve", lambda e: e.reciprocal(out=acc[64:65, sl], in_=acc[64:65, sl]), reads=[accb], writes=[accb])
                    C.op("pe", lambda e: e.matmul(PS[:, 6 + qi, :], sel_f[0:65, :], acc[0:65, sl], start=True, stop=True),
                         reads=[accb], writes=[psb[6 + qi]], signal=True)
                    C.op("dve", lambda e: e.tensor_tensor(out=ost[qi][:], in0=acc[0:64, sl], in1=PS[0:64, 6 + qi, :], op=ALU.mult),
                         reads=[accb, psb[6 + qi]], writes=[ostb[qi]])
                    C.dma("sp", OT[6 + h // 2, (h % 2) * 64:(h % 2) * 64 + 64, s0 + q * 512:s0 + (q + 1) * 512], ost[qi][:],
                          ostb[qi], reads=[ostb[qi]])
                C.barrier()

    def out_phase(l, xsrc, xdst, xTdst):
        Wo = C.sb("Wo", [128, 8, D], BF16)
        wb = Buf()
        C.dma("pool", Wo[:], w_out[l].rearrange("(k p) f -> p k f", p=128), wb, writes=[wb])
        oT = [C.sb("oT", [128, 8, 512], BF16) for _ in range(2)]
        oTb = bufs(2)
        L = ln_setup(l, 1)
        pend = [None]
        ntile = NT // 512
        C.dma("sp", oT[0][:], OT[:, :, 0:512].rearrange("c p t -> p c t"), oTb[0], writes=[oTb[0]])
        for t in range(ntile):
            t0 = t * 512
            i = t % 2
            if t + 1 < ntile:
                C.dma("sp", oT[1 - i][:], OT[:, :, t0 + 512:t0 + 1024].rearrange("c p t -> p c t"), oTb[1 - i], writes=[oTb[1 - i]])
            ln_load_x(L, xsrc, t0)
            for s in range(4):
                for hf in range(2):
                    yb = 4 + hf
                    for c in range(8):
                        C.op("pe", lambda e: e.matmul(bank(yb), oT[i][:, c, s * 128:(s + 1) * 128], Wo[:, c, hf * 512:(hf + 1) * 512],
                                                      start=(c == 0), stop=(c == 7)),
                             reads=[oTb[i], wb], writes=[psb[yb]], signal=(c == 7))
                if s < 3:
                    ln_load_x(L, xsrc, t0 + (s + 1) * 128)
                if pend[0] is not None:
                    pend[0]()
                pend[0] = ln_epilogue(L, (4, 5), ALPHA, 1.0, xdst, xTdst, t0 + s * 128, 6)
        if pend[0] is not None:
            pend[0]()
        C.barrier()

    def on(p):
        return phases is None or p in phases

    if on("p0"):
        phase0(xin, xTa)
    xcur, xTcur = xin, xTa
    xalt = [xa, xb_]
    xTalt = [xTb, xTa]
    step = 0
    for l in range(depth):
        last_layer = (l == depth - 1)
        xd, xTd = xalt[step % 2], xTalt[step % 2]
        if on("ffn"):
            ffn_phase(l, 0, xcur, xTcur, xd, xTd)
        xcur, xTcur = xd, xTd
        step += 1
        if on("proj"):
            proj_phase(l, xTcur)
        if on("mla"):
            mla_phase()
        if on("diff"):
            diff_phase(l)
        if on("dil"):
            dil_phase()
        xd, xTd = xalt[step % 2], xTalt[step % 2]
        if on("out"):
            out_phase(l, xcur, xd, xTd)
        xcur, xTcur = xd, xTd
        step += 1
        xd, xTd = (y, None) if last_layer else (xalt[step % 2], xTalt[step % 2])
        if on("ffn"):
            ffn_phase(l, 1, xcur, xTcur, xd, xTd)
        xcur, xTcur = xd, xTd
        step += 1
    return nc


def _swap_cols(w, base, width, dim):
    blk = w[..., base:base + width].reshape(w.shape[:-1] + (width // dim, dim))
    return np.concatenate([blk[..., dim // 2:], blk[..., :dim // 2]], axis=-1).reshape(w.shape[:-1] + (width,))


def _tables():
    def tab(dim):
        half = dim // 2
        inv = (1.0 / (10000.0 ** (np.arange(0, dim, 2, dtype=np.float32) / np.float32(dim)))).astype(np.float32)
        ang = np.arange(MAXPOS, dtype=np.float32)[:, None] * inv[None, :]
        c, s = np.cos(ang).astype(np.float32), np.sin(ang).astype(np.float32)
        rows = np.arange(128)
        i = rows % dim
        f = i % half
        sign = np.where(i < half, -1.0, 1.0).astype(np.float32)
        return np.ascontiguousarray(c[:, f].T), np.ascontiguousarray((s[:, f] * sign[None, :]).T)
    c64, s64 = tab(64)
    c32, s32 = tab(32)
    ident = np.eye(128, dtype=np.float32).astype(ml_dtypes.bfloat16)
    kk = np.arange(128)[:, None]
    qq = np.arange(256)[None, :]
    band = ((qq - kk >= 0) & (qq - kk <= 128)).astype(np.float32).astype(ml_dtypes.bfloat16)
    return c64, s64, c32, s32, ident, band


def make_in_maps(x_prompt, x_sample, ln_g, ln_b, ffn_w_gate, ffn_w_up, ffn_w_down, w_in, mla_q_norm, mla_kv_norm,
                 mla_w_uq, mla_w_ukv, diff_lambda, diff_subln, w_out):
    f = lambda a: np.ascontiguousarray(np.asarray(a, dtype=np.float32))
    w_in = f(w_in)
    w_sw = np.concatenate([_swap_cols(w_in, O_QB, 256, 32), _swap_cols(w_in, O_KB, 256, 32),
                           _swap_cols(w_in, O_QC, 256, 64), _swap_cols(w_in, O_KC, 256, 64), _swap_cols(w_in, O_KPE, 32, 32)], axis=-1)
    w_uq = f(mla_w_uq)
    uq4 = w_uq.reshape(w_uq.shape[0], 384, 8, 96)
    w_uqs = np.concatenate([uq4[..., :64], uq4[..., 80:96], uq4[..., 64:80]], axis=-1).reshape(w_uq.shape[0], 384, 768)
    c64, s64, c32, s32, ident, band = _tables()
    shared = dict(ln_g=f(ln_g), ln_b=f(ln_b), wg=f(ffn_w_gate), wu=f(ffn_w_up), wd=f(ffn_w_down), w_in=w_in,
                  w_sw=np.ascontiguousarray(w_sw), q_norm=f(mla_q_norm), kv_norm=f(mla_kv_norm), w_uq=w_uq,
                  w_uqs=np.ascontiguousarray(w_uqs), w_ukv=f(mla_w_ukv), dlam=f(diff_lambda).reshape(-1, 128),
                  dsub=f(diff_subln), w_out=f(w_out), cos64=c64, sin64=s64, cos32=c32, sin32=s32, ident=ident, bandmask=band)
    xp, xs = f(x_prompt), f(x_sample)
    maps = []
    for b in range(xp.shape[0]):
        m = dict(shared)
        m["xin"] = np.ascontiguousarray(np.concatenate([xp[b], xs[b]], axis=0))
        maps.append(m)
    return maps


def kernel(**inputs):
    SP = inputs["x_prompt"].shape[1]
    SS = inputs["x_sample"].shape[1]
    nb = inputs["x_prompt"].shape[0]
    nc = build(SP, SS, DEPTH)
    maps = make_in_maps(**inputs)
    res = run_bass_kernel_spmd(nc, maps, core_ids=list(range(nb)))
    ys = [np.asarray(r["y"], dtype=np.float32) for r in res.results]
    y_prompt = np.stack([yy[:SP] for yy in ys], axis=0)
    y_sample = np.stack([yy[SP:] for yy in ys], axis=0)
    return (y_prompt, y_sample)
```
